# Optimizing a Trainium2 kernel written in Bass

```python
import math
import jax, jax.numpy as jnp
from jax import lax
import numpy as np

D_MODEL = 1024
BATCH = 8
SEQ = 2048
DEPTH = 1
DEC_BATCH = 128
DEC_SEQ = 8
PAST_LEN = 16384
PAGE_SIZE = 128

MIX_WIDTH = D_MODEL
GLA_HEADS = 4
GLA_DV = (MIX_WIDTH // 2) // GLA_HEADS
GLA_DK = GLA_DV // 2
GLA_QK = GLA_HEADS * GLA_DK
GLA_V = GLA_HEADS * GLA_DV
GLA_GATE_RANK = 16
GLA_GATE_TAU = 16.0
GDN_HEADS = 4
GDN_DK = (MIX_WIDTH // 2) // GDN_HEADS
GDN_DV = GDN_DK
GDN_QK = GDN_HEADS * GDN_DK
GDN_V = GDN_HEADS * GDN_DV
GDN_CONV_DIM = 2 * GDN_QK + GDN_V
CONV_W = 4
IN_SIZES = (GLA_QK, GLA_QK, GLA_V, GLA_V, GLA_GATE_RANK, GDN_CONV_DIM, GDN_V, GDN_HEADS, GDN_HEADS)
IN_DIM = sum(IN_SIZES)
D_FF = -(-8 * D_MODEL // (3 * 256)) * 256
CHUNK = 64
NORM_EPS = 1e-5
L2_EPS = 1e-6
DN_ALPHA = (2 * DEPTH) ** 0.25
DN_BETA = (8 * DEPTH) ** -0.25

kernel_name = "hybrid_gla_gdn_deepnorm_step"


def _layer_norm(x, g, b):
    x32 = x.astype(jnp.float32)
    mu = jnp.mean(x32, -1, keepdims=True)
    var = jnp.mean(jnp.square(x32 - mu), -1, keepdims=True)
    return ((x32 - mu) * lax.rsqrt(var + NORM_EPS) * g + b).astype(x.dtype)


def _rms_norm(x, g):
    x32 = x.astype(jnp.float32)
    return x32 * lax.rsqrt(jnp.mean(jnp.square(x32), -1, keepdims=True) + NORM_EPS) * g.astype(jnp.float32)


def _l2_norm(x):
    return x * lax.rsqrt(jnp.sum(jnp.square(x), -1, keepdims=True) + L2_EPS)


def _to_chunks(a, c, n):
    b_, t_, h_, d_ = a.shape
    a = jnp.pad(a, ((0, 0), (0, n * c - t_), (0, 0), (0, 0)))
    return a.reshape(b_, n, c, h_, d_).transpose(1, 0, 3, 2, 4)


def _from_chunks(o, t_):
    n, b_, h_, c, d_ = o.shape
    return o.transpose(1, 0, 3, 2, 4).reshape(b_, n * c, h_, d_)[:, :t_]


def gla_chunked(q, k, v, log_a, s0):
    t_ = q.shape[1]
    c = min(CHUNK, t_)
    n = -(-t_ // c)
    qc, kc, vc, gc = (_to_chunks(a, c, n) for a in (q, k, v, log_a))
    causal = jnp.tril(jnp.ones((c, c), bool))

    def step(s, inp):
        qi, ki, vi, gi = inp
        b = jnp.cumsum(gi, axis=2)
        diff = b[:, :, :, None, :] - b[:, :, None, :, :]
        decay = jnp.exp(jnp.where(causal[:, :, None], diff, -jnp.inf))
        attn = jnp.einsum('bhid,bhjd,bhijd->bhij', qi, ki, decay)
        o = (jnp.einsum('bhij,bhjv->bhiv', attn, vi)
             + jnp.einsum('bhid,bhdv->bhiv', qi * jnp.exp(b), s))
        b_last = b[:, :, -1:, :]
        s = (jnp.exp(b_last[:, :, 0, :])[..., None] * s
             + jnp.einsum('bhjd,bhjv->bhdv', ki * jnp.exp(b_last - b), vi))
        return s, o

    s, o = lax.scan(step, s0, (qc, kc, vc, gc))
    return _from_chunks(o, t_), s


def gdn_chunked(q, k, v, g, beta, s0):
    t_ = q.shape[1]
    dv = v.shape[-1]
    c = min(CHUNK, t_)
    n = -(-t_ // c)
    qc, kc, vc = (_to_chunks(a, c, n) for a in (q, k, v))
    gc, bc = (_to_chunks(a[..., None], c, n)[..., 0] for a in (g, beta))
    causal = jnp.tril(jnp.ones((c, c), bool))
    strict = jnp.tril(jnp.ones((c, c), bool), -1)
    eye = jnp.eye(c, dtype=jnp.float32)

    def step(s, inp):
        qi, ki, vi, gi, bi = inp
        gcum = jnp.cumsum(gi, -1)
        gamma = jnp.exp(jnp.where(causal, gcum[..., :, None] - gcum[..., None, :], -jnp.inf))
        kk = jnp.einsum('bhid,bhjd->bhij', ki, ki)
        a_mat = eye + jnp.where(strict, bi[..., :, None] * kk * gamma, 0.0)
        rhs = jnp.concatenate([vi * bi[..., None], ki * (bi * jnp.exp(gcum))[..., None]], -1)
        sol = lax.linalg.triangular_solve(a_mat, rhs, left_side=True, lower=True, unit_diagonal=True)
        u, w = sol[..., :dv], sol[..., dv:]
        delta = u - jnp.einsum('bhid,bhdv->bhiv', w, s)
        qk = jnp.einsum('bhid,bhjd->bhij', qi, ki) * gamma
        o = (jnp.einsum('bhid,bhdv->bhiv', qi * jnp.exp(gcum)[..., None], s)
             + jnp.einsum('bhij,bhjv->bhiv', qk, delta))
        s = (jnp.exp(gcum[..., -1])[..., None, None] * s
             + jnp.einsum('bhjd,bhjv->bhdv', ki * jnp.exp(gcum[..., -1:] - gcum)[..., None], delta))
        return s, o

    s, o = lax.scan(step, s0, (qc, kc, vc, gc, bc))
    return _from_chunks(o, t_), s


def hybrid_mixer(x, s_gla, s_gdn, conv_buf, w_in, gla_w_gate_up, gla_b_gate, gla_norm_g,
                 gdn_conv_w, gdn_a_log, gdn_dt_bias, gdn_norm_g, w_out):
    f32 = jnp.float32
    bsz, t_, _ = x.shape
    h = jnp.einsum('btd,de->bte', x, w_in).astype(f32)
    parts = []
    off = 0
    for n_cols in IN_SIZES:
        parts.append(h[..., off:off + n_cols])
        off += n_cols
    gq, gk, gv, gg, ga, dqkv, dg, da, db = parts

    q = gq.reshape(bsz, t_, GLA_HEADS, GLA_DK) * GLA_DK ** -0.5
    k = gk.reshape(bsz, t_, GLA_HEADS, GLA_DK)
    v = gv.reshape(bsz, t_, GLA_HEADS, GLA_DV)
    z = jnp.einsum('btr,re->bte', ga, gla_w_gate_up.astype(f32)) + gla_b_gate.astype(f32)
    log_a = (jax.nn.log_sigmoid(z) / GLA_GATE_TAU).reshape(bsz, t_, GLA_HEADS, GLA_DK)
    o_gla, s_gla_new = gla_chunked(q, k, v, log_a, s_gla.astype(f32))
    o_gla = _rms_norm(o_gla, gla_norm_g) * jax.nn.silu(gg.reshape(bsz, t_, GLA_HEADS, GLA_DV))

    xc = jnp.concatenate([conv_buf.astype(f32), dqkv], axis=1)
    cw = gdn_conv_w.astype(f32)
    conv = xc[:, 0:t_] * cw[0]
    for i in range(1, CONV_W):
        conv = conv + xc[:, i:i + t_] * cw[i]
    conv = jax.nn.silu(conv)
    new_buf = xc[:, -(CONV_W - 1):]
    dq = _l2_norm(conv[..., :GDN_QK].reshape(bsz, t_, GDN_HEADS, GDN_DK)) * GDN_DK ** -0.5
    dk = _l2_norm(conv[..., GDN_QK:2 * GDN_QK].reshape(bsz, t_, GDN_HEADS, GDN_DK))
    dvv = conv[..., 2 * GDN_QK:].reshape(bsz, t_, GDN_HEADS, GDN_DV)
    g = -jnp.exp(gdn_a_log.astype(f32)) * jax.nn.softplus(da + gdn_dt_bias.astype(f32))
    beta = jax.nn.sigmoid(db)
    o_gdn, s_gdn_new = gdn_chunked(dq, dk, dvv, g, beta, s_gdn.astype(f32))
    o_gdn = _rms_norm(o_gdn, gdn_norm_g) * jax.nn.silu(dg.reshape(bsz, t_, GDN_HEADS, GDN_DV))

    o = jnp.concatenate([o_gla.reshape(bsz, t_, GLA_V), o_gdn.reshape(bsz, t_, GDN_V)], -1).astype(x.dtype)
    out = jnp.einsum('bte,ed->btd', o, w_out)
    dt_s = s_gla.dtype
    return out, s_gla_new.astype(dt_s), s_gdn_new.astype(dt_s), new_buf.astype(conv_buf.dtype)


def trunk_layer(x, s_gla, s_gdn, conv_buf, w_in, gla_w_gate_up, gla_b_gate, gla_norm_g,
                gdn_conv_w, gdn_a_log, gdn_dt_bias, gdn_norm_g, w_out, ln1_g, ln1_b,
                w_ffn_gate, w_ffn_up, w_ffn_down, ln2_g, ln2_b):
    m, s_gla_new, s_gdn_new, buf_new = hybrid_mixer(
        x, s_gla, s_gdn, conv_buf, w_in, gla_w_gate_up, gla_b_gate, gla_norm_g,
        gdn_conv_w, gdn_a_log, gdn_dt_bias, gdn_norm_g, w_out)
    x = _layer_norm(DN_ALPHA * x + m, ln1_g, ln1_b)
    hid = jax.nn.silu(jnp.einsum('btd,df->btf', x, w_ffn_gate)) * jnp.einsum('btd,df->btf', x, w_ffn_up)
    f = jnp.einsum('btf,fd->btd', hid, w_ffn_down)
    x = _layer_norm(DN_ALPHA * x + f, ln2_g, ln2_b)
    return x, s_gla_new, s_gdn_new, buf_new


def setup_inputs(seed: int = 0) -> dict:
    key = jax.random.key(seed)
    ks = jax.random.split(key, 24)
    f32 = jnp.float32
    L = DEPTH

    def nrm(k, shape, s):
        return jax.random.normal(k, shape, f32) * s

    dt = jnp.exp(jax.random.uniform(ks[11], (L, GDN_HEADS), f32, math.log(1e-3), math.log(1e-1)))
    return {
        "x_prompt": nrm(ks[0], (BATCH, SEQ, D_MODEL), 1.0),
        "x_sample": nrm(ks[1], (DEC_BATCH, DEC_SEQ, D_MODEL), 1.0),
        "state_gla": nrm(ks[2], (L, DEC_BATCH, GLA_HEADS, GLA_DK, GLA_DV), 0.5),
        "state_gdn": nrm(ks[3], (L, DEC_BATCH, GDN_HEADS, GDN_DK, GDN_DV), 0.5),
        "state_gdn_conv": nrm(ks[4], (L, DEC_BATCH, CONV_W - 1, GDN_CONV_DIM), 1.0),
        "w_in": nrm(ks[5], (L, D_MODEL, IN_DIM), D_MODEL ** -0.5),
        "gla_w_gate_up": nrm(ks[6], (L, GLA_GATE_RANK, GLA_QK), GLA_GATE_RANK ** -0.5),
        "gla_b_gate": nrm(ks[7], (L, GLA_QK), 0.1),
        "gla_norm_g": 1.0 + nrm(ks[8], (L, GLA_DV), 0.02),
        "gdn_conv_w": nrm(ks[9], (L, CONV_W, GDN_CONV_DIM), CONV_W ** -0.5),
        "gdn_a_log": jnp.log(jax.random.uniform(ks[10], (L, GDN_HEADS), f32, 1.0, 16.0)),
        "gdn_dt_bias": dt + jnp.log(-jnp.expm1(-dt)),
        "gdn_norm_g": 1.0 + nrm(ks[12], (L, GDN_DV), 0.02),
        "w_out": nrm(ks[13], (L, MIX_WIDTH, D_MODEL), DN_BETA * MIX_WIDTH ** -0.5),
        "ln1_g": 1.0 + nrm(ks[14], (L, D_MODEL), 0.02),
        "ln1_b": nrm(ks[15], (L, D_MODEL), 0.02),
        "w_ffn_gate": nrm(ks[16], (L, D_MODEL, D_FF), D_MODEL ** -0.5),
        "w_ffn_up": nrm(ks[17], (L, D_MODEL, D_FF), D_MODEL ** -0.5),
        "w_ffn_down": nrm(ks[18], (L, D_FF, D_MODEL), DN_BETA * D_FF ** -0.5),
        "ln2_g": 1.0 + nrm(ks[19], (L, D_MODEL), 0.02),
        "ln2_b": nrm(ks[20], (L, D_MODEL), 0.02),
    }


def reference(x_prompt, x_sample, state_gla, state_gdn, state_gdn_conv, w_in, gla_w_gate_up,
              gla_b_gate, gla_norm_g, gdn_conv_w, gdn_a_log, gdn_dt_bias, gdn_norm_g, w_out,
              ln1_g, ln1_b, w_ffn_gate, w_ffn_up, w_ffn_down, ln2_g, ln2_b):
    bp = x_prompt.shape[0]
    yp, ys = x_prompt, x_sample
    p_gla, p_gdn, p_conv, s_gla, s_gdn, s_conv = [], [], [], [], [], []
    for l in range(DEPTH):
        params = (w_in[l], gla_w_gate_up[l], gla_b_gate[l], gla_norm_g[l], gdn_conv_w[l],
                  gdn_a_log[l], gdn_dt_bias[l], gdn_norm_g[l], w_out[l], ln1_g[l], ln1_b[l],
                  w_ffn_gate[l], w_ffn_up[l], w_ffn_down[l], ln2_g[l], ln2_b[l])
        z_gla = jnp.zeros((bp, GLA_HEADS, GLA_DK, GLA_DV), state_gla.dtype)
        z_gdn = jnp.zeros((bp, GDN_HEADS, GDN_DK, GDN_DV), state_gdn.dtype)
        z_conv = jnp.zeros((bp, CONV_W - 1, GDN_CONV_DIM), state_gdn_conv.dtype)
        yp, a1, a2, a3 = trunk_layer(yp, z_gla, z_gdn, z_conv, *params)
        ys, b1, b2, b3 = trunk_layer(ys, state_gla[l], state_gdn[l], state_gdn_conv[l], *params)
        p_gla.append(a1); p_gdn.append(a2); p_conv.append(a3)
        s_gla.append(b1); s_gdn.append(b2); s_conv.append(b3)
    return (yp, ys, jnp.stack(p_gla), jnp.stack(p_gdn), jnp.stack(p_conv),
            jnp.stack(s_gla), jnp.stack(s_gdn), jnp.stack(s_conv))
```

```python
import contextlib
from math import prod
import numpy as np
import concourse.bass as bass
import concourse.mybir as mybir
from concourse.bass_utils import run_bass_kernel_spmd

F32 = mybir.dt.float32
BF16 = mybir.dt.bfloat16
F32R = mybir.dt.float32r


def R(ap):
    return ap.bitcast(F32R)
AF = mybir.ActivationFunctionType
ALU = mybir.AluOpType

N_DMA_SLOTS = 40
_ESZ = {str(F32): 4, str(BF16): 2}

D = 1024
NT = 17
IN_DIM = 3608
DFF = 2816
NFC = DFF // 128
OFF_GQ, OFF_GK, OFF_GV, OFF_GG, OFF_GA, OFF_DQKV, OFF_DG, OFF_DAB = 0, 256, 512, 1024, 1536, 1552, 3088, 3600
ALPHA = 2.0 ** 0.25
NEG = -30000.0
import os as _os0
DEBUG = {"rr0": [128, 1024], "sm": [128, 64], "gate": [128, 1024], "xt": [128, 1024], "x1": [128, 1024], "x2": [128, 1024], "r2": [128, 1024], "st2": [128, 16]} if _os0.environ.get("KDBG") else {}


_ALIAS = {}


def _box(ap):
    b = _box0(ap)
    al = _ALIAS.get(b[0])
    if al is not None:
        return (al[0], b[1], b[2], b[3] + al[1], b[4] + al[1])
    return b


def _box0(ap):
    name = ap.tensor.name
    pat = ap.ap
    off = int(ap.offset)
    es = _ESZ.get(str(ap.dtype), 4)
    sp = str(ap.space)
    if sp in ("SB", "PSUM"):
        pstep, pcnt = pat[0]
        if pstep > 0:
            p0 = off // pstep
            f0 = off - p0 * pstep
        else:
            p0, f0 = 0, off
        ext = 1
        for st, cn in pat[1:]:
            ext += abs(st) * (cn - 1)
        return (name, p0, p0 + pcnt, f0 * es, (f0 + ext) * es)
    ext = 1
    for st, cn in pat:
        ext += abs(st) * (cn - 1)
    return (name, 0, 1, off * es, (off + ext) * es)


def _overlap(a, b):
    return a[1] < b[2] and b[1] < a[2] and a[3] < b[4] and b[3] < a[4]


def _covers(a, b):
    return a[1] <= b[1] and a[2] >= b[2] and a[3] <= b[3] and a[4] >= b[4]


class Op:
    __slots__ = ("idx", "eng", "fn", "dma", "deps", "signal", "semval", "slot", "slot_prev")

    def __init__(self, idx, eng, fn, dma):
        self.idx, self.eng, self.fn, self.dma = idx, eng, fn, dma
        self.deps = set()
        self.signal = False
        self.semval = None
        self.slot = None
        self.slot_prev = 0


class Prog:
    ENGS = ("pe", "act", "dve", "pool", "sp")

    def __init__(self, nc, untracked=()):
        self.nc = nc
        self.ops = []
        self.wr = {}
        self.rd = {}
        self.untracked = set(untracked)

    def add(self, eng, fn, reads=(), writes=(), dma=False):
        op = Op(len(self.ops), eng, fn, dma)
        self.ops.append(op)
        for ap in reads:
            b = _box(ap)
            if b[0] in self.untracked:
                continue
            for (wb, wi) in self.wr.get(b[0], ()):
                if _overlap(wb, b):
                    op.deps.add(wi)
            self.rd.setdefault(b[0], []).append((b, op.idx))
        for ap in writes:
            b = _box(ap)
            if b[0] in self.untracked:
                continue
            wl = self.wr.get(b[0], [])
            for (wb, wi) in wl:
                if _overlap(wb, b):
                    w = self.ops[wi]
                    if w.dma or op.dma or w.eng != op.eng:
                        op.deps.add(wi)
            rl = self.rd.get(b[0], [])
            for (rb, ri) in rl:
                if ri != op.idx and _overlap(rb, b):
                    r = self.ops[ri]
                    if r.dma or op.dma or r.eng != op.eng:
                        op.deps.add(ri)
            self.wr[b[0]] = [(wb, wi) for (wb, wi) in wl if not _covers(b, wb)] + [(b, op.idx)]
            self.rd[b[0]] = [(rb, ri) for (rb, ri) in rl if (ri == op.idx) or not _covers(b, rb)]
        for ap in list(reads) + list(writes):
            if str(ap.space) == "PSUM":
                key = "LOCK_" + ap.tensor.name
                last = self.wr.get(key)
                if last is not None and last != op.idx and self.ops[last].eng != op.eng:
                    op.deps.add(last)
                self.wr[key] = op.idx
        return op

    def dma(self, q, out, in_, **kw):
        return self.add(q, lambda e: e.dma_start(out=out, in_=in_, **kw), [in_], [out], dma=True)

    def mm(self, out, lhsT, rhs, start=True, stop=True):
        return self.add("pe", lambda e: e.matmul(out, lhsT, rhs, start=start, stop=stop), [lhsT, rhs], [out])

    def tr(self, out, in_, ident):
        return self.add("pe", lambda e: e.transpose(out, in_, ident), [in_, ident], [out])

    def act(self, out, in_, func, bias=None, scale=None, accum_out=None):
        reads = [in_]
        kw = {}
        if bias is not None:
            kw["bias"] = bias
            if not isinstance(bias, (int, float)):
                reads.append(bias)
        if scale is not None:
            kw["scale"] = scale
            if not isinstance(scale, (int, float)):
                reads.append(scale)
        writes = [out]
        if accum_out is not None:
            kw["accum_out"] = accum_out
            writes.append(accum_out)
        return self.add("act", lambda e: e.activation(out, in_, func, **kw), reads, writes)

    def tt(self, eng, out, in0, in1, op):
        return self.add(eng, lambda e: e.tensor_tensor(out, in0, in1, op), [in0, in1], [out])

    def ts(self, eng, out, in0, s1, s2, op0, op1=None):
        reads = [in0]
        for s in (s1, s2):
            if s is not None and not isinstance(s, (int, float)):
                reads.append(s)
        if op1 is None:
            return self.add(eng, lambda e: e.tensor_scalar(out, in0, s1, None, op0), reads, [out])
        return self.add(eng, lambda e: e.tensor_scalar(out, in0, s1, s2, op0, op1), reads, [out])

    def stt(self, out, in0, scalar, in1, op0, op1):
        reads = [in0, in1]
        if not isinstance(scalar, (int, float)):
            reads.append(scalar)
        return self.add("dve", lambda e: e.scalar_tensor_tensor(out, in0, scalar, in1, op0, op1), reads, [out])

    def copy(self, eng, out, in_):
        if eng == "act":
            return self.add("act", lambda e: e.copy(out, in_), [in_], [out])
        return self.add(eng, lambda e: e.tensor_copy(out, in_), [in_], [out])

    def memset(self, eng, out, val):
        return self.add(eng, lambda e: e.memset(out, val), [], [out])

    def barrier(self, eng, reads):
        return self.add(eng, None, reads, [])

    def emit(self, es):
        nc = self.nc
        import os
        ops = self.ops[:int(os.environ.get("KSTOP", "100000000"))]
        for op in ops:
            for d in op.deps:
                ops[d].signal = True
        cnt = {e: 0 for e in self.ENGS}
        slot_uses = [0] * N_DMA_SLOTS
        nd = 0
        for op in ops:
            if op.dma:
                op.slot = nd % N_DMA_SLOTS
                nd += 1
                op.slot_prev = slot_uses[op.slot] * 16
                slot_uses[op.slot] += 1
                op.semval = slot_uses[op.slot] * 16
            elif op.signal:
                cnt[op.eng] += 1
                op.semval = cnt[op.eng]
        esem = {e: es.enter_context(nc.semaphore("s_" + e)) for e in self.ENGS}
        dsem = [es.enter_context(nc.semaphore("d%d" % i)) for i in range(N_DMA_SLOTS)]
        block = es.enter_context(nc.Block())
        stats = {}

        def gen(E):
            def body(e):
                waited = {}
                nw = ni = 0
                for op in ops:
                    if op.eng != E:
                        continue
                    need = {}
                    for d in op.deps:
                        dop = ops[d]
                        s = ("d", dop.slot) if dop.dma else ("e", dop.eng)
                        if need.get(s, 0) < dop.semval:
                            need[s] = dop.semval
                    if op.dma and op.slot_prev > 0:
                        s = ("d", op.slot)
                        if need.get(s, 0) < op.slot_prev:
                            need[s] = op.slot_prev
                    for s, v in need.items():
                        if waited.get(s, 0) >= v:
                            continue
                        waited[s] = v
                        e.wait_ge(dsem[s[1]] if s[0] == "d" else esem[s[1]], v)
                        nw += 1
                    if op.fn is None:
                        continue
                    ins = op.fn(e)
                    ni += 1
                    if op.dma:
                        ins.then_inc(dsem[op.slot], 16)
                    elif op.signal:
                        ins.then_inc(esem[E], 1)
                stats[E] = (ni, nw)
            return body

        block.tensor(gen("pe"))
        block.scalar(gen("act"))
        block.vector(gen("dve"))
        block.gpsimd(gen("pool"))
        block.sync(gen("sp"))
        return stats


class Arena:
    def __init__(self, nc, es, nbytes):
        self.nbytes = nbytes
        self.h = {BF16: es.enter_context(nc.sbuf_tensor("A", [128, nbytes // 2], BF16))}
        self.h[F32] = self.h[BF16].bitcast(F32)
        self.top = nbytes
        self.cur = {}

    def view(self, off, shape, dt, parts=128):
        es = 4 if dt == F32 else 2
        assert off % 4 == 0
        n = prod(shape)
        ap = self.h[dt][0:parts, off // es: off // es + n]
        if len(shape) > 1:
            names = ["a%d" % i for i in range(len(shape))]
            kw = {nm: s for nm, s in zip(names[1:], shape[1:])}
            ap = ap.rearrange("p (%s) -> p %s" % (" ".join(names), " ".join(names)), **kw)
        return ap

    def persist(self, shape, dt, parts=128):
        nb = (prod(shape) * (4 if dt == F32 else 2) + 31) // 32 * 32
        self.top -= nb
        return self.view(self.top, shape, dt, parts)

    def alloc(self, region, shape, dt, parts=128):
        nb = (prod(shape) * (4 if dt == F32 else 2) + 31) // 32 * 32
        off = self.cur[region]
        self.cur[region] = off + nb
        assert self.cur[region] <= self.top, (region, self.cur[region], self.top)
        return self.view(off, shape, dt, parts)


class PsumPool:
    def __init__(self, nc, es):
        self.banks = [es.enter_context(nc.psum_tensor("ps%d" % i, [128, 512], F32)) for i in range(8)]
        self.free = list(range(8))

    def alloc(self):
        assert self.free, "out of PSUM banks"
        return self.free.pop(0)

    def release(self, b):
        self.free.append(b)


def build_program():
    nc = bass.Bass("TRN2", target_bir_lowering=False)

    def din(name, shape):
        return nc.dram_tensor(name, list(shape), F32, kind="ExternalInput").ap()

    def dout(name, shape):
        return nc.dram_tensor(name, list(shape), F32, kind="ExternalOutput").ap()

    xs = din("xs", [NT, 128, D])
    sgla_i = din("sgla", [16, 4, 64, 128])
    sgdn_i = din("sgdn", [16, 4, 128, 128])
    sconv_i = din("sconv", [48, 1536])
    w_in = din("w_in", [D, IN_DIM])
    w_out = din("w_out", [D, D])
    w_g = din("w_g", [D, DFF])
    w_u = din("w_u", [D, DFF])
    w_d = din("w_d", [DFF, D])
    wgu_i = din("wgu", [16, 256])
    cwT_i = din("cwT", [128, 48])
    NCONST = 128 * 14 + 16
    consts_i = din("consts", [128, NCONST])
    prm_i = din("prm", [128, 2568])
    prm2_i = din("prm2", [128, 2048])
    gnc_i = din("gnc", [128, 2])

    y_o = dout("y", [NT, 128, D])
    pgla_o = dout("pgla", [4, 64, 128])
    pgdn_o = dout("pgdn", [4, 128, 128])
    pconv_o = dout("pconv", [3, 1536])
    sgla_o = dout("sgla_o", [16, 4, 64, 128])
    sgdn_o = dout("sgdn_o", [16, 4, 128, 128])
    sconv_o = dout("sconv_o", [16, 3, 1536])
    x1s = nc.dram_tensor("x1s", [NT, 128, D], F32, kind="Internal").ap()
    dbg = {}
    for k, shp in DEBUG.items():
        dbg[k] = dout("dbg_" + k, shp)

    es = contextlib.ExitStack()
    with es:
        AR_SZ = 33856 + 10240
        A = Arena(nc, es, 212800 - AR_SZ)
        arh = es.enter_context(nc.sbuf_tensor("AR", [128, AR_SZ // 4], F32))
        ar_cur = [0]

        def ar1(shape):
            n = prod(shape)
            ap = arh[:, ar_cur[0]:ar_cur[0] + n]
            ar_cur[0] += (n + 7) // 8 * 8
            assert ar_cur[0] * 4 <= AR_SZ, ar_cur[0] * 4
            if len(shape) > 1:
                names = ["a%d" % i for i in range(len(shape))]
                kw = {nm: sz for nm, sz in zip(names[1:], shape[1:])}
                ap = ap.rearrange("p (%s) -> p %s" % (" ".join(names), " ".join(names)), **kw)
            return ap

        PS = PsumPool(nc, es)
        ps = PS.banks
        P = Prog(nc, untracked=["xs", "sgla", "sgdn", "sconv", "w_in", "w_out", "w_g", "w_u", "w_d",
                                "wgu", "cwT", "consts", "prm", "prm2", "gnc"])

        A.cur["p1"] = 74112
        A.cur["p2"] = 135168
        CT = A.alloc("p1", [NCONST], F32)
        identP = A.persist([128], F32)
        nhalfP = A.persist([8], F32)
        (c_ident, c_Uf, c_Us, c_nUf, c_nUs, c_u16f, c_u16s,
         c_mbuf, c_mbus, c_mblf, c_mbls, c_blk64, c_ones, c_nhalf) = [CT[:, i * 128:(i + 1) * 128] for i in range(14)]
        c_bind = CT[:, 1792:1808]
        CB = A.persist([6 * 128], BF16)
        b_ident, b_mbuf, b_mbus, b_mblf, b_mbls, b_ones = [CB[:, i * 128:(i + 1) * 128] for i in range(6)]
        PRM = A.persist([2568], F32)
        p_gng, p_gnd = PRM[:, 0:128], PRM[:, 128:256]
        p_lng, p_lnb = PRM[:, 256:1280], PRM[:, 1280:2304]
        p_alog, p_dtb, p_bgate = PRM[:, 2304:2308], PRM[:, 2308:2312], PRM[0:1, 2312:2568]
        WGU = A.persist([256], F32, parts=16)
        CWT = A.persist([12, 4], F32)
        NEGA = A.persist([4], F32)
        GNC = A.persist([2], F32)
        SGp = ar1([2, 128])
        SDp = ar1([4, 128])
        rU = {}
        for nm_ in ("Uf", "Us", "nUf", "nUs", "u16f", "u16s"):
            rU[nm_] = ar1([128])
        HALO = A.persist([12, 3], F32)
        stats = A.persist([16], F32)

        win = A.view(0, [8, IN_DIM], BF16)
        wout = A.view(57728, [8, D], BF16)
        wg_sb = A.view(0, [8, DFF], BF16)
        wu_sb = A.view(45056, [8, DFF], BF16)
        wd_sb = A.view(90112, [NFC, D], BF16)

        def a1(shape, dt=F32, parts=128):
            return A.alloc("p1", shape, dt, parts)

        xtb = a1([2, D])
        xT = a1([8, 128], BF16)
        gaT = a1([128], F32, parts=16)
        lz = ar1([256])
        eb = a1([256])
        enb = a1([256])
        qtT = a1([256])
        ktT = ar1([256])
        qtTm = ar1([4, 128])
        ktok = ar1([256])
        vg = ar1([512])
        gate = a1([1024])
        atm = ar1([512])
        og = a1([1024], BF16)
        oT = a1([8, 128], BF16)
        sm = a1([64])
        tokd = A.view(A.cur["p1"], [1536], F32)
        cbuf = a1([12, 176])
        SCV = A.view(A.cur["p1"], [1536], F32, parts=48)
        cv_off = A.cur["p1"]
        cv = a1([12, 128])
        sqb = a1([8, 128], BF16)
        rinv = a1([8, 128])
        junk = rinv[:, 7, :]
        ST2 = A.view(cv_off + 2048, [16, 128], F32)
        qkn = ar1([4, 2, 128])
        gB = ar1([4, 128])
        GT = a1([2, 128])
        G = a1([2, 128])
        Nn = ar1([2, 128])
        Cm = ar1([2, 128])
        Mb = ar1([2, 2, 128])
        MTb = ar1([2, 2, 128])
        Pb = ar1([2, 2, 128])
        XdT = ar1([2, 128])
        Yb = ar1([2, 128])
        XT = ar1([4, 128])
        QKm = ar1([4, 128])
        vb = ar1([512])
        kb = ar1([512])
        kd = ar1([512])
        nwT = ar1([2, 128])
        delta = ar1([2, 128])
        osb = a1([2, 128])
        ST = a1([16, 128])
        EXP = a1([2176])
        VEX = a1([16, 128])
        GT1 = a1([2, 128])
        G1 = a1([2, 128])
        bufsets = [
            (Nn, Cm, Mb, MTb, Pb, XdT, Yb, GT, G),
            (lz.rearrange("p (a b) -> p a b", a=2), ktT.rearrange("p (a b) -> p a b", a=2),
             atm.rearrange("p (a b c) -> p a b c", a=2, b=2), vg.rearrange("p (a b c) -> p a b c", a=2, b=2),
             qtTm.rearrange("p (a b) c -> p a b c", a=2), ktok.rearrange("p (a b) -> p a b", a=2),
             nwT, GT1, G1),
        ]
        nwT4 = Mb.rearrange("p a b c -> p (a b) c")
        delta4 = MTb.rearrange("p a b c -> p (a b) c")
        osb4_p = rinv[:, 0:4, :]
        osb4_s = cv[:, 0:4, :]
        p1_end = A.cur["p1"]

        ssq = sm[:, 0:8]
        rstd = sm[:, 8:16]
        dab = sm[:, 16:24]
        gg = ar1([4])
        beta = sm[:, 28:32]
        gcs = sm[:, 32:36]
        ngc = sm[:, 36:40]
        egc = sm[:, 40:44]
        bgs = sm[:, 44:48]
        egl = sm[:, 48:52]
        tmp4 = sm[:, 52:56]
        gcl = sm[:, 56:60]
        decs = a1([64])

        P.dma("sp", CT, consts_i)
        P.copy("dve", identP, c_ident)
        P.copy("dve", nhalfP, c_nhalf[:, 0:8])
        P.dma("sp", PRM, prm_i)
        P.dma("sp", WGU, wgu_i)
        P.dma("sp", GNC, gnc_i)
        P.dma("sp", CWT.rearrange("p a b -> p (a b)"), cwT_i)
        for (c0, c1) in ((1024, 1552), (3088, 3608), (1552, 2064), (2064, 2576), (2576, 3088), (0, 1024)):
            for k in range(8):
                P.dma("pool", win[:, k, c0:c1], w_in[k * 128:(k + 1) * 128, c0:c1])
        for k in range(8):
            P.dma("pool", wout[:, k, :], w_out[k * 128:(k + 1) * 128, :])
        for i, src in enumerate([c_ident, c_mbuf, c_mbus, c_mblf, c_mbls, c_ones]):
            P.copy("dve", CB[:, i * 128:(i + 1) * 128], src)
        P.act(NEGA, p_alog, AF.Exp)
        P.ts("dve", NEGA, NEGA, -1.0, None, ALU.mult)
        P.ts("pool", R(SGp.rearrange("p a b -> p (a b)")), CT[:, 0:256], 0.0, None, ALU.mult)
        P.ts("pool", R(SDp.rearrange("p a b -> p (a b)")), CT[:, 0:512], 0.0, None, ALU.mult)
        P.memset("pool", HALO, 0.0)
        P.memset("pool", EXP, 0.0)
        for nm_, src_ in (("Uf", c_Uf), ("Us", c_Us), ("nUf", c_nUf), ("nUs", c_nUs), ("u16f", c_u16f), ("u16s", c_u16s)):
            P.copy("dve", R(rU[nm_]), src_)

        conv_hoisted = {}

        def conv_gen():
            P.copy("dve", cbuf[:, :, 0:3], HALO)
            for g3 in range(3):
                bd = PS.alloc()
                for j in range(4):
                    c0 = OFF_DQKV + (g3 * 4 + j) * 128
                    for k in range(8):
                        P.mm(ps[bd][:, j * 128:(j + 1) * 128], win[:, k, c0:c0 + 128], xT[:, k, :], start=(k == 0), stop=(k == 7))
                for j in range(4):
                    P.copy("act", cbuf[:, g3 * 4 + j, 3:131], ps[bd][:, j * 128:(j + 1) * 128])
                PS.release(bd)
                yield
            P.copy("dve", HALO, cbuf[:, :, 128:131])
            for cc in range(12):
                P.act(cv[:, cc, :], cbuf[:, cc, 0:128], AF.Copy, scale=CWT[:, cc, 0:1])
                for i in range(1, 4):
                    P.stt(cv[:, cc, :], cbuf[:, cc, i:i + 128], CWT[:, cc, i:i + 1], cv[:, cc, :], ALU.mult, ALU.add)
                yield

        def front(t):
            xt = xtb[:, tile_pos[t] % 2, :]
            P.dma("sp", xt, xs[t])
            b0, b1 = PS.alloc(), PS.alloc()
            for k in range(8):
                bk = ps[b0] if k < 4 else ps[b1]
                P.tr(bk[:, (k % 4) * 128:(k % 4 + 1) * 128], xt[:, k * 128:(k + 1) * 128], c_ident)
            P.act(xT[:, 0:4, :].rearrange("p a b -> p (a b)"), ps[b0][:, :], AF.Copy)
            P.copy("dve", xT[:, 4:8, :].rearrange("p a b -> p (a b)"), ps[b1][:, :])
            PS.release(b0)
            PS.release(b1)

        def phase1_tile(t):
            sample = (t == NT - 1)
            osb4 = osb4_s if sample else osb4_p
            nseq = 16 if sample else 1
            U = rU["Us"] if sample else rU["Uf"]
            nU = rU["nUs"] if sample else rU["nUf"]
            u16 = rU["u16s"] if sample else rU["u16f"]
            m01 = c_Us if sample else c_Uf
            mbu = b_mbus if sample else b_mbuf
            mbl = b_mbls if sample else b_mblf
            Ulast = rU["Us"][:, 7:128:8] if sample else rU["Uf"][:, 126:128]
            nlast = 16 if sample else 2
            nlev = 3 if sample else 6

            xt = xtb[:, tile_pos[t] % 2, :]
            rr = xt

            def mm_tok(out_ap, c0, n):
                for k in range(8):
                    P.mm(out_ap, xT[:, k, :], win[:, k, c0:c0 + n], start=(k == 0), stop=(k == 7))

            def mm_feat(out_ap, c0, m):
                for k in range(8):
                    P.mm(out_ap, win[:, k, c0:c0 + m], xT[:, k, :], start=(k == 0), stop=(k == 7))

            bg_ = PS.alloc()
            mm_tok(ps[bg_][:, :], OFF_GG, 512)
            P.act(gate[:, 0:512], ps[bg_][:, :], AF.Silu)
            PS.release(bg_)
            bg_ = PS.alloc()
            mm_tok(ps[bg_][:, :], OFF_DG, 512)
            P.act(gate[:, 512:1024], ps[bg_][:, :], AF.Silu)
            PS.release(bg_)

            if sample:
                P.dma("sp", SCV, sconv_i)
                bh0, bh1 = PS.alloc(), PS.alloc()
                for cc in range(12):
                    bk = ps[bh0] if cc < 8 else ps[bh1]
                    P.tr(bk[:, (cc % 8) * 48:(cc % 8 + 1) * 48], SCV[:, cc * 128:(cc + 1) * 128], c_ident[0:48, 0:48])
                for cc in range(12):
                    bk = ps[bh0] if cc < 8 else ps[bh1]
                    P.copy("dve" if cc % 2 else "act",
                           cbuf[:, cc, :].rearrange("p (s w) -> p s w", w=11)[:, :, 0:3],
                           bk[:, (cc % 8) * 48:(cc % 8 + 1) * 48].rearrange("p (s r) -> p s r", r=3))
                PS.release(bh0)
                PS.release(bh1)
                for g3 in range(3):
                    bd = PS.alloc()
                    for j in range(4):
                        mm_feat(ps[bd][:, j * 128:(j + 1) * 128], OFF_DQKV + (g3 * 4 + j) * 128, 128)
                    for j in range(4):
                        cc = g3 * 4 + j
                        P.copy("act", cbuf[:, cc, :].rearrange("p (s w) -> p s w", w=11)[:, :, 3:11],
                               ps[bd][:, j * 128:(j + 1) * 128].rearrange("p (s t) -> p s t", t=8))
                    PS.release(bd)
                for cc in range(12):
                    def tap(i):
                        return cbuf[:, cc, :].rearrange("p (s w) -> p s w", w=11)[:, :, i:i + 8]
                    dst = cv[:, cc, :].rearrange("p (s t) -> p s t", t=8)
                    P.act(dst, tap(0), AF.Copy, scale=CWT[:, cc, 0:1])
                    for i in range(1, 4):
                        P.stt(dst, tap(i), CWT[:, cc, i:i + 1], dst, ALU.mult, ALU.add)
            elif not conv_hoisted.get(t):
                for _ in conv_gen():
                    pass
            P.act(cv.rearrange("p a b -> p (a b)"), cv.rearrange("p a b -> p (a b)"), AF.Silu)

            def run_il(gens):
                gens = list(gens)
                while gens:
                    for g in list(gens):
                        try:
                            next(g)
                        except StopIteration:
                            gens.remove(g)

            def gdn_gen():
                P.act(sqb.rearrange("p a b -> p (a b)"), cv[:, 0:8, :].rearrange("p a b -> p (a b)"), AF.Square)
                bn0, bn1 = PS.alloc(), PS.alloc()
                for cc in range(8):
                    bk = ps[bn0] if cc < 4 else ps[bn1]
                    P.mm(bk[:, (cc % 4) * 128:(cc % 4 + 1) * 128], b_ones, sqb[:, cc, :])
                P.act(rinv[:, 0:4, :].rearrange("p a b -> p (a b)"), ps[bn0][:, :], AF.Ln, bias=1e-6)
                P.act(rinv[:, 4:8, :].rearrange("p a b -> p (a b)"), ps[bn1][:, :], AF.Ln, bias=1e-6)
                PS.release(bn0)
                yield
                PS.release(bn1)
                yield
                P.act(rinv.rearrange("p a b -> p (a b)"), rinv.rearrange("p a b -> p (a b)"), AF.Exp, scale=-0.5)
                for h in range(4):
                    P.stt(R(qkn[:, h, 1, :]), cv[:, h, :], 128.0 ** -0.5, rinv[:, h, :], ALU.mult, ALU.mult)
                    P.tt("dve", R(qkn[:, h, 0, :]), cv[:, 4 + h, :], rinv[:, 4 + h, :], ALU.mult)
                btk, btv = PS.alloc(), PS.alloc()
                for h in range(4):
                    P.tr(ps[btk][:, h * 128:(h + 1) * 128], qkn[:, h, 0, :], c_ident)
                    P.tr(ps[btv][:, h * 128:(h + 1) * 128], cv[:, 8 + h, :], c_ident)
                yield "L2DONE"
                bgc = PS.alloc()
                P.mm(ps[bgc][:, 0:4], R(U), R(gg))
                P.copy("dve", gcs, ps[bgc][:, 0:4])
                P.ts("dve", ngc, ps[bgc][:, 0:4], -1.0, None, ALU.mult)
                P.act(egc, ps[bgc][:, 0:4], AF.Exp)
                for h in range(4):
                    P.mm(ps[bgc][:, 64 + h * 16:64 + h * 16 + nlast], R(gB[:, h, :]), R(Ulast))
                for h in range(4):
                    P.act(decs[:, h * 16:h * 16 + nseq], ps[bgc][:, 64 + h * 16 + nlast - nseq:64 + h * 16 + nlast], AF.Exp)
                if not sample:
                    P.copy("dve", gcl, ps[bgc][:, 65:129:16])
                PS.release(bgc)
                yield
                P.tt("dve", bgs, beta, egc, ALU.mult)
                if not sample:
                    for h in range(4):
                        P.act(egl[:, h:h + 1], gcs[:, h:h + 1], AF.Exp, scale=-1.0, bias=gcl[:, h:h + 1])
                    for h in range(4):
                        hs = slice(h * 128, (h + 1) * 128)
                        P.ts("dve", R(vb[:, hs]), ps[btv][:, hs], beta[:, h:h + 1], None, ALU.mult)
                        P.ts("dve", R(kb[:, hs]), ps[btk][:, hs], bgs[:, h:h + 1], None, ALU.mult)
                        P.act(R(kd[:, hs]), ps[btk][:, hs], AF.Copy, scale=egl[:, h:h + 1])
                    PS.release(btk)
                    PS.release(btv)
                    yield
                bkq0, bkq1 = PS.alloc(), PS.alloc()
                for h in range(4):
                    bk = ps[bkq0] if h < 2 else ps[bkq1]
                    P.mm(bk[:, (h % 2) * 256:(h % 2 + 1) * 256], R(qkn[:, h, 0, :]), R(qkn[:, h, :, :].rearrange("p a b -> p (a b)")))

                while not gla_done[0]:
                    yield

                def pair_gen(hp):
                    Nn, Cm, Mb, MTb, Pb, XdT, Yb, GT, G = bufsets[hp]
                    bpa, bpb = PS.alloc(), PS.alloc()
                    for i in range(2):
                        h = hp * 2 + i
                        pa = ps[bpa][:, i * 128:(i + 1) * 128]
                        pb = ps[bpb][:, i * 128:(i + 1) * 128]
                        P.mm(pa, R(gB[:, h, :]), R(U), start=True, stop=False)
                        P.mm(pa, b_ident, mbu, start=False, stop=True)
                        P.mm(pb, R(gB[:, h, :]), R(nU), start=True, stop=False)
                        P.mm(pb, b_ident, mbl, start=False, stop=True)
                    for i in range(2):
                        h = hp * 2 + i
                        P.act(GT[:, i, :], ps[bpa][:, i * 128:(i + 1) * 128], AF.Exp, bias=ngc[:, h:h + 1])
                        P.act(G[:, i, :], ps[bpb][:, i * 128:(i + 1) * 128], AF.Exp, bias=gcs[:, h:h + 1])
                    PS.release(bpa)
                    yield
                    PS.release(bpb)
                    yield
                    for i in range(2):
                        h = hp * 2 + i
                        kq = (ps[bkq0] if h < 2 else ps[bkq1])[:, (h % 2) * 256:(h % 2 + 1) * 256]
                        P.stt(R(Nn[:, i, :]), kq[:, 0:128], beta[:, h:h + 1], G[:, i, :], ALU.mult, ALU.mult)
                        P.tt("dve", R(QKm[:, h, :]), kq[:, 128:256], GT[:, i, :], ALU.mult)
                        if sample:
                            P.add("dve", lambda e, i=i, h=h: e.reduce_sum(egl[:, h:h + 1], GT[:, i, 7:128:8], mybir.AxisListType.X),
                                  [GT[:, i, 7:128:8]], [egl[:, h:h + 1]])
                        if not sample:
                            P.tt("dve", R(Cm[:, i, :]), Nn[:, i, :], c_blk64, ALU.mult)
                            P.tt("dve", R(Nn[:, i, :]), Nn[:, i, :], Cm[:, i, :], ALU.subtract)
                            nd = Cm[:, i, :]
                        else:
                            nd = Nn[:, i, :]
                        P.act(R(Mb[:, 0, i, :]), nd, AF.Copy, scale=-1.0)
                        P.stt(R(Pb[:, 0, i, :]), nd, -1.0, c_ident, ALU.mult, ALU.add)
                    bx = PS.alloc()
                    for i in range(2):
                        nd = Nn[:, i, :] if sample else Cm[:, i, :]
                        P.tr(ps[bx][:, i * 128:(i + 1) * 128], nd, c_ident)
                    P.act(R(MTb[:, 0, 0, :]), ps[bx][:, 0:128], AF.Copy, scale=-1.0)
                    P.ts("dve", R(MTb[:, 0, 1, :]), ps[bx][:, 128:256], -1.0, None, ALU.mult)
                    PS.release(bx)
                    yield
                    for k in range(1, nlev):
                        a, b = (k - 1) % 2, k % 2
                        bm = PS.alloc()
                        for i in range(2):
                            P.mm(ps[bm][:, i * 128:(i + 1) * 128], R(Mb[:, a, i, :]), R(MTb[:, a, i, :]))
                            if k < nlev - 1:
                                P.mm(ps[bm][:, 256 + i * 128:256 + (i + 1) * 128], R(MTb[:, a, i, :]), R(Mb[:, a, i, :]))
                        P.copy("act", R(MTb[:, b, :, :].rearrange("p a b -> p (a b)")), ps[bm][:, 0:256])
                        if k < nlev - 1:
                            P.copy("dve", R(Mb[:, b, :, :].rearrange("p a b -> p (a b)")), ps[bm][:, 256:512])
                        PS.release(bm)
                        yield
                        bp = PS.alloc()
                        for i in range(2):
                            P.mm(ps[bp][:, i * 128:(i + 1) * 128], R(MTb[:, b, i, :]), R(Pb[:, a, i, :]))
                        P.tt("dve", R(Pb[:, b, :, :].rearrange("p a b -> p (a b)")), ps[bp][:, 0:256],
                             Pb[:, a, :, :].rearrange("p a b -> p (a b)"), ALU.add)
                        PS.release(bp)
                        yield
                    fin = (nlev - 1) % 2
                    bx = PS.alloc()
                    for i in range(2):
                        P.tr(ps[bx][:, i * 128:(i + 1) * 128], Pb[:, fin, i, :], c_ident)
                    for i in range(2):
                        h = hp * 2 + i
                        dst = R(XT[:, h, :]) if sample else R(XdT[:, i, :])
                        P.copy("act", dst, ps[bx][:, i * 128:(i + 1) * 128])
                    PS.release(bx)
                    yield
                    if not sample:
                        by = PS.alloc()
                        for i in range(2):
                            P.mm(ps[by][:, i * 128:(i + 1) * 128], R(Nn[:, i, :]), R(XdT[:, i, :]))
                        for i in range(2):
                            P.copy("act", R(Yb[:, i, :]), ps[by][:, i * 128:(i + 1) * 128])
                        for i in range(2):
                            P.mm(ps[by][:, 256 + i * 128:256 + (i + 1) * 128], R(Pb[:, fin, i, :]), R(Yb[:, i, :]))
                        for i in range(2):
                            h = hp * 2 + i
                            P.tt("dve", R(XT[:, h, :]), XdT[:, i, :], ps[by][:, 256 + i * 128:256 + (i + 1) * 128], ALU.subtract)
                        PS.release(by)
                        yield

                run_il([pair_gen(0), pair_gen(1)])
                PS.release(bkq0)
                yield
                PS.release(bkq1)
                yield
                if sample:
                    for h in range(4):
                        hs = slice(h * 128, (h + 1) * 128)
                        P.ts("dve", R(vb[:, hs]), ps[btv][:, hs], beta[:, h:h + 1], None, ALU.mult)
                        P.ts("dve", R(kb[:, hs]), ps[btk][:, hs], bgs[:, h:h + 1], None, ALU.mult)
                        P.act(R(kd[:, hs]), ps[btk][:, hs], AF.Copy, scale=egl[:, h:h + 1])
                    PS.release(btk)
                    yield
                    PS.release(btv)
                yield


            if sample or t == NT - 2:
                for g3 in range(3):
                    bd = PS.alloc()
                    mm_tok(ps[bd][:, :], OFF_DQKV + g3 * 512, 512)
                    P.copy("dve", tokd[:, g3 * 512:(g3 + 1) * 512], ps[bd][:, :])
                    PS.release(bd)
                if sample:
                    for s in range(16):
                        P.dma("sp", sconv_o[s], tokd[s * 8 + 5:s * 8 + 8, :])
                else:
                    P.dma("sp", pconv_o, tokd[125:128, :])

            bv = PS.alloc()
            mm_tok(ps[bv][:, :], OFF_GV, 512)
            P.copy("dve", R(vg), ps[bv][:, :])
            PS.release(bv)
            bqk = PS.alloc()
            for j in range(4):
                mm_feat(ps[bqk][:, j * 128:(j + 1) * 128], OFF_GQ + j * 128, 128)
            bga = PS.alloc()
            mm_feat(ps[bga][0:16, 0:128], OFF_GA, 16)
            P.copy("dve", gaT, ps[bga][0:16, 0:128])
            mm_tok(ps[bga][:, 128:136], OFF_DAB, 8)
            P.copy("dve", dab, ps[bga][:, 128:136])
            P.mm(ps[bga][:, 256:512], gaT, WGU, start=True, stop=False)
            P.mm(ps[bga][:, 256:512], c_ones[0:1, :], p_bgate, start=False, stop=True)
            if tile_pos[t] + 1 < len(p1_order):
                pass
            elif KP2 > 0:
                for k in range(8):
                    P.dma("pool", wg_sb[:, k, :], w_g[k * 128:(k + 1) * 128, :])
                for k in range(2):
                    P.dma("pool", wu_sb[:, k, :], w_u[k * 128:(k + 1) * 128, :])
                wg_loaded.append(1)

            gdn = gdn_gen()
            for _m in gdn:
                if _m == "L2DONE":
                    break

            if tile_pos[t] + 1 < len(p1_order):
                front(p1_order[tile_pos[t] + 1])

            P.act(R(lz), ps[bga][:, 256:512], AF.Exp, scale=-1.0)
            PS.release(bga)
            P.act(R(lz), lz, AF.Ln, bias=1.0)
            bb = PS.alloc()
            for c in range(2):
                P.mm(ps[bb][:, c * 128:(c + 1) * 128], R(lz[:, c * 128:(c + 1) * 128]), R(u16))
            P.act(eb, ps[bb][:, 0:256], AF.Exp)
            P.act(enb, ps[bb][:, 0:256], AF.Exp, scale=-1.0)
            PS.release(bb)
            P.stt(qtT, ps[bqk][:, 0:256], 0.125, eb, ALU.mult, ALU.mult)
            P.tt("dve", R(ktT), ps[bqk][:, 256:512], enb, ALU.mult)
            PS.release(bqk)
            for h in range(4):
                P.act(R(qtTm[:, h, :]), qtT[:, (h // 2) * 128:(h // 2 + 1) * 128], AF.Copy,
                      scale=c_blk64[:, (h % 2) * 64:(h % 2) * 64 + 1])
            P.tt("dve", tmp4, dab[:, 0:4], p_dtb, ALU.add)
            P.act(tmp4, tmp4, AF.Exp)
            P.act(tmp4, tmp4, AF.Ln, bias=1.0)
            P.tt("dve", R(gg), tmp4, NEGA, ALU.mult)
            P.act(beta, dab[:, 4:8], AF.Exp, scale=-1.0)
            P.ts("dve", beta, beta, 1.0, None, ALU.add)
            P.add("dve", lambda e: e.reciprocal(beta, beta), [beta], [beta])
            for h in range(4):
                P.act(R(gB[:, h, :]), c_ones, AF.Copy, scale=gg[:, h:h + 1])


            def gla_gen():
                bt = PS.alloc()
                for c in range(2):
                    P.tr(ps[bt][:, c * 128:(c + 1) * 128], ktT[:, c * 128:(c + 1) * 128], c_ident)
                P.copy("act", R(ktok), ps[bt][:, 0:256])
                PS.release(bt)
                yield
                bat = PS.alloc()
                for h in range(4):
                    c, r = h // 2, h % 2
                    rs = slice(r * 64, (r + 1) * 64)
                    P.mm(ps[bat][:, h * 128:(h + 1) * 128], R(ktT[:, c * 128:(c + 1) * 128]), R(qtTm[:, h, :]))
                for h in range(4):
                    P.tt("dve", R(atm[:, h * 128:(h + 1) * 128]), ps[bat][:, h * 128:(h + 1) * 128], m01, ALU.mult)
                PS.release(bat)
                yield
                bo = PS.alloc()
                for c in range(2):
                    if sample:
                        P.dma("sp", ST, sgla_i[:, 2 * c:2 * c + 2].rearrange("s r d v -> (r d) s v"))
                    for r in range(2):
                        h = 2 * c + r
                        rs = slice(r * 64, (r + 1) * 64)
                        oh = ps[bo][:, h * 128:(h + 1) * 128]
                        P.mm(oh, R(atm[:, h * 128:(h + 1) * 128]), R(vg[:, h * 128:(h + 1) * 128]), start=True, stop=False)
                        if not sample:
                            P.mm(oh, R(qtTm[:, h, :]), R(SGp[:, c, :]), start=False, stop=True)
                        else:
                            P.copy("dve", EXP.rearrange("p (s w) -> p s w", w=136)[:, :, 0:8],
                                   qtTm[:, h, :].rearrange("p (s t) -> p s t", t=8))
                            for s in range(16):
                                P.mm(oh, EXP[:, s * 128:(s + 1) * 128], ST[:, s, :], start=False, stop=(s == 15))
                    for r in range(2):
                        h = 2 * c + r
                        rs = slice(r * 64, (r + 1) * 64)
                        if not sample:
                            bs = PS.alloc()
                            P.mm(ps[bs][:, 0:128], R(ktok[:, c * 128:(c + 1) * 128]), R(vg[:, h * 128:(h + 1) * 128]))
                            ebl = eb[rs, c * 128 + 127:c * 128 + 128]
                            P.stt(R(SGp[rs, c, :]), ps[bs][rs, 0:128], 1.0, SGp[rs, c, :], ALU.mult, ALU.add)
                            P.ts("dve", R(SGp[rs, c, :]), SGp[rs, c, :], ebl, None, ALU.mult)
                            PS.release(bs)
                            yield
                        else:
                            for s in range(16):
                                if s % 2:
                                    P.ts("dve", VEX[:, s, :], vg[:, h * 128:(h + 1) * 128], c_bind[:, s:s + 1], None, ALU.mult)
                                else:
                                    P.act(VEX[:, s, :], vg[:, h * 128:(h + 1) * 128], AF.Copy, scale=c_bind[:, s:s + 1])
                            for sg in range(4):
                                bs = PS.alloc()
                                P.mm(ps[bs][:, :], ktok[:, c * 128:(c + 1) * 128],
                                     VEX[:, sg * 4:(sg + 1) * 4, :].rearrange("p a b -> p (a b)"))
                                for s4 in range(4):
                                    s = sg * 4 + s4
                                    ebl = eb[rs, c * 128 + 8 * s + 7:c * 128 + 8 * s + 8]
                                    P.stt(ST[rs, s, :], ps[bs][rs, s4 * 128:(s4 + 1) * 128], 1.0, ST[rs, s, :], ALU.mult, ALU.add)
                                    P.ts("dve", ST[rs, s, :], ST[rs, s, :], ebl, None, ALU.mult)
                                PS.release(bs)
                                yield
                    if sample:
                        P.dma("sp", sgla_o[:, 2 * c:2 * c + 2].rearrange("s r d v -> (r d) s v"), ST)
                for h in range(4):
                    P.act(junk, ps[bo][:, h * 128:(h + 1) * 128], AF.Square, accum_out=ssq[:, h:h + 1])
                P.ts("dve", rstd[:, 0:4], ssq[:, 0:4], 1.0 / 128.0, 1e-5, ALU.mult, ALU.add)
                P.tt("pool", rstd[:, 0:4], rstd[:, 0:4], c_nhalf[:, 0:4], ALU.pow)
                for h in range(4):
                    P.stt(og[:, h * 128:(h + 1) * 128], ps[bo][:, h * 128:(h + 1) * 128], rstd[:, h:h + 1],
                          gate[:, h * 128:(h + 1) * 128], ALU.mult, ALU.mult)
                PS.release(bo)
                yield


            gla_done = [False]

            def gla_wrapped():
                yield from gla_gen()
                gla_done[0] = True

            run_il([gla_wrapped(), gdn])

            def head_gen(h):
                i = h
                hs = slice(h * 128, (h + 1) * 128)
                STh = (ST if h % 2 == 0 else ST2) if sample else None
                if sample:
                    P.dma("sp", STh, sgdn_i[:, h].rearrange("s d v -> d s v"))

                def S_of(s):
                    return STh[:, s, :] if sample else SDp[:, h, :]

                def state_terms(out_ap, srcT, first):
                    if not sample:
                        P.mm(out_ap, R(srcT), R(S_of(0)), start=first, stop=True)
                    else:
                        P.copy("dve", EXP.rearrange("p (s w) -> p s w", w=136)[:, :, 0:8],
                               srcT.rearrange("p (s t) -> p s t", t=8))
                        for s in range(16):
                            P.mm(out_ap, EXP[:, s * 128:(s + 1) * 128], S_of(s), start=(first and s == 0), stop=(s == 15))

                bw = PS.alloc()
                P.mm(ps[bw][:, 0:128], R(kb[:, hs]), R(XT[:, h, :]))
                P.act(R(nwT4[:, i, :]), ps[bw][:, 0:128], AF.Copy, scale=-1.0)
                yield
                P.mm(ps[bw][:, 128:256], R(XT[:, h, :]), R(vb[:, hs]), start=True, stop=False)
                state_terms(ps[bw][:, 128:256], nwT4[:, i, :], False)
                P.copy("act", R(delta4[:, i, :]), ps[bw][:, 128:256])
                yield
                state_terms(ps[bw][:, 256:384], qkn[:, h, 1, :], True)
                P.mm(ps[bw][:, 384:512], R(QKm[:, h, :]), R(delta4[:, i, :]))
                P.act(osb4[:, i, :], ps[bw][:, 256:384], AF.Copy, scale=egc[:, h:h + 1])
                P.tt("dve", osb4[:, i, :], osb4[:, i, :], ps[bw][:, 384:512], ALU.add)
                yield
                PS.release(bw)
                yield
                P.act(junk, osb4[:, i, :], AF.Square, accum_out=ssq[:, 4 + h:5 + h])
                P.ts("dve", rstd[:, 4 + h:5 + h], ssq[:, 4 + h:5 + h], 1.0 / 128.0, 1e-5, ALU.mult, ALU.add)
                P.tt("pool", rstd[:, 4 + h:5 + h], rstd[:, 4 + h:5 + h], c_nhalf[:, 0:1], ALU.pow)
                P.stt(og[:, 512 + h * 128:512 + (h + 1) * 128], osb4[:, i, :], rstd[:, 4 + h:5 + h],
                      gate[:, 512 + h * 128:512 + (h + 1) * 128], ALU.mult, ALU.mult)
                yield
                if not sample:
                    bs = PS.alloc()
                    P.mm(ps[bs][:, 0:128], R(kd[:, hs]), R(delta4[:, i, :]))
                    P.stt(R(SDp[:, h, :]), SDp[:, h, :], decs[:, h * 16:h * 16 + 1], ps[bs][:, 0:128], ALU.mult, ALU.add)
                    PS.release(bs)
                    yield
                else:
                    for s in range(16):
                        if s % 2:
                            P.ts("dve", VEX[:, s, :], delta4[:, i, :], c_bind[:, s:s + 1], None, ALU.mult)
                        else:
                            P.act(VEX[:, s, :], delta4[:, i, :], AF.Copy, scale=c_bind[:, s:s + 1])
                    for sg in range(4):
                        bs = PS.alloc()
                        P.mm(ps[bs][:, :], kd[:, hs], VEX[:, sg * 4:(sg + 1) * 4, :].rearrange("p a b -> p (a b)"))
                        for s4 in range(4):
                            s = sg * 4 + s4
                            P.stt(STh[:, s, :], STh[:, s, :], decs[:, h * 16 + s:h * 16 + s + 1],
                                  ps[bs][:, s4 * 128:(s4 + 1) * 128], ALU.mult, ALU.add)
                        PS.release(bs)
                    P.dma("sp", sgdn_o[:, h].rearrange("s d v -> d s v"), STh)


            if sample:
                run_il([head_gen(0), head_gen(1)])
                run_il([head_gen(2), head_gen(3)])
            else:
                gens_ = [head_gen(0), head_gen(1), head_gen(2), head_gen(3)]
                if tile_pos[t] + 1 < len(p1_order):
                    conv_hoisted[p1_order[tile_pos[t] + 1]] = True
                    gens_.append(conv_gen())
                run_il(gens_)

            bto = PS.alloc()
            pb16 = ps[bto].bitcast(BF16)
            for e8 in range(8):
                P.tr(pb16[:, e8 * 128:(e8 + 1) * 128], og[:, e8 * 128:(e8 + 1) * 128], b_ident)
            P.act(oT[:, 0:4, :].rearrange("p a b -> p (a b)"), pb16[:, 0:512], AF.Copy, scale=GNC[:, 0:1])
            P.ts("dve", oT[:, 4:8, :].rearrange("p a b -> p (a b)"), pb16[:, 512:1024], GNC[:, 1:2], None, ALU.mult)
            PS.release(bto)
            bm0, bm1 = PS.alloc(), PS.alloc()
            for half, bk in ((0, bm0), (1, bm1)):
                for e8 in range(8):
                    P.mm(ps[bk][:, :], oT[:, e8, :], wout[:, e8, half * 512:(half + 1) * 512], start=(e8 == 0), stop=(e8 == 7))
            for half, bk in ((0, bm0), (1, bm1)):
                P.stt(rr[:, half * 512:(half + 1) * 512], xt[:, half * 512:(half + 1) * 512], ALPHA, ps[bk][:, :], ALU.mult, ALU.add)
            PS.release(bm0)
            PS.release(bm1)
            if "rr0" in dbg and t == 0:
                P.dma("sp", dbg["rr0"], rr)
                P.dma("sp", dbg["sm"], sm)
                P.dma("sp", dbg["gate"], gate)
                P.dma("sp", dbg["xt"], xt)
            layer_norm(rr, p_lng, p_lnb)
            if "x1" in dbg and t == 0:
                P.dma("sp", dbg["x1"], rr)
            P.dma("sp", x1s[t], rr)

        def layer_norm(buf, gB_, bB_):
            for half in range(2):
                P.add("dve", lambda e, half=half: e.bn_stats(stats[:, half * 6:(half + 1) * 6], buf[:, half * 512:(half + 1) * 512]),
                      [buf[:, half * 512:(half + 1) * 512]], [stats[:, half * 6:(half + 1) * 6]])
            P.add("dve", lambda e: e.bn_aggr(stats[:, 12:14], stats[:, 0:12]), [stats[:, 0:12]], [stats[:, 12:14]])
            P.ts("dve", stats[:, 14:15], stats[:, 13:14], 1e-5, None, ALU.add)
            P.tt("pool", stats[:, 14:15], stats[:, 14:15], nhalfP[:, 0:1], ALU.pow)
            P.ts("dve", buf, buf, stats[:, 12:13], stats[:, 14:15], ALU.subtract, ALU.mult)
            P.tt("dve", buf, buf, gB_, ALU.mult)
            P.tt("dve", buf, buf, bB_, ALU.add)

        dbg_taps = {}
        import os as _os
        order = [NT - 1] + list(range(NT - 1))
        if _os.environ.get("KORDER") == "p":
            order = list(range(NT))
        import os
        KP1 = int(os.environ.get("KP1", "17"))
        KP2 = int(os.environ.get("KP2", "17"))
        wg_loaded = []
        p1_order = order[:KP1]
        tile_pos = {t: i for i, t in enumerate(p1_order)}
        if p1_order:
            front(p1_order[0])
        for t in p1_order:
            phase1_tile(t)
        P.dma("sp", pgla_o.rearrange("(c r) d v -> (r d) c v", r=2), SGp)
        P.dma("sp", pgdn_o.rearrange("h d v -> d h v"), SDp)

        def a2(shape, dt=F32, parts=128):
            return A.alloc("p2", shape, dt, parts)

        x2b = a2([2, D])
        x2T = a2([8, 128], BF16)
        hidT = a2([NFC, 128], BF16)
        hidg = a2([2, 512], BF16)
        if not wg_loaded:
            for k in range(8):
                P.dma("pool", wg_sb[:, k, :], w_g[k * 128:(k + 1) * 128, :])
        for k in range(2 if wg_loaded else 0, 8):
            P.dma("pool", wu_sb[:, k, :], w_u[k * 128:(k + 1) * 128, :])
        for f in range(NFC):
            P.dma("pool", wd_sb[:, f, :], w_d[f * 128:(f + 1) * 128, :])
        P.dma("sp", PRM[:, 256:2304], prm2_i)

        def front2(t):
            x2 = x2b[:, pos2[t] % 2, :]
            P.dma("sp", x2, x1s[t])
            b0, b1 = PS.alloc(), PS.alloc()
            for k in range(8):
                bk = ps[b0] if k < 4 else ps[b1]
                P.tr(bk[:, (k % 4) * 128:(k % 4 + 1) * 128], x2[:, k * 128:(k + 1) * 128], identP)
            P.act(x2T[:, 0:4, :].rearrange("p a b -> p (a b)"), ps[b0][:, :], AF.Copy)
            P.copy("dve", x2T[:, 4:8, :].rearrange("p a b -> p (a b)"), ps[b1][:, :])
            PS.release(b0)
            PS.release(b1)

        def phase2_tile(t):
            x2 = x2b[:, pos2[t] % 2, :]
            def hid_transposes(gi, g0, n):
                hg = hidg[:, gi % 2, :]
                bt_ = PS.alloc()
                pb16 = ps[bt_].bitcast(BF16)
                nf = n // 128
                for j in range(nf):
                    P.tr(pb16[:, j * 128:(j + 1) * 128], hg[:, j * 128:(j + 1) * 128], b_ident)
                f0 = g0 // 128
                if gi % 2 == 0:
                    P.act(hidT[:, f0:f0 + nf, :].rearrange("p a b -> p (a b)"), pb16[:, 0:nf * 128], AF.Copy)
                else:
                    P.copy("dve", hidT[:, f0:f0 + nf, :].rearrange("p a b -> p (a b)"), pb16[:, 0:nf * 128])
                PS.release(bt_)

            pending = None
            for gi, g0 in enumerate(range(0, DFF, 512)):
                n = min(512, DFF - g0)
                hg = hidg[:, gi % 2, :]
                bgt, bup = PS.alloc(), PS.alloc()
                for k in range(8):
                    P.mm(ps[bgt][:, 0:n], x2T[:, k, :], wg_sb[:, k, g0:g0 + n], start=(k == 0), stop=(k == 7))
                for k in range(8):
                    P.mm(ps[bup][:, 0:n], x2T[:, k, :], wu_sb[:, k, g0:g0 + n], start=(k == 0), stop=(k == 7))
                P.act(hg[:, 0:n], ps[bgt][:, 0:n], AF.Silu)
                P.tt("dve", hg[:, 0:n], hg[:, 0:n], ps[bup][:, 0:n], ALU.mult)
                PS.release(bgt)
                PS.release(bup)
                if pending is not None:
                    hid_transposes(*pending)
                pending = (gi, g0, n)
            if pos2[t] + 1 < len(p2_order):
                front2(p2_order[pos2[t] + 1])
            hid_transposes(*pending)
            bm0, bm1 = PS.alloc(), PS.alloc()
            for half, bk in ((0, bm0), (1, bm1)):
                for f in range(NFC):
                    P.mm(ps[bk][:, :], hidT[:, f, :], wd_sb[:, f, half * 512:(half + 1) * 512], start=(f == 0), stop=(f == NFC - 1))
            for half, bk in ((0, bm0), (1, bm1)):
                P.stt(x2[:, half * 512:(half + 1) * 512], x2[:, half * 512:(half + 1) * 512], ALPHA, ps[bk][:, :], ALU.mult, ALU.add)
            PS.release(bm0)
            PS.release(bm1)
            layer_norm(x2, p_lng, p_lnb)
            P.dma("sp", y_o[t], x2)

        p2_order = order[:KP2]
        pos2 = {t: i for i, t in enumerate(p2_order)}
        if p2_order:
            front2(p2_order[0])
        for t in p2_order:
            phase2_tile(t)
        P.barrier("sp", [y_o, pgla_o, pgdn_o, pconv_o, sgla_o, sgdn_o, sconv_o])
        st = P.emit(es)
        print("ops", len(P.ops), "per-engine (instr, waits):", st, "p1_end", p1_end, "top", A.top)
    return nc


def _consts():
    i = np.arange(128)
    same8 = (i[:, None] // 8) == (i[None, :] // 8)
    up = i[:, None] <= i[None, :]
    lowstrict = i[:, None] > i[None, :]
    Uf = up.astype(np.float32)
    Us = (up & same8).astype(np.float32)
    mats = [np.eye(128, dtype=np.float32), Uf, Us, -Uf, -Us, -Uf / 16.0, -Us / 16.0,
            np.where(up, 0.0, NEG), np.where(up & same8, 0.0, NEG),
            np.where(lowstrict, 0.0, NEG), np.where(lowstrict & same8, 0.0, NEG),
            ((i[:, None] // 64) == (i[None, :] // 64)).astype(np.float32),
            np.ones((128, 128), np.float32), np.full((128, 128), -0.5, np.float32)]
    bind = ((i[:, None] // 8) == np.arange(16)[None, :]).astype(np.float32)
    return np.ascontiguousarray(np.concatenate([m.astype(np.float32) for m in mats] + [bind], axis=1))


_NC_CACHE = {}


def kernel(x_prompt, x_sample, state_gla, state_gdn, state_gdn_conv, w_in, gla_w_gate_up,
           gla_b_gate, gla_norm_g, gdn_conv_w, gdn_a_log, gdn_dt_bias, gdn_norm_g, w_out,
           ln1_g, ln1_b, w_ffn_gate, w_ffn_up, w_ffn_down, ln2_g, ln2_b):
    f = lambda a: np.ascontiguousarray(np.asarray(a, dtype=np.float32))
    x_prompt, x_sample = f(x_prompt), f(x_sample)
    state_gla, state_gdn, state_gdn_conv = f(state_gla), f(state_gdn), f(state_gdn_conv)
    if "nc" not in _NC_CACHE:
        _NC_CACHE["nc"] = build_program()
    nc = _NC_CACHE["nc"]
    rep = lambda v: np.broadcast_to(f(v).reshape(1, -1), (128, f(v).size))
    prm = np.ascontiguousarray(np.concatenate([rep(gla_norm_g), rep(gdn_norm_g), rep(ln1_g), rep(ln1_b),
                                               rep(gdn_a_log), rep(gdn_dt_bias), rep(gla_b_gate)], axis=1))
    prm2 = np.ascontiguousarray(np.concatenate([rep(ln2_g), rep(ln2_b)], axis=1))
    cw = f(gdn_conv_w)[0]
    cwT = np.ascontiguousarray(cw.reshape(4, 12, 128).transpose(2, 1, 0).reshape(128, 48))
    gnc = np.ascontiguousarray(np.stack([f(gla_norm_g).reshape(128), f(gdn_norm_g).reshape(128)], axis=1))
    common = {"gnc": gnc, "w_in": f(w_in)[0], "w_out": f(w_out)[0], "w_g": f(w_ffn_gate)[0], "w_u": f(w_ffn_up)[0],
              "w_d": f(w_ffn_down)[0], "wgu": f(gla_w_gate_up)[0], "cwT": cwT, "consts": _consts(),
              "prm": prm, "prm2": prm2}
    in_maps = []
    for c in range(8):
        xs = np.concatenate([x_prompt[c].reshape(16, 128, D), x_sample[16 * c:16 * c + 16].reshape(1, 128, D)], axis=0)
        m = dict(common)
        m["xs"] = np.ascontiguousarray(xs)
        m["sgla"] = np.ascontiguousarray(state_gla[0, 16 * c:16 * c + 16])
        m["sgdn"] = np.ascontiguousarray(state_gdn[0, 16 * c:16 * c + 16])
        m["sconv"] = np.ascontiguousarray(state_gdn_conv[0, 16 * c:16 * c + 16].reshape(48, 1536))
        in_maps.append(m)
    res = run_bass_kernel_spmd(nc, in_maps, core_ids=list(range(8)))
    R = res.results
    _NC_CACHE["last"] = R
    yp = np.stack([R[c]["y"][0:16].reshape(2048, D) for c in range(8)])
    ys = np.concatenate([R[c]["y"][16].reshape(16, 8, D) for c in range(8)], axis=0)
    pgla = np.stack([R[c]["pgla"] for c in range(8)])[None]
    pgdn = np.stack([R[c]["pgdn"] for c in range(8)])[None]
    pconv = np.stack([R[c]["pconv"] for c in range(8)])[None]
    sgla = np.concatenate([R[c]["sgla_o"] for c in range(8)], axis=0)[None]
    sgdn = np.concatenate([R[c]["sgdn_o"] for c in range(8)], axis=0)[None]
    sconv = np.concatenate([R[c]["sconv_o"] for c in range(8)], axis=0)[None]
    return tuple(np.ascontiguousarray(a, dtype=np.float32) for a in (yp, ys, pgla, pgdn, pconv, sgla, sgdn, sconv))
```

```python
import contextlib
from math import prod
import numpy as np
import concourse.bass as bass
import concourse.mybir as mybir
from concourse.bass_utils import run_bass_kernel_spmd

F32 = mybir.dt.float32
BF16 = mybir.dt.bfloat16
F32R = mybir.dt.float32r


def R(ap):
    return ap.bitcast(F32R)
AF = mybir.ActivationFunctionType
ALU = mybir.AluOpType

N_DMA_SLOTS = 40
_ESZ = {str(F32): 4, str(BF16): 2}

D = 1024
NT = 17
IN_DIM = 3608
DFF = 2816
NFC = DFF // 128
OFF_GQ, OFF_GK, OFF_GV, OFF_GG, OFF_GA, OFF_DQKV, OFF_DG, OFF_DAB = 0, 256, 512, 1024, 1536, 1552, 3088, 3600
ALPHA = 2.0 ** 0.25
NEG = -30000.0
import os as _os0
DEBUG = {"rr0": [128, 1024], "sm": [128, 64], "gate": [128, 1024], "xt": [128, 1024], "x1": [128, 1024], "x2": [128, 1024], "r2": [128, 1024], "st2": [128, 16]} if _os0.environ.get("KDBG") else {}


_ALIAS = {}


def _box(ap):
    b = _box0(ap)
    al = _ALIAS.get(b[0])
    if al is not None:
        return (al[0], b[1], b[2], b[3] + al[1], b[4] + al[1])
    return b


def _box0(ap):
    name = ap.tensor.name
    pat = ap.ap
    off = int(ap.offset)
    es = _ESZ.get(str(ap.dtype), 4)
    sp = str(ap.space)
    if sp in ("SB", "PSUM"):
        pstep, pcnt = pat[0]
        if pstep > 0:
            p0 = off // pstep
            f0 = off - p0 * pstep
        else:
            p0, f0 = 0, off
        ext = 1
        for st, cn in pat[1:]:
            ext += abs(st) * (cn - 1)
        return (name, p0, p0 + pcnt, f0 * es, (f0 + ext) * es)
    ext = 1
    for st, cn in pat:
        ext += abs(st) * (cn - 1)
    return (name, 0, 1, off * es, (off + ext) * es)


def _overlap(a, b):
    return a[1] < b[2] and b[1] < a[2] and a[3] < b[4] and b[3] < a[4]


def _covers(a, b):
    return a[1] <= b[1] and a[2] >= b[2] and a[3] <= b[3] and a[4] >= b[4]


class Op:
    __slots__ = ("idx", "eng", "fn", "dma", "deps", "signal", "semval", "slot", "slot_prev")

    def __init__(self, idx, eng, fn, dma):
        self.idx, self.eng, self.fn, self.dma = idx, eng, fn, dma
        self.deps = set()
        self.signal = False
        self.semval = None
        self.slot = None
        self.slot_prev = 0


class Prog:
    ENGS = ("pe", "act", "dve", "pool", "sp")

    def __init__(self, nc, untracked=()):
        self.nc = nc
        self.ops = []
        self.wr = {}
        self.rd = {}
        self.untracked = set(untracked)

    def add(self, eng, fn, reads=(), writes=(), dma=False):
        op = Op(len(self.ops), eng, fn, dma)
        self.ops.append(op)
        for ap in reads:
            b = _box(ap)
            if b[0] in self.untracked:
                continue
            for (wb, wi) in self.wr.get(b[0], ()):
                if _overlap(wb, b):
                    op.deps.add(wi)
            self.rd.setdefault(b[0], []).append((b, op.idx))
        for ap in writes:
            b = _box(ap)
            if b[0] in self.untracked:
                continue
            wl = self.wr.get(b[0], [])
            for (wb, wi) in wl:
                if _overlap(wb, b):
                    w = self.ops[wi]
                    if w.dma or op.dma or w.eng != op.eng:
                        op.deps.add(wi)
            rl = self.rd.get(b[0], [])
            for (rb, ri) in rl:
                if ri != op.idx and _overlap(rb, b):
                    r = self.ops[ri]
                    if r.dma or op.dma or r.eng != op.eng:
                        op.deps.add(ri)
            self.wr[b[0]] = [(wb, wi) for (wb, wi) in wl if not _covers(b, wb)] + [(b, op.idx)]
            self.rd[b[0]] = [(rb, ri) for (rb, ri) in rl if (ri == op.idx) or not _covers(b, rb)]
        for ap in list(reads) + list(writes):
            if str(ap.space) == "PSUM":
                key = "LOCK_" + ap.tensor.name
                last = self.wr.get(key)
                if last is not None and last != op.idx and self.ops[last].eng != op.eng:
                    op.deps.add(last)
                self.wr[key] = op.idx
        return op

    def dma(self, q, out, in_, **kw):
        return self.add(q, lambda e: e.dma_start(out=out, in_=in_, **kw), [in_], [out], dma=True)

    def mm(self, out, lhsT, rhs, start=True, stop=True):
        return self.add("pe", lambda e: e.matmul(out, lhsT, rhs, start=start, stop=stop), [lhsT, rhs], [out])

    def tr(self, out, in_, ident):
        return self.add("pe", lambda e: e.transpose(out, in_, ident), [in_, ident], [out])

    def act(self, out, in_, func, bias=None, scale=None, accum_out=None):
        reads = [in_]
        kw = {}
        if bias is not None:
            kw["bias"] = bias
            if not isinstance(bias, (int, float)):
                reads.append(bias)
        if scale is not None:
            kw["scale"] = scale
            if not isinstance(scale, (int, float)):
                reads.append(scale)
        writes = [out]
        if accum_out is not None:
            kw["accum_out"] = accum_out
            writes.append(accum_out)
        return self.add("act", lambda e: e.activation(out, in_, func, **kw), reads, writes)

    def tt(self, eng, out, in0, in1, op):
        return self.add(eng, lambda e: e.tensor_tensor(out, in0, in1, op), [in0, in1], [out])

    def ts(self, eng, out, in0, s1, s2, op0, op1=None):
        reads = [in0]
        for s in (s1, s2):
            if s is not None and not isinstance(s, (int, float)):
                reads.append(s)
        if op1 is None:
            return self.add(eng, lambda e: e.tensor_scalar(out, in0, s1, None, op0), reads, [out])
        return self.add(eng, lambda e: e.tensor_scalar(out, in0, s1, s2, op0, op1), reads, [out])

    def stt(self, out, in0, scalar, in1, op0, op1):
        reads = [in0, in1]
        if not isinstance(scalar, (int, float)):
            reads.append(scalar)
        return self.add("dve", lambda e: e.scalar_tensor_tensor(out, in0, scalar, in1, op0, op1), reads, [out])

    def copy(self, eng, out, in_):
        if eng == "act":
            return self.add("act", lambda e: e.copy(out, in_), [in_], [out])
        return self.add(eng, lambda e: e.tensor_copy(out, in_), [in_], [out])

    def memset(self, eng, out, val):
        return self.add(eng, lambda e: e.memset(out, val), [], [out])

    def barrier(self, eng, reads):
        return self.add(eng, None, reads, [])

    def emit(self, es):
        nc = self.nc
        import os
        ops = self.ops[:int(os.environ.get("KSTOP", "100000000"))]
        for op in ops:
            for d in op.deps:
                ops[d].signal = True
        cnt = {e: 0 for e in self.ENGS}
        slot_uses = [0] * N_DMA_SLOTS
        nd = 0
        for op in ops:
            if op.dma:
                op.slot = nd % N_DMA_SLOTS
                nd += 1
                op.slot_prev = slot_uses[op.slot] * 16
                slot_uses[op.slot] += 1
                op.semval = slot_uses[op.slot] * 16
            elif op.signal:
                cnt[op.eng] += 1
                op.semval = cnt[op.eng]
        esem = {e: es.enter_context(nc.semaphore("s_" + e)) for e in self.ENGS}
        dsem = [es.enter_context(nc.semaphore("d%d" % i)) for i in range(N_DMA_SLOTS)]
        block = es.enter_context(nc.Block())
        stats = {}

        def gen(E):
            def body(e):
                waited = {}
                nw = ni = 0
                for op in ops:
                    if op.eng != E:
                        continue
                    need = {}
                    for d in op.deps:
                        dop = ops[d]
                        s = ("d", dop.slot) if dop.dma else ("e", dop.eng)
                        if need.get(s, 0) < dop.semval:
                            need[s] = dop.semval
                    if op.dma and op.slot_prev > 0:
                        s = ("d", op.slot)
                        if need.get(s, 0) < op.slot_prev:
                            need[s] = op.slot_prev
                    for s, v in need.items():
                        if waited.get(s, 0) >= v:
                            continue
                        waited[s] = v
                        e.wait_ge(dsem[s[1]] if s[0] == "d" else esem[s[1]], v)
                        nw += 1
                    if op.fn is None:
                        continue
                    ins = op.fn(e)
                    ni += 1
                    if op.dma:
                        ins.then_inc(dsem[op.slot], 16)
                    elif op.signal:
                        ins.then_inc(esem[E], 1)
                stats[E] = (ni, nw)
            return body

        block.tensor(gen("pe"))
        block.scalar(gen("act"))
        block.vector(gen("dve"))
        block.gpsimd(gen("pool"))
        block.sync(gen("sp"))
        return stats


class Arena:
    def __init__(self, nc, es, nbytes):
        self.nbytes = nbytes
        self.h = {BF16: es.enter_context(nc.sbuf_tensor("A", [128, nbytes // 2], BF16))}
        self.h[F32] = self.h[BF16].bitcast(F32)
        self.top = nbytes
        self.cur = {}

    def view(self, off, shape, dt, parts=128):
        es = 4 if dt == F32 else 2
        assert off % 4 == 0
        n = prod(shape)
        ap = self.h[dt][0:parts, off // es: off // es + n]
        if len(shape) > 1:
            names = ["a%d" % i for i in range(len(shape))]
            kw = {nm: s for nm, s in zip(names[1:], shape[1:])}
            ap = ap.rearrange("p (%s) -> p %s" % (" ".join(names), " ".join(names)), **kw)
        return ap

    def persist(self, shape, dt, parts=128):
        nb = (prod(shape) * (4 if dt == F32 else 2) + 31) // 32 * 32
        self.top -= nb
        return self.view(self.top, shape, dt, parts)

    def alloc(self, region, shape, dt, parts=128):
        nb = (prod(shape) * (4 if dt == F32 else 2) + 31) // 32 * 32
        off = self.cur[region]
        self.cur[region] = off + nb
        assert self.cur[region] <= self.top, (region, self.cur[region], self.top)
        return self.view(off, shape, dt, parts)


class PsumPool:
    def __init__(self, nc, es):
        self.banks = [es.enter_context(nc.psum_tensor("ps%d" % i, [128, 512], F32)) for i in range(8)]
        self.free = list(range(8))

    def alloc(self):
        assert self.free, "out of PSUM banks"
        return self.free.pop(0)

    def release(self, b):
        self.free.append(b)


def build_program():
    nc = bass.Bass("TRN2", target_bir_lowering=False)

    def din(name, shape):
        return nc.dram_tensor(name, list(shape), F32, kind="ExternalInput").ap()

    def dout(name, shape):
        return nc.dram_tensor(name, list(shape), F32, kind="ExternalOutput").ap()

    xs = din("xs", [NT, 128, D])
    sgla_i = din("sgla", [16, 4, 64, 128])
    sgdn_i = din("sgdn", [16, 4, 128, 128])
    sconv_i = din("sconv", [48, 1536])
    w_in = din("w_in", [D, IN_DIM])
    w_out = din("w_out", [D, D])
    w_g = din("w_g", [D, DFF])
    w_u = din("w_u", [D, DFF])
    w_d = din("w_d", [DFF, D])
    wgu_i = din("wgu", [16, 256])
    cwT_i = din("cwT", [128, 48])
    NCONST = 128 * 14 + 16
    consts_i = din("consts", [128, NCONST])
    prm_i = din("prm", [128, 2568])
    prm2_i = din("prm2", [128, 2048])
    gnc_i = din("gnc", [128, 2])

    y_o = dout("y", [NT, 128, D])
    pgla_o = dout("pgla", [4, 64, 128])
    pgdn_o = dout("pgdn", [4, 128, 128])
    pconv_o = dout("pconv", [3, 1536])
    sgla_o = dout("sgla_o", [16, 4, 64, 128])
    sgdn_o = dout("sgdn_o", [16, 4, 128, 128])
    sconv_o = dout("sconv_o", [16, 3, 1536])
    x1s = nc.dram_tensor("x1s", [NT, 128, D], F32, kind="Internal").ap()
    dbg = {}
    for k, shp in DEBUG.items():
        dbg[k] = dout("dbg_" + k, shp)

    es = contextlib.ExitStack()
    with es:
        AR_SZ = 33856 + 10240
        A = Arena(nc, es, 212800 - AR_SZ)
        arh = es.enter_context(nc.sbuf_tensor("AR", [128, AR_SZ // 4], F32))
        ar_cur = [0]

        def ar1(shape):
            n = prod(shape)
            ap = arh[:, ar_cur[0]:ar_cur[0] + n]
            ar_cur[0] += (n + 7) // 8 * 8
            assert ar_cur[0] * 4 <= AR_SZ, ar_cur[0] * 4
            if len(shape) > 1:
                names = ["a%d" % i for i in range(len(shape))]
                kw = {nm: sz for nm, sz in zip(names[1:], shape[1:])}
                ap = ap.rearrange("p (%s) -> p %s" % (" ".join(names), " ".join(names)), **kw)
            return ap

        PS = PsumPool(nc, es)
        ps = PS.banks
        P = Prog(nc, untracked=["xs", "sgla", "sgdn", "sconv", "w_in", "w_out", "w_g", "w_u", "w_d",
                                "wgu", "cwT", "consts", "prm", "prm2", "gnc"])

        A.cur["p1"] = 74112
        A.cur["p2"] = 135168
        CT = A.alloc("p1", [NCONST], F32)
        identP = A.persist([128], F32)
        nhalfP = A.persist([8], F32)
        (c_ident, c_Uf, c_Us, c_nUf, c_nUs, c_u16f, c_u16s,
         c_mbuf, c_mbus, c_mblf, c_mbls, c_blk64, c_ones, c_nhalf) = [CT[:, i * 128:(i + 1) * 128] for i in range(14)]
        c_bind = CT[:, 1792:1808]
        CB = A.persist([6 * 128], BF16)
        b_ident, b_mbuf, b_mbus, b_mblf, b_mbls, b_ones = [CB[:, i * 128:(i + 1) * 128] for i in range(6)]
        PRM = A.persist([2568], F32)
        p_gng, p_gnd = PRM[:, 0:128], PRM[:, 128:256]
        p_lng, p_lnb = PRM[:, 256:1280], PRM[:, 1280:2304]
        p_alog, p_dtb, p_bgate = PRM[:, 2304:2308], PRM[:, 2308:2312], PRM[0:1, 2312:2568]
        WGU = A.persist([256], F32, parts=16)
        CWT = A.persist([12, 4], F32)
        NEGA = A.persist([4], F32)
        GNC = A.persist([2], F32)
        SGp = ar1([2, 128])
        SDp = ar1([4, 128])
        rU = {}
        for nm_ in ("Uf", "Us", "nUf", "nUs", "u16f", "u16s"):
            rU[nm_] = ar1([128])
        HALO = A.persist([12, 3], F32)
        stats = A.persist([16], F32)

        win = A.view(0, [8, IN_DIM], BF16)
        wout = A.view(57728, [8, D], BF16)
        wg_sb = A.view(0, [8, DFF], BF16)
        wu_sb = A.view(45056, [8, DFF], BF16)
        wd_sb = A.view(90112, [NFC, D], BF16)

        def a1(shape, dt=F32, parts=128):
            return A.alloc("p1", shape, dt, parts)

        xtb = a1([2, D])
        xT = a1([8, 128], BF16)
        gaT = a1([128], F32, parts=16)
        lz = ar1([256])
        eb = a1([256])
        enb = a1([256])
        qtT = a1([256])
        ktT = ar1([256])
        qtTm = ar1([4, 128])
        ktok = ar1([256])
        vg = ar1([512])
        gate = a1([1024])
        atm = ar1([512])
        og = a1([1024], BF16)
        oT = a1([8, 128], BF16)
        sm = a1([64])
        tokd = A.view(A.cur["p1"], [1536], F32)
        cbuf = a1([12, 176])
        SCV = A.view(A.cur["p1"], [1536], F32, parts=48)
        cv_off = A.cur["p1"]
        cv = a1([12, 128])
        sqb = a1([8, 128], BF16)
        rinv = a1([8, 128])
        junk = rinv[:, 7, :]
        ST2 = A.view(cv_off + 2048, [16, 128], F32)
        qkn = ar1([4, 2, 128])
        gB = ar1([4, 128])
        GT = a1([2, 128])
        G = a1([2, 128])
        Nn = ar1([2, 128])
        Cm = ar1([2, 128])
        Mb = ar1([2, 2, 128])
        MTb = ar1([2, 2, 128])
        Pb = ar1([2, 2, 128])
        XdT = ar1([2, 128])
        Yb = ar1([2, 128])
        XT = ar1([4, 128])
        QKm = ar1([4, 128])
        vb = ar1([512])
        kb = ar1([512])
        kd = ar1([512])
        nwT = ar1([2, 128])
        delta = ar1([2, 128])
        osb = a1([2, 128])
        ST = a1([16, 128])
        EXP = a1([2176])
        VEX = a1([16, 128])
        GT1 = a1([2, 128])
        G1 = a1([2, 128])
        bufsets = [
            (Nn, Cm, Mb, MTb, Pb, XdT, Yb, GT, G),
            (lz.rearrange("p (a b) -> p a b", a=2), ktT.rearrange("p (a b) -> p a b", a=2),
             atm.rearrange("p (a b c) -> p a b c", a=2, b=2), vg.rearrange("p (a b c) -> p a b c", a=2, b=2),
             qtTm.rearrange("p (a b) c -> p a b c", a=2), ktok.rearrange("p (a b) -> p a b", a=2),
             nwT, GT1, G1),
        ]
        nwT4 = Mb.rearrange("p a b c -> p (a b) c")
        delta4 = MTb.rearrange("p a b c -> p (a b) c")
        osb4_p = rinv[:, 0:4, :]
        osb4_s = cv[:, 0:4, :]
        p1_end = A.cur["p1"]

        ssq = sm[:, 0:8]
        rstd = sm[:, 8:16]
        dab = sm[:, 16:24]
        gg = ar1([4])
        beta = sm[:, 28:32]
        gcs = sm[:, 32:36]
        ngc = sm[:, 36:40]
        egc = sm[:, 40:44]
        bgs = sm[:, 44:48]
        egl = sm[:, 48:52]
        tmp4 = sm[:, 52:56]
        gcl = sm[:, 56:60]
        decs = a1([64])

        P.dma("sp", CT, consts_i)
        P.copy("dve", identP, c_ident)
        P.copy("dve", nhalfP, c_nhalf[:, 0:8])
        P.dma("sp", PRM, prm_i)
        P.dma("sp", WGU, wgu_i)
        P.dma("sp", GNC, gnc_i)
        P.dma("sp", CWT.rearrange("p a b -> p (a b)"), cwT_i)
        for (c0, c1) in ((1024, 1552), (3088, 3608), (1552, 2064), (2064, 2576), (2576, 3088), (0, 1024)):
            for k in range(8):
                P.dma("pool", win[:, k, c0:c1], w_in[k * 128:(k + 1) * 128, c0:c1])
        for k in range(8):
            P.dma("pool", wout[:, k, :], w_out[k * 128:(k + 1) * 128, :])
        for i, src in enumerate([c_ident, c_mbuf, c_mbus, c_mblf, c_mbls, c_ones]):
            P.copy("dve", CB[:, i * 128:(i + 1) * 128], src)
        P.act(NEGA, p_alog, AF.Exp)
        P.ts("dve", NEGA, NEGA, -1.0, None, ALU.mult)
        P.ts("pool", R(SGp.rearrange("p a b -> p (a b)")), CT[:, 0:256], 0.0, None, ALU.mult)
        P.ts("pool", R(SDp.rearrange("p a b -> p (a b)")), CT[:, 0:512], 0.0, None, ALU.mult)
        P.memset("pool", HALO, 0.0)
        P.memset("pool", EXP, 0.0)
        for nm_, src_ in (("Uf", c_Uf), ("Us", c_Us), ("nUf", c_nUf), ("nUs", c_nUs), ("u16f", c_u16f), ("u16s", c_u16s)):
            P.copy("dve", R(rU[nm_]), src_)

        conv_hoisted = {}

        def conv_gen():
            P.copy("dve", cbuf[:, :, 0:3], HALO)
            for g3 in range(3):
                bd = PS.alloc()
                for j in range(4):
                    c0 = OFF_DQKV + (g3 * 4 + j) * 128
                    for k in range(8):
                        P.mm(ps[bd][:, j * 128:(j + 1) * 128], win[:, k, c0:c0 + 128], xT[:, k, :], start=(k == 0), stop=(k == 7))
                for j in range(4):
                    P.copy("act", cbuf[:, g3 * 4 + j, 3:131], ps[bd][:, j * 128:(j + 1) * 128])
                PS.release(bd)
                yield
            P.copy("dve", HALO, cbuf[:, :, 128:131])
            for cc in range(12):
                P.act(cv[:, cc, :], cbuf[:, cc, 0:128], AF.Copy, scale=CWT[:, cc, 0:1])
                for i in range(1, 4):
                    P.stt(cv[:, cc, :], cbuf[:, cc, i:i + 128], CWT[:, cc, i:i + 1], cv[:, cc, :], ALU.mult, ALU.add)
                yield
            P.act(cv.rearrange("p a b -> p (a b)"), cv.rearrange("p a b -> p (a b)"), AF.Silu)

        def front(t):
            xt = xtb[:, tile_pos[t] % 2, :]
            P.dma("sp", xt, xs[t])
            b0, b1 = PS.alloc(), PS.alloc()
            for k in range(8):
                bk = ps[b0] if k < 4 else ps[b1]
                P.tr(bk[:, (k % 4) * 128:(k % 4 + 1) * 128], xt[:, k * 128:(k + 1) * 128], c_ident)
            P.act(xT[:, 0:4, :].rearrange("p a b -> p (a b)"), ps[b0][:, :], AF.Copy)
            P.copy("dve", xT[:, 4:8, :].rearrange("p a b -> p (a b)"), ps[b1][:, :])
            PS.release(b0)
            PS.release(b1)

        def phase1_tile(t):
            sample = (t == NT - 1)
            osb4 = osb4_s if sample else osb4_p
            nseq = 16 if sample else 1
            U = rU["Us"] if sample else rU["Uf"]
            nU = rU["nUs"] if sample else rU["nUf"]
            u16 = rU["u16s"] if sample else rU["u16f"]
            m01 = c_Us if sample else c_Uf
            mbu = b_mbus if sample else b_mbuf
            mbl = b_mbls if sample else b_mblf
            Ulast = rU["Us"][:, 7:128:8] if sample else rU["Uf"][:, 126:128]
            nlast = 16 if sample else 2
            nlev = 3 if sample else 6

            xt = xtb[:, tile_pos[t] % 2, :]
            rr = xt

            def mm_tok(out_ap, c0, n):
                for k in range(8):
                    P.mm(out_ap, xT[:, k, :], win[:, k, c0:c0 + n], start=(k == 0), stop=(k == 7))

            def mm_feat(out_ap, c0, m):
                for k in range(8):
                    P.mm(out_ap, win[:, k, c0:c0 + m], xT[:, k, :], start=(k == 0), stop=(k == 7))

            bg_ = PS.alloc()
            mm_tok(ps[bg_][:, :], OFF_GG, 512)
            P.act(gate[:, 0:512], ps[bg_][:, :], AF.Silu)
            PS.release(bg_)
            bg_ = PS.alloc()
            mm_tok(ps[bg_][:, :], OFF_DG, 512)
            P.act(gate[:, 512:1024], ps[bg_][:, :], AF.Silu)
            PS.release(bg_)

            if sample:
                P.dma("sp", SCV, sconv_i)
                bh0, bh1 = PS.alloc(), PS.alloc()
                for cc in range(12):
                    bk = ps[bh0] if cc < 8 else ps[bh1]
                    P.tr(bk[:, (cc % 8) * 48:(cc % 8 + 1) * 48], SCV[:, cc * 128:(cc + 1) * 128], c_ident[0:48, 0:48])
                for cc in range(12):
                    bk = ps[bh0] if cc < 8 else ps[bh1]
                    P.copy("dve" if cc % 2 else "act",
                           cbuf[:, cc, :].rearrange("p (s w) -> p s w", w=11)[:, :, 0:3],
                           bk[:, (cc % 8) * 48:(cc % 8 + 1) * 48].rearrange("p (s r) -> p s r", r=3))
                PS.release(bh0)
                PS.release(bh1)
                for g3 in range(3):
                    bd = PS.alloc()
                    for j in range(4):
                        mm_feat(ps[bd][:, j * 128:(j + 1) * 128], OFF_DQKV + (g3 * 4 + j) * 128, 128)
                    for j in range(4):
                        cc = g3 * 4 + j
                        P.copy("act", cbuf[:, cc, :].rearrange("p (s w) -> p s w", w=11)[:, :, 3:11],
                               ps[bd][:, j * 128:(j + 1) * 128].rearrange("p (s t) -> p s t", t=8))
                    PS.release(bd)
                for cc in range(12):
                    def tap(i):
                        return cbuf[:, cc, :].rearrange("p (s w) -> p s w", w=11)[:, :, i:i + 8]
                    dst = cv[:, cc, :].rearrange("p (s t) -> p s t", t=8)
                    P.act(dst, tap(0), AF.Copy, scale=CWT[:, cc, 0:1])
                    for i in range(1, 4):
                        P.stt(dst, tap(i), CWT[:, cc, i:i + 1], dst, ALU.mult, ALU.add)
            elif not conv_hoisted.get(t):
                for _ in conv_gen():
                    pass
            if sample:
                P.act(cv.rearrange("p a b -> p (a b)"), cv.rearrange("p a b -> p (a b)"), AF.Silu)

            def run_il(gens):
                gens = list(gens)
                while gens:
                    for g in list(gens):
                        try:
                            next(g)
                        except StopIteration:
                            gens.remove(g)

            def gdn_gen():
                P.act(sqb.rearrange("p a b -> p (a b)"), cv[:, 0:8, :].rearrange("p a b -> p (a b)"), AF.Square)
                bn0, bn1 = PS.alloc(), PS.alloc()
                for cc in range(8):
                    bk = ps[bn0] if cc < 4 else ps[bn1]
                    P.mm(bk[:, (cc % 4) * 128:(cc % 4 + 1) * 128], b_ones, sqb[:, cc, :])
                P.act(rinv[:, 0:4, :].rearrange("p a b -> p (a b)"), ps[bn0][:, :], AF.Ln, bias=1e-6)
                P.act(rinv[:, 4:8, :].rearrange("p a b -> p (a b)"), ps[bn1][:, :], AF.Ln, bias=1e-6)
                PS.release(bn0)
                yield
                PS.release(bn1)
                yield
                P.act(rinv.rearrange("p a b -> p (a b)"), rinv.rearrange("p a b -> p (a b)"), AF.Exp, scale=-0.5)
                for h in range(4):
                    P.stt(R(qkn[:, h, 1, :]), cv[:, h, :], 128.0 ** -0.5, rinv[:, h, :], ALU.mult, ALU.mult)
                    P.tt("dve", R(qkn[:, h, 0, :]), cv[:, 4 + h, :], rinv[:, 4 + h, :], ALU.mult)
                btk, btv = PS.alloc(), PS.alloc()
                for h in range(4):
                    P.tr(ps[btk][:, h * 128:(h + 1) * 128], qkn[:, h, 0, :], c_ident)
                    P.tr(ps[btv][:, h * 128:(h + 1) * 128], cv[:, 8 + h, :], c_ident)
                yield "L2DONE"
                bgc = PS.alloc()
                P.mm(ps[bgc][:, 0:4], R(U), R(gg))
                P.copy("dve", gcs, ps[bgc][:, 0:4])
                P.ts("dve", ngc, ps[bgc][:, 0:4], -1.0, None, ALU.mult)
                P.act(egc, ps[bgc][:, 0:4], AF.Exp)
                for h in range(4):
                    P.mm(ps[bgc][:, 64 + h * 16:64 + h * 16 + nlast], R(gB[:, h, :]), R(Ulast))
                for h in range(4):
                    P.act(decs[:, h * 16:h * 16 + nseq], ps[bgc][:, 64 + h * 16 + nlast - nseq:64 + h * 16 + nlast], AF.Exp)
                if not sample:
                    P.copy("dve", gcl, ps[bgc][:, 65:129:16])
                PS.release(bgc)
                yield
                P.tt("dve", bgs, beta, egc, ALU.mult)
                if not sample:
                    for h in range(4):
                        P.act(egl[:, h:h + 1], gcs[:, h:h + 1], AF.Exp, scale=-1.0, bias=gcl[:, h:h + 1])
                    for h in range(4):
                        hs = slice(h * 128, (h + 1) * 128)
                        P.ts("dve", R(vb[:, hs]), ps[btv][:, hs], beta[:, h:h + 1], None, ALU.mult)
                        P.ts("dve", R(kb[:, hs]), ps[btk][:, hs], bgs[:, h:h + 1], None, ALU.mult)
                        P.act(R(kd[:, hs]), ps[btk][:, hs], AF.Copy, scale=egl[:, h:h + 1])
                    PS.release(btk)
                    PS.release(btv)
                    yield
                bkq0, bkq1 = PS.alloc(), PS.alloc()
                for h in range(4):
                    bk = ps[bkq0] if h < 2 else ps[bkq1]
                    P.mm(bk[:, (h % 2) * 256:(h % 2 + 1) * 256], R(qkn[:, h, 0, :]), R(qkn[:, h, :, :].rearrange("p a b -> p (a b)")))

                while not gla_done[0]:
                    yield

                def pair_gen(hp):
                    Nn, Cm, Mb, MTb, Pb, XdT, Yb, GT, G = bufsets[hp]
                    bpa, bpb = PS.alloc(), PS.alloc()
                    for i in range(2):
                        h = hp * 2 + i
                        pa = ps[bpa][:, i * 128:(i + 1) * 128]
                        pb = ps[bpb][:, i * 128:(i + 1) * 128]
                        P.mm(pa, R(gB[:, h, :]), R(U), start=True, stop=False)
                        P.mm(pa, b_ident, mbu, start=False, stop=True)
                        P.mm(pb, R(gB[:, h, :]), R(nU), start=True, stop=False)
                        P.mm(pb, b_ident, mbl, start=False, stop=True)
                    for i in range(2):
                        h = hp * 2 + i
                        P.act(GT[:, i, :], ps[bpa][:, i * 128:(i + 1) * 128], AF.Exp, bias=ngc[:, h:h + 1])
                        P.act(G[:, i, :], ps[bpb][:, i * 128:(i + 1) * 128], AF.Exp, bias=gcs[:, h:h + 1])
                    PS.release(bpa)
                    yield
                    PS.release(bpb)
                    yield
                    for i in range(2):
                        h = hp * 2 + i
                        kq = (ps[bkq0] if h < 2 else ps[bkq1])[:, (h % 2) * 256:(h % 2 + 1) * 256]
                        P.stt(R(Nn[:, i, :]), kq[:, 0:128], beta[:, h:h + 1], G[:, i, :], ALU.mult, ALU.mult)
                        P.tt("dve", R(QKm[:, h, :]), kq[:, 128:256], GT[:, i, :], ALU.mult)
                        if sample:
                            P.add("dve", lambda e, i=i, h=h: e.reduce_sum(egl[:, h:h + 1], GT[:, i, 7:128:8], mybir.AxisListType.X),
                                  [GT[:, i, 7:128:8]], [egl[:, h:h + 1]])
                        if not sample:
                            P.tt("dve", R(Cm[:, i, :]), Nn[:, i, :], c_blk64, ALU.mult)
                            P.tt("dve", R(Nn[:, i, :]), Nn[:, i, :], Cm[:, i, :], ALU.subtract)
                            nd = Cm[:, i, :]
                        else:
                            nd = Nn[:, i, :]
                        P.act(R(Mb[:, 0, i, :]), nd, AF.Copy, scale=-1.0)
                        P.stt(R(Pb[:, 0, i, :]), nd, -1.0, c_ident, ALU.mult, ALU.add)
                    bx = PS.alloc()
                    for i in range(2):
                        nd = Nn[:, i, :] if sample else Cm[:, i, :]
                        P.tr(ps[bx][:, i * 128:(i + 1) * 128], nd, c_ident)
                    P.act(R(MTb[:, 0, 0, :]), ps[bx][:, 0:128], AF.Copy, scale=-1.0)
                    P.ts("dve", R(MTb[:, 0, 1, :]), ps[bx][:, 128:256], -1.0, None, ALU.mult)
                    PS.release(bx)
                    yield
                    for k in range(1, nlev):
                        a, b = (k - 1) % 2, k % 2
                        bm = PS.alloc()
                        for i in range(2):
                            P.mm(ps[bm][:, i * 128:(i + 1) * 128], R(Mb[:, a, i, :]), R(MTb[:, a, i, :]))
                            if k < nlev - 1:
                                P.mm(ps[bm][:, 256 + i * 128:256 + (i + 1) * 128], R(MTb[:, a, i, :]), R(Mb[:, a, i, :]))
                        P.copy("act", R(MTb[:, b, :, :].rearrange("p a b -> p (a b)")), ps[bm][:, 0:256])
                        if k < nlev - 1:
                            P.copy("dve", R(Mb[:, b, :, :].rearrange("p a b -> p (a b)")), ps[bm][:, 256:512])
                        PS.release(bm)
                        yield
                        bp = PS.alloc()
                        for i in range(2):
                            P.mm(ps[bp][:, i * 128:(i + 1) * 128], R(MTb[:, b, i, :]), R(Pb[:, a, i, :]))
                        P.tt("dve", R(Pb[:, b, :, :].rearrange("p a b -> p (a b)")), ps[bp][:, 0:256],
                             Pb[:, a, :, :].rearrange("p a b -> p (a b)"), ALU.add)
                        PS.release(bp)
                        yield
                    fin = (nlev - 1) % 2
                    bx = PS.alloc()
                    for i in range(2):
                        P.tr(ps[bx][:, i * 128:(i + 1) * 128], Pb[:, fin, i, :], c_ident)
                    for i in range(2):
                        h = hp * 2 + i
                        dst = R(XT[:, h, :]) if sample else R(XdT[:, i, :])
                        P.copy("act", dst, ps[bx][:, i * 128:(i + 1) * 128])
                    PS.release(bx)
                    yield
                    if not sample:
                        by = PS.alloc()
                        for i in range(2):
                            P.mm(ps[by][:, i * 128:(i + 1) * 128], R(Nn[:, i, :]), R(XdT[:, i, :]))
                        for i in range(2):
                            P.copy("act", R(Yb[:, i, :]), ps[by][:, i * 128:(i + 1) * 128])
                        for i in range(2):
                            P.mm(ps[by][:, 256 + i * 128:256 + (i + 1) * 128], R(Pb[:, fin, i, :]), R(Yb[:, i, :]))
                        for i in range(2):
                            h = hp * 2 + i
                            P.tt("dve", R(XT[:, h, :]), XdT[:, i, :], ps[by][:, 256 + i * 128:256 + (i + 1) * 128], ALU.subtract)
                        PS.release(by)
                        yield

                gens_ = [pair_gen(0), pair_gen(1)]
                if (not sample) and tile_pos[t] + 1 < len(p1_order):
                    conv_hoisted[p1_order[tile_pos[t] + 1]] = True
                    gens_.append(conv_gen())
                run_il(gens_)
                PS.release(bkq0)
                yield
                PS.release(bkq1)
                yield
                if sample:
                    for h in range(4):
                        hs = slice(h * 128, (h + 1) * 128)
                        P.ts("dve", R(vb[:, hs]), ps[btv][:, hs], beta[:, h:h + 1], None, ALU.mult)
                        P.ts("dve", R(kb[:, hs]), ps[btk][:, hs], bgs[:, h:h + 1], None, ALU.mult)
                        P.act(R(kd[:, hs]), ps[btk][:, hs], AF.Copy, scale=egl[:, h:h + 1])
                    PS.release(btk)
                    yield
                    PS.release(btv)
                yield


            if sample or t == NT - 2:
                for g3 in range(3):
                    bd = PS.alloc()
                    mm_tok(ps[bd][:, :], OFF_DQKV + g3 * 512, 512)
                    P.copy("dve", tokd[:, g3 * 512:(g3 + 1) * 512], ps[bd][:, :])
                    PS.release(bd)
                if sample:
                    for s in range(16):
                        P.dma("sp", sconv_o[s], tokd[s * 8 + 5:s * 8 + 8, :])
                else:
                    P.dma("sp", pconv_o, tokd[125:128, :])

            bv = PS.alloc()
            mm_tok(ps[bv][:, :], OFF_GV, 512)
            P.copy("dve", R(vg), ps[bv][:, :])
            PS.release(bv)
            bqk = PS.alloc()
            for j in range(4):
                mm_feat(ps[bqk][:, j * 128:(j + 1) * 128], OFF_GQ + j * 128, 128)
            bga = PS.alloc()
            mm_feat(ps[bga][0:16, 0:128], OFF_GA, 16)
            P.copy("dve", gaT, ps[bga][0:16, 0:128])
            mm_tok(ps[bga][:, 128:136], OFF_DAB, 8)
            P.copy("dve", dab, ps[bga][:, 128:136])
            P.mm(ps[bga][:, 256:512], gaT, WGU, start=True, stop=False)
            P.mm(ps[bga][:, 256:512], c_ones[0:1, :], p_bgate, start=False, stop=True)
            if tile_pos[t] + 1 < len(p1_order):
                pass
            elif KP2 > 0:
                for k in range(8):
                    P.dma("pool", wg_sb[:, k, :], w_g[k * 128:(k + 1) * 128, :])
                for k in range(2):
                    P.dma("pool", wu_sb[:, k, :], w_u[k * 128:(k + 1) * 128, :])
                wg_loaded.append(1)

            gdn = gdn_gen()
            for _m in gdn:
                if _m == "L2DONE":
                    break

            if tile_pos[t] + 1 < len(p1_order):
                front(p1_order[tile_pos[t] + 1])

            P.act(R(lz), ps[bga][:, 256:512], AF.Exp, scale=-1.0)
            PS.release(bga)
            P.act(R(lz), lz, AF.Ln, bias=1.0)
            bb = PS.alloc()
            for c in range(2):
                P.mm(ps[bb][:, c * 128:(c + 1) * 128], R(lz[:, c * 128:(c + 1) * 128]), R(u16))
            P.act(eb, ps[bb][:, 0:256], AF.Exp)
            P.act(enb, ps[bb][:, 0:256], AF.Exp, scale=-1.0)
            PS.release(bb)
            P.stt(qtT, ps[bqk][:, 0:256], 0.125, eb, ALU.mult, ALU.mult)
            P.tt("dve", R(ktT), ps[bqk][:, 256:512], enb, ALU.mult)
            PS.release(bqk)
            for h in range(4):
                P.act(R(qtTm[:, h, :]), qtT[:, (h // 2) * 128:(h // 2 + 1) * 128], AF.Copy,
                      scale=c_blk64[:, (h % 2) * 64:(h % 2) * 64 + 1])
            P.tt("dve", tmp4, dab[:, 0:4], p_dtb, ALU.add)
            P.act(tmp4, tmp4, AF.Exp)
            P.act(tmp4, tmp4, AF.Ln, bias=1.0)
            P.tt("dve", R(gg), tmp4, NEGA, ALU.mult)
            P.act(beta, dab[:, 4:8], AF.Exp, scale=-1.0)
            P.ts("dve", beta, beta, 1.0, None, ALU.add)
            P.add("dve", lambda e: e.reciprocal(beta, beta), [beta], [beta])
            for h in range(4):
                P.act(R(gB[:, h, :]), c_ones, AF.Copy, scale=gg[:, h:h + 1])


            def gla_gen():
                bt = PS.alloc()
                for c in range(2):
                    P.tr(ps[bt][:, c * 128:(c + 1) * 128], ktT[:, c * 128:(c + 1) * 128], c_ident)
                P.copy("act", R(ktok), ps[bt][:, 0:256])
                PS.release(bt)
                yield
                bat = PS.alloc()
                for h in range(4):
                    c, r = h // 2, h % 2
                    rs = slice(r * 64, (r + 1) * 64)
                    P.mm(ps[bat][:, h * 128:(h + 1) * 128], R(ktT[:, c * 128:(c + 1) * 128]), R(qtTm[:, h, :]))
                for h in range(4):
                    P.tt("dve", R(atm[:, h * 128:(h + 1) * 128]), ps[bat][:, h * 128:(h + 1) * 128], m01, ALU.mult)
                PS.release(bat)
                yield
                bo = PS.alloc()
                for c in range(2):
                    if sample:
                        P.dma("sp", ST, sgla_i[:, 2 * c:2 * c + 2].rearrange("s r d v -> (r d) s v"))
                    for r in range(2):
                        h = 2 * c + r
                        rs = slice(r * 64, (r + 1) * 64)
                        oh = ps[bo][:, h * 128:(h + 1) * 128]
                        P.mm(oh, R(atm[:, h * 128:(h + 1) * 128]), R(vg[:, h * 128:(h + 1) * 128]), start=True, stop=False)
                        if not sample:
                            P.mm(oh, R(qtTm[:, h, :]), R(SGp[:, c, :]), start=False, stop=True)
                        else:
                            P.copy("dve", EXP.rearrange("p (s w) -> p s w", w=136)[:, :, 0:8],
                                   qtTm[:, h, :].rearrange("p (s t) -> p s t", t=8))
                            for s in range(16):
                                P.mm(oh, EXP[:, s * 128:(s + 1) * 128], ST[:, s, :], start=False, stop=(s == 15))
                    for r in range(2):
                        h = 2 * c + r
                        rs = slice(r * 64, (r + 1) * 64)
                        if not sample:
                            bs = PS.alloc()
                            P.mm(ps[bs][:, 0:128], R(ktok[:, c * 128:(c + 1) * 128]), R(vg[:, h * 128:(h + 1) * 128]))
                            ebl = eb[rs, c * 128 + 127:c * 128 + 128]
                            P.stt(R(SGp[rs, c, :]), ps[bs][rs, 0:128], 1.0, SGp[rs, c, :], ALU.mult, ALU.add)
                            P.ts("dve", R(SGp[rs, c, :]), SGp[rs, c, :], ebl, None, ALU.mult)
                            PS.release(bs)
                            yield
                        else:
                            for s in range(16):
                                if s % 2:
                                    P.ts("dve", VEX[:, s, :], vg[:, h * 128:(h + 1) * 128], c_bind[:, s:s + 1], None, ALU.mult)
                                else:
                                    P.act(VEX[:, s, :], vg[:, h * 128:(h + 1) * 128], AF.Copy, scale=c_bind[:, s:s + 1])
                            for sg in range(4):
                                bs = PS.alloc()
                                P.mm(ps[bs][:, :], ktok[:, c * 128:(c + 1) * 128],
                                     VEX[:, sg * 4:(sg + 1) * 4, :].rearrange("p a b -> p (a b)"))
                                for s4 in range(4):
                                    s = sg * 4 + s4
                                    ebl = eb[rs, c * 128 + 8 * s + 7:c * 128 + 8 * s + 8]
                                    P.stt(ST[rs, s, :], ps[bs][rs, s4 * 128:(s4 + 1) * 128], 1.0, ST[rs, s, :], ALU.mult, ALU.add)
                                    P.ts("dve", ST[rs, s, :], ST[rs, s, :], ebl, None, ALU.mult)
                                PS.release(bs)
                                yield
                    if sample:
                        P.dma("sp", sgla_o[:, 2 * c:2 * c + 2].rearrange("s r d v -> (r d) s v"), ST)
                for h in range(4):
                    P.act(junk, ps[bo][:, h * 128:(h + 1) * 128], AF.Square, accum_out=ssq[:, h:h + 1])
                P.ts("dve", rstd[:, 0:4], ssq[:, 0:4], 1.0 / 128.0, 1e-5, ALU.mult, ALU.add)
                P.tt("pool", rstd[:, 0:4], rstd[:, 0:4], c_nhalf[:, 0:4], ALU.pow)
                for h in range(4):
                    P.stt(og[:, h * 128:(h + 1) * 128], ps[bo][:, h * 128:(h + 1) * 128], rstd[:, h:h + 1],
                          gate[:, h * 128:(h + 1) * 128], ALU.mult, ALU.mult)
                PS.release(bo)
                yield


            gla_done = [False]

            def gla_wrapped():
                yield from gla_gen()
                gla_done[0] = True

            run_il([gla_wrapped(), gdn])

            def head_gen(h):
                i = h
                hs = slice(h * 128, (h + 1) * 128)
                STh = (ST if h % 2 == 0 else ST2) if sample else None
                if sample:
                    P.dma("sp", STh, sgdn_i[:, h].rearrange("s d v -> d s v"))

                def S_of(s):
                    return STh[:, s, :] if sample else SDp[:, h, :]

                def state_terms(out_ap, srcT, first):
                    if not sample:
                        P.mm(out_ap, R(srcT), R(S_of(0)), start=first, stop=True)
                    else:
                        P.copy("dve", EXP.rearrange("p (s w) -> p s w", w=136)[:, :, 0:8],
                               srcT.rearrange("p (s t) -> p s t", t=8))
                        for s in range(16):
                            P.mm(out_ap, EXP[:, s * 128:(s + 1) * 128], S_of(s), start=(first and s == 0), stop=(s == 15))

                bw = PS.alloc()
                P.mm(ps[bw][:, 0:128], R(kb[:, hs]), R(XT[:, h, :]))
                P.act(R(nwT4[:, i, :]), ps[bw][:, 0:128], AF.Copy, scale=-1.0)
                yield
                P.mm(ps[bw][:, 128:256], R(XT[:, h, :]), R(vb[:, hs]), start=True, stop=False)
                state_terms(ps[bw][:, 128:256], nwT4[:, i, :], False)
                P.copy("act", R(delta4[:, i, :]), ps[bw][:, 128:256])
                yield
                state_terms(ps[bw][:, 256:384], qkn[:, h, 1, :], True)
                P.mm(ps[bw][:, 384:512], R(QKm[:, h, :]), R(delta4[:, i, :]))
                P.act(osb4[:, i, :], ps[bw][:, 256:384], AF.Copy, scale=egc[:, h:h + 1])
                P.tt("dve", osb4[:, i, :], osb4[:, i, :], ps[bw][:, 384:512], ALU.add)
                yield
                PS.release(bw)
                yield
                P.act(junk, osb4[:, i, :], AF.Square, accum_out=ssq[:, 4 + h:5 + h])
                P.ts("dve", rstd[:, 4 + h:5 + h], ssq[:, 4 + h:5 + h], 1.0 / 128.0, 1e-5, ALU.mult, ALU.add)
                P.tt("pool", rstd[:, 4 + h:5 + h], rstd[:, 4 + h:5 + h], c_nhalf[:, 0:1], ALU.pow)
                P.stt(og[:, 512 + h * 128:512 + (h + 1) * 128], osb4[:, i, :], rstd[:, 4 + h:5 + h],
                      gate[:, 512 + h * 128:512 + (h + 1) * 128], ALU.mult, ALU.mult)
                yield
                if not sample:
                    bs = PS.alloc()
                    P.mm(ps[bs][:, 0:128], R(kd[:, hs]), R(delta4[:, i, :]))
                    P.stt(R(SDp[:, h, :]), SDp[:, h, :], decs[:, h * 16:h * 16 + 1], ps[bs][:, 0:128], ALU.mult, ALU.add)
                    PS.release(bs)
                    yield
                else:
                    for s in range(16):
                        if s % 2:
                            P.ts("dve", VEX[:, s, :], delta4[:, i, :], c_bind[:, s:s + 1], None, ALU.mult)
                        else:
                            P.act(VEX[:, s, :], delta4[:, i, :], AF.Copy, scale=c_bind[:, s:s + 1])
                    for sg in range(4):
                        bs = PS.alloc()
                        P.mm(ps[bs][:, :], kd[:, hs], VEX[:, sg * 4:(sg + 1) * 4, :].rearrange("p a b -> p (a b)"))
                        for s4 in range(4):
                            s = sg * 4 + s4
                            P.stt(STh[:, s, :], STh[:, s, :], decs[:, h * 16 + s:h * 16 + s + 1],
                                  ps[bs][:, s4 * 128:(s4 + 1) * 128], ALU.mult, ALU.add)
                        PS.release(bs)
                    P.dma("sp", sgdn_o[:, h].rearrange("s d v -> d s v"), STh)


            if sample:
                run_il([head_gen(0), head_gen(1)])
                run_il([head_gen(2), head_gen(3)])
            else:
                run_il([head_gen(0), head_gen(1), head_gen(2), head_gen(3)])

            bto = PS.alloc()
            pb16 = ps[bto].bitcast(BF16)
            for e8 in range(8):
                P.tr(pb16[:, e8 * 128:(e8 + 1) * 128], og[:, e8 * 128:(e8 + 1) * 128], b_ident)
            P.act(oT[:, 0:4, :].rearrange("p a b -> p (a b)"), pb16[:, 0:512], AF.Copy, scale=GNC[:, 0:1])
            P.ts("dve", oT[:, 4:8, :].rearrange("p a b -> p (a b)"), pb16[:, 512:1024], GNC[:, 1:2], None, ALU.mult)
            PS.release(bto)
            bm0, bm1 = PS.alloc(), PS.alloc()
            for half, bk in ((0, bm0), (1, bm1)):
                for e8 in range(8):
                    P.mm(ps[bk][:, :], oT[:, e8, :], wout[:, e8, half * 512:(half + 1) * 512], start=(e8 == 0), stop=(e8 == 7))
            for half, bk in ((0, bm0), (1, bm1)):
                P.stt(rr[:, half * 512:(half + 1) * 512], xt[:, half * 512:(half + 1) * 512], ALPHA, ps[bk][:, :], ALU.mult, ALU.add)
            PS.release(bm0)
            PS.release(bm1)
            if "rr0" in dbg and t == 0:
                P.dma("sp", dbg["rr0"], rr)
                P.dma("sp", dbg["sm"], sm)
                P.dma("sp", dbg["gate"], gate)
                P.dma("sp", dbg["xt"], xt)
            layer_norm(rr, p_lng, p_lnb)
            if "x1" in dbg and t == 0:
                P.dma("sp", dbg["x1"], rr)
            P.dma("sp", x1s[t], rr)

        def layer_norm(buf, gB_, bB_):
            for half in range(2):
                P.add("dve", lambda e, half=half: e.bn_stats(stats[:, half * 6:(half + 1) * 6], buf[:, half * 512:(half + 1) * 512]),
                      [buf[:, half * 512:(half + 1) * 512]], [stats[:, half * 6:(half + 1) * 6]])
            P.add("dve", lambda e: e.bn_aggr(stats[:, 12:14], stats[:, 0:12]), [stats[:, 0:12]], [stats[:, 12:14]])
            P.ts("dve", stats[:, 14:15], stats[:, 13:14], 1e-5, None, ALU.add)
            P.tt("pool", stats[:, 14:15], stats[:, 14:15], nhalfP[:, 0:1], ALU.pow)
            P.ts("dve", buf, buf, stats[:, 12:13], stats[:, 14:15], ALU.subtract, ALU.mult)
            P.tt("dve", buf, buf, gB_, ALU.mult)
            P.tt("dve", buf, buf, bB_, ALU.add)

        dbg_taps = {}
        import os as _os
        order = [NT - 1] + list(range(NT - 1))
        if _os.environ.get("KORDER") == "p":
            order = list(range(NT))
        import os
        KP1 = int(os.environ.get("KP1", "17"))
        KP2 = int(os.environ.get("KP2", "17"))
        wg_loaded = []
        p1_order = order[:KP1]
        tile_pos = {t: i for i, t in enumerate(p1_order)}
        if p1_order:
            front(p1_order[0])
        for t in p1_order:
            phase1_tile(t)
        P.dma("sp", pgla_o.rearrange("(c r) d v -> (r d) c v", r=2), SGp)
        P.dma("sp", pgdn_o.rearrange("h d v -> d h v"), SDp)

        def a2(shape, dt=F32, parts=128):
            return A.alloc("p2", shape, dt, parts)

        x2b = a2([2, D])
        x2T = a2([8, 128], BF16)
        hidT = a2([NFC, 128], BF16)
        hidg = a2([2, 512], BF16)
        if not wg_loaded:
            for k in range(8):
                P.dma("pool", wg_sb[:, k, :], w_g[k * 128:(k + 1) * 128, :])
        for k in range(2 if wg_loaded else 0, 8):
            P.dma("pool", wu_sb[:, k, :], w_u[k * 128:(k + 1) * 128, :])
        for f in range(NFC):
            P.dma("pool", wd_sb[:, f, :], w_d[f * 128:(f + 1) * 128, :])
        P.dma("sp", PRM[:, 256:2304], prm2_i)

        def front2(t):
            x2 = x2b[:, pos2[t] % 2, :]
            P.dma("sp", x2, x1s[t])
            b0, b1 = PS.alloc(), PS.alloc()
            for k in range(8):
                bk = ps[b0] if k < 4 else ps[b1]
                P.tr(bk[:, (k % 4) * 128:(k % 4 + 1) * 128], x2[:, k * 128:(k + 1) * 128], identP)
            P.act(x2T[:, 0:4, :].rearrange("p a b -> p (a b)"), ps[b0][:, :], AF.Copy)
            P.copy("dve", x2T[:, 4:8, :].rearrange("p a b -> p (a b)"), ps[b1][:, :])
            PS.release(b0)
            PS.release(b1)

        def phase2_tile(t):
            x2 = x2b[:, pos2[t] % 2, :]
            def hid_transposes(gi, g0, n):
                hg = hidg[:, gi % 2, :]
                bt_ = PS.alloc()
                pb16 = ps[bt_].bitcast(BF16)
                nf = n // 128
                for j in range(nf):
                    P.tr(pb16[:, j * 128:(j + 1) * 128], hg[:, j * 128:(j + 1) * 128], b_ident)
                f0 = g0 // 128
                if gi % 2 == 0:
                    P.act(hidT[:, f0:f0 + nf, :].rearrange("p a b -> p (a b)"), pb16[:, 0:nf * 128], AF.Copy)
                else:
                    P.copy("dve", hidT[:, f0:f0 + nf, :].rearrange("p a b -> p (a b)"), pb16[:, 0:nf * 128])
                PS.release(bt_)

            pending = None
            for gi, g0 in enumerate(range(0, DFF, 512)):
                n = min(512, DFF - g0)
                hg = hidg[:, gi % 2, :]
                bgt, bup = PS.alloc(), PS.alloc()
                for k in range(8):
                    P.mm(ps[bgt][:, 0:n], x2T[:, k, :], wg_sb[:, k, g0:g0 + n], start=(k == 0), stop=(k == 7))
                for k in range(8):
                    P.mm(ps[bup][:, 0:n], x2T[:, k, :], wu_sb[:, k, g0:g0 + n], start=(k == 0), stop=(k == 7))
                P.act(hg[:, 0:n], ps[bgt][:, 0:n], AF.Silu)
                P.tt("dve", hg[:, 0:n], hg[:, 0:n], ps[bup][:, 0:n], ALU.mult)
                PS.release(bgt)
                PS.release(bup)
                if pending is not None:
                    hid_transposes(*pending)
                pending = (gi, g0, n)
            if pos2[t] + 1 < len(p2_order):
                front2(p2_order[pos2[t] + 1])
            hid_transposes(*pending)
            bm0, bm1 = PS.alloc(), PS.alloc()
            for half, bk in ((0, bm0), (1, bm1)):
                for f in range(NFC):
                    P.mm(ps[bk][:, :], hidT[:, f, :], wd_sb[:, f, half * 512:(half + 1) * 512], start=(f == 0), stop=(f == NFC - 1))
            for half, bk in ((0, bm0), (1, bm1)):
                P.stt(x2[:, half * 512:(half + 1) * 512], x2[:, half * 512:(half + 1) * 512], ALPHA, ps[bk][:, :], ALU.mult, ALU.add)
            PS.release(bm0)
            PS.release(bm1)
            layer_norm(x2, p_lng, p_lnb)
            P.dma("sp", y_o[t], x2)

        p2_order = order[:KP2]
        pos2 = {t: i for i, t in enumerate(p2_order)}
        if p2_order:
            front2(p2_order[0])
        for t in p2_order:
            phase2_tile(t)
        P.barrier("sp", [y_o, pgla_o, pgdn_o, pconv_o, sgla_o, sgdn_o, sconv_o])
        st = P.emit(es)
        print("ops", len(P.ops), "per-engine (instr, waits):", st, "p1_end", p1_end, "top", A.top)
    return nc


def _consts():
    i = np.arange(128)
    same8 = (i[:, None] // 8) == (i[None, :] // 8)
    up = i[:, None] <= i[None, :]
    lowstrict = i[:, None] > i[None, :]
    Uf = up.astype(np.float32)
    Us = (up & same8).astype(np.float32)
    mats = [np.eye(128, dtype=np.float32), Uf, Us, -Uf, -Us, -Uf / 16.0, -Us / 16.0,
            np.where(up, 0.0, NEG), np.where(up & same8, 0.0, NEG),
            np.where(lowstrict, 0.0, NEG), np.where(lowstrict & same8, 0.0, NEG),
            ((i[:, None] // 64) == (i[None, :] // 64)).astype(np.float32),
            np.ones((128, 128), np.float32), np.full((128, 128), -0.5, np.float32)]
    bind = ((i[:, None] // 8) == np.arange(16)[None, :]).astype(np.float32)
    return np.ascontiguousarray(np.concatenate([m.astype(np.float32) for m in mats] + [bind], axis=1))


_NC_CACHE = {}


def kernel(x_prompt, x_sample, state_gla, state_gdn, state_gdn_conv, w_in, gla_w_gate_up,
           gla_b_gate, gla_norm_g, gdn_conv_w, gdn_a_log, gdn_dt_bias, gdn_norm_g, w_out,
           ln1_g, ln1_b, w_ffn_gate, w_ffn_up, w_ffn_down, ln2_g, ln2_b):
    f = lambda a: np.ascontiguousarray(np.asarray(a, dtype=np.float32))
    x_prompt, x_sample = f(x_prompt), f(x_sample)
    state_gla, state_gdn, state_gdn_conv = f(state_gla), f(state_gdn), f(state_gdn_conv)
    if "nc" not in _NC_CACHE:
        _NC_CACHE["nc"] = build_program()
    nc = _NC_CACHE["nc"]
    rep = lambda v: np.broadcast_to(f(v).reshape(1, -1), (128, f(v).size))
    prm = np.ascontiguousarray(np.concatenate([rep(gla_norm_g), rep(gdn_norm_g), rep(ln1_g), rep(ln1_b),
                                               rep(gdn_a_log), rep(gdn_dt_bias), rep(gla_b_gate)], axis=1))
    prm2 = np.ascontiguousarray(np.concatenate([rep(ln2_g), rep(ln2_b)], axis=1))
    cw = f(gdn_conv_w)[0]
    cwT = np.ascontiguousarray(cw.reshape(4, 12, 128).transpose(2, 1, 0).reshape(128, 48))
    gnc = np.ascontiguousarray(np.stack([f(gla_norm_g).reshape(128), f(gdn_norm_g).reshape(128)], axis=1))
    common = {"gnc": gnc, "w_in": f(w_in)[0], "w_out": f(w_out)[0], "w_g": f(w_ffn_gate)[0], "w_u": f(w_ffn_up)[0],
              "w_d": f(w_ffn_down)[0], "wgu": f(gla_w_gate_up)[0], "cwT": cwT, "consts": _consts(),
              "prm": prm, "prm2": prm2}
    in_maps = []
    for c in range(8):
        xs = np.concatenate([x_prompt[c].reshape(16, 128, D), x_sample[16 * c:16 * c + 16].reshape(1, 128, D)], axis=0)
        m = dict(common)
        m["xs"] = np.ascontiguousarray(xs)
        m["sgla"] = np.ascontiguousarray(state_gla[0, 16 * c:16 * c + 16])
        m["sgdn"] = np.ascontiguousarray(state_gdn[0, 16 * c:16 * c + 16])
        m["sconv"] = np.ascontiguousarray(state_gdn_conv[0, 16 * c:16 * c + 16].reshape(48, 1536))
        in_maps.append(m)
    res = run_bass_kernel_spmd(nc, in_maps, core_ids=list(range(8)))
    R = res.results
    _NC_CACHE["last"] = R
    yp = np.stack([R[c]["y"][0:16].reshape(2048, D) for c in range(8)])
    ys = np.concatenate([R[c]["y"][16].reshape(16, 8, D) for c in range(8)], axis=0)
    pgla = np.stack([R[c]["pgla"] for c in range(8)])[None]
    pgdn = np.stack([R[c]["pgdn"] for c in range(8)])[None]
    pconv = np.stack([R[c]["pconv"] for c in range(8)])[None]
    sgla = np.concatenate([R[c]["sgla_o"] for c in range(8)], axis=0)[None]
    sgdn = np.concatenate([R[c]["sgdn_o"] for c in range(8)], axis=0)[None]
    sconv = np.concatenate([R[c]["sconv_o"] for c in range(8)], axis=0)[None]
    return tuple(np.ascontiguousarray(a, dtype=np.float32) for a in (yp, ys, pgla, pgdn, pconv, sgla, sgdn, sconv))
```

```python
import contextlib
from math import prod
import numpy as np
import concourse.bass as bass
import concourse.mybir as mybir
from concourse.bass_utils import run_bass_kernel_spmd

F32 = mybir.dt.float32
BF16 = mybir.dt.bfloat16
F32R = mybir.dt.float32r


def R(ap):
    return ap.bitcast(F32R)
AF = mybir.ActivationFunctionType
ALU = mybir.AluOpType

N_DMA_SLOTS = 64
_ESZ = {str(F32): 4, str(BF16): 2}

D = 1024
NT = 17
IN_DIM = 3608
DFF = 2816
NFC = DFF // 128
OFF_GQ, OFF_GK, OFF_GV, OFF_GG, OFF_GA, OFF_DQKV, OFF_DG, OFF_DAB = 0, 256, 512, 1024, 1536, 1552, 3088, 3600
ALPHA = 2.0 ** 0.25
NEG = -30000.0
import os as _os0
DEBUG = {"rr0": [128, 1024], "sm": [128, 64], "gate": [128, 1024], "xt": [128, 1024], "x1": [128, 1024], "x2": [128, 1024], "r2": [128, 1024], "st2": [128, 16]} if _os0.environ.get("KDBG") else {}


_ALIAS = {}


def _box(ap):
    b = _box0(ap)
    al = _ALIAS.get(b[0])
    if al is not None:
        return (al[0], b[1], b[2], b[3] + al[1], b[4] + al[1])
    return b


def _box0(ap):
    name = ap.tensor.name
    pat = ap.ap
    off = int(ap.offset)
    es = _ESZ.get(str(ap.dtype), 4)
    sp = str(ap.space)
    if sp in ("SB", "PSUM"):
        pstep, pcnt = pat[0]
        if pstep > 0:
            p0 = off // pstep
            f0 = off - p0 * pstep
        else:
            p0, f0 = 0, off
        ext = 1
        for st, cn in pat[1:]:
            ext += abs(st) * (cn - 1)
        return (name, p0, p0 + pcnt, f0 * es, (f0 + ext) * es)
    ext = 1
    for st, cn in pat:
        ext += abs(st) * (cn - 1)
    return (name, 0, 1, off * es, (off + ext) * es)


def _overlap(a, b):
    return a[1] < b[2] and b[1] < a[2] and a[3] < b[4] and b[3] < a[4]


def _covers(a, b):
    return a[1] <= b[1] and a[2] >= b[2] and a[3] <= b[3] and a[4] >= b[4]


class Op:
    __slots__ = ("idx", "eng", "fn", "dma", "deps", "signal", "semval", "slot", "slot_prev")

    def __init__(self, idx, eng, fn, dma):
        self.idx, self.eng, self.fn, self.dma = idx, eng, fn, dma
        self.deps = set()
        self.signal = False
        self.semval = None
        self.slot = None
        self.slot_prev = 0


class Prog:
    ENGS = ("pe", "act", "dve", "pool", "sp")

    def __init__(self, nc, untracked=()):
        self.nc = nc
        self.ops = []
        self.wr = {}
        self.rd = {}
        self.untracked = set(untracked)

    def add(self, eng, fn, reads=(), writes=(), dma=False):
        op = Op(len(self.ops), eng, fn, dma)
        self.ops.append(op)
        for ap in reads:
            b = _box(ap)
            if b[0] in self.untracked:
                continue
            for (wb, wi) in self.wr.get(b[0], ()):
                if _overlap(wb, b):
                    op.deps.add(wi)
            self.rd.setdefault(b[0], []).append((b, op.idx))
        for ap in writes:
            b = _box(ap)
            if b[0] in self.untracked:
                continue
            wl = self.wr.get(b[0], [])
            for (wb, wi) in wl:
                if _overlap(wb, b):
                    w = self.ops[wi]
                    if w.dma or op.dma or w.eng != op.eng:
                        op.deps.add(wi)
            rl = self.rd.get(b[0], [])
            for (rb, ri) in rl:
                if ri != op.idx and _overlap(rb, b):
                    r = self.ops[ri]
                    if r.dma or op.dma or r.eng != op.eng:
                        op.deps.add(ri)
            self.wr[b[0]] = [(wb, wi) for (wb, wi) in wl if not _covers(b, wb)] + [(b, op.idx)]
            self.rd[b[0]] = [(rb, ri) for (rb, ri) in rl if (ri == op.idx) or not _covers(b, rb)]
        for ap in list(reads) + list(writes):
            if str(ap.space) == "PSUM":
                key = "LOCK_" + ap.tensor.name
                last = self.wr.get(key)
                if last is not None and last != op.idx and self.ops[last].eng != op.eng:
                    op.deps.add(last)
                self.wr[key] = op.idx
        return op

    def dma(self, q, out, in_, **kw):
        return self.add(q, lambda e: e.dma_start(out=out, in_=in_, **kw), [in_], [out], dma=True)

    def mm(self, out, lhsT, rhs, start=True, stop=True):
        return self.add("pe", lambda e: e.matmul(out, lhsT, rhs, start=start, stop=stop), [lhsT, rhs], [out])

    def tr(self, out, in_, ident):
        return self.add("pe", lambda e: e.transpose(out, in_, ident), [in_, ident], [out])

    def act(self, out, in_, func, bias=None, scale=None, accum_out=None):
        reads = [in_]
        kw = {}
        if bias is not None:
            kw["bias"] = bias
            if not isinstance(bias, (int, float)):
                reads.append(bias)
        if scale is not None:
            kw["scale"] = scale
            if not isinstance(scale, (int, float)):
                reads.append(scale)
        writes = [out]
        if accum_out is not None:
            kw["accum_out"] = accum_out
            writes.append(accum_out)
        return self.add("act", lambda e: e.activation(out, in_, func, **kw), reads, writes)

    def tt(self, eng, out, in0, in1, op):
        return self.add(eng, lambda e: e.tensor_tensor(out, in0, in1, op), [in0, in1], [out])

    def ts(self, eng, out, in0, s1, s2, op0, op1=None):
        reads = [in0]
        for s in (s1, s2):
            if s is not None and not isinstance(s, (int, float)):
                reads.append(s)
        if op1 is None:
            return self.add(eng, lambda e: e.tensor_scalar(out, in0, s1, None, op0), reads, [out])
        return self.add(eng, lambda e: e.tensor_scalar(out, in0, s1, s2, op0, op1), reads, [out])

    def stt(self, out, in0, scalar, in1, op0, op1):
        reads = [in0, in1]
        if not isinstance(scalar, (int, float)):
            reads.append(scalar)
        return self.add("dve", lambda e: e.scalar_tensor_tensor(out, in0, scalar, in1, op0, op1), reads, [out])

    def copy(self, eng, out, in_):
        if eng == "act":
            return self.add("act", lambda e: e.copy(out, in_), [in_], [out])
        return self.add(eng, lambda e: e.tensor_copy(out, in_), [in_], [out])

    def memset(self, eng, out, val):
        return self.add(eng, lambda e: e.memset(out, val), [], [out])

    def barrier(self, eng, reads):
        return self.add(eng, None, reads, [])

    def emit(self, es):
        nc = self.nc
        import os
        ops = self.ops[:int(os.environ.get("KSTOP", "100000000"))]
        for op in ops:
            for d in op.deps:
                ops[d].signal = True
        cnt = {e: 0 for e in self.ENGS}
        slot_uses = [0] * N_DMA_SLOTS
        half = N_DMA_SLOTS // 2
        nd = {True: 0, False: 0}
        for op in ops:
            if op.dma:
                sw = (op.eng == "pool")
                op.slot = (nd[sw] % half) + (half if sw else 0)
                nd[sw] += 1
                op.slot_prev = slot_uses[op.slot] * 16
                slot_uses[op.slot] += 1
                op.semval = slot_uses[op.slot] * 16
            elif op.signal:
                cnt[op.eng] += 1
                op.semval = cnt[op.eng]
        esem = {e: es.enter_context(nc.semaphore("s_" + e)) for e in self.ENGS}
        dsem = [es.enter_context(nc.semaphore("d%d" % i)) for i in range(N_DMA_SLOTS)]
        block = es.enter_context(nc.Block())
        stats = {}

        def gen(E):
            def body(e):
                waited = {}
                nw = ni = 0
                for op in ops:
                    if op.eng != E:
                        continue
                    need = {}
                    for d in op.deps:
                        dop = ops[d]
                        s = ("d", dop.slot) if dop.dma else ("e", dop.eng)
                        if need.get(s, 0) < dop.semval:
                            need[s] = dop.semval
                    if op.dma and op.slot_prev > 0:
                        s = ("d", op.slot)
                        if need.get(s, 0) < op.slot_prev:
                            need[s] = op.slot_prev
                    for s, v in need.items():
                        if waited.get(s, 0) >= v:
                            continue
                        waited[s] = v
                        e.wait_ge(dsem[s[1]] if s[0] == "d" else esem[s[1]], v)
                        nw += 1
                    if op.fn is None:
                        continue
                    ins = op.fn(e)
                    ni += 1
                    if op.dma:
                        ins.then_inc(dsem[op.slot], 16)
                    elif op.signal:
                        ins.then_inc(esem[E], 1)
                stats[E] = (ni, nw)
            return body

        block.tensor(gen("pe"))
        block.scalar(gen("act"))
        block.vector(gen("dve"))
        block.gpsimd(gen("pool"))
        block.sync(gen("sp"))
        return stats


class Arena:
    def __init__(self, nc, es, nbytes):
        self.nbytes = nbytes
        self.h = {BF16: es.enter_context(nc.sbuf_tensor("A", [128, nbytes // 2], BF16))}
        self.h[F32] = self.h[BF16].bitcast(F32)
        self.top = nbytes
        self.cur = {}

    def view(self, off, shape, dt, parts=128):
        es = 4 if dt == F32 else 2
        assert off % 4 == 0
        n = prod(shape)
        ap = self.h[dt][0:parts, off // es: off // es + n]
        if len(shape) > 1:
            names = ["a%d" % i for i in range(len(shape))]
            kw = {nm: s for nm, s in zip(names[1:], shape[1:])}
            ap = ap.rearrange("p (%s) -> p %s" % (" ".join(names), " ".join(names)), **kw)
        return ap

    def persist(self, shape, dt, parts=128):
        nb = (prod(shape) * (4 if dt == F32 else 2) + 31) // 32 * 32
        self.top -= nb
        return self.view(self.top, shape, dt, parts)

    def alloc(self, region, shape, dt, parts=128):
        nb = (prod(shape) * (4 if dt == F32 else 2) + 31) // 32 * 32
        off = self.cur[region]
        self.cur[region] = off + nb
        assert self.cur[region] <= self.top, (region, self.cur[region], self.top)
        return self.view(off, shape, dt, parts)


class PsumPool:
    def __init__(self, nc, es):
        self.banks = [es.enter_context(nc.psum_tensor("ps%d" % i, [128, 512], F32)) for i in range(8)]
        self.free = list(range(8))

    def alloc(self):
        assert self.free, "out of PSUM banks"
        return self.free.pop(0)

    def release(self, b):
        self.free.append(b)


def build_program():
    nc = bass.Bass("TRN2", target_bir_lowering=False)

    def din(name, shape):
        return nc.dram_tensor(name, list(shape), F32, kind="ExternalInput").ap()

    def dout(name, shape):
        return nc.dram_tensor(name, list(shape), F32, kind="ExternalOutput").ap()

    xs = din("xs", [NT, 128, D])
    sgla_i = din("sgla", [16, 4, 64, 128])
    sgdn_i = din("sgdn", [16, 4, 128, 128])
    sconv_i = din("sconv", [48, 1536])
    w_in = din("w_in", [D, IN_DIM])
    w_out = din("w_out", [D, D])
    w_g = din("w_g", [D, DFF])
    w_u = din("w_u", [D, DFF])
    w_d = din("w_d", [DFF, D])
    wgu_i = din("wgu", [16, 256])
    cwT_i = din("cwT", [128, 48])
    NCONST = 128 * 14 + 16
    consts_i = din("consts", [128, NCONST])
    prm_i = din("prm", [128, 2568])
    prm2_i = din("prm2", [128, 2048])
    gnc_i = din("gnc", [128, 2])

    y_o = dout("y", [NT, 128, D])
    pgla_o = dout("pgla", [4, 64, 128])
    pgdn_o = dout("pgdn", [4, 128, 128])
    pconv_o = dout("pconv", [3, 1536])
    sgla_o = dout("sgla_o", [16, 4, 64, 128])
    sgdn_o = dout("sgdn_o", [16, 4, 128, 128])
    sconv_o = dout("sconv_o", [16, 3, 1536])
    x1s = nc.dram_tensor("x1s", [NT, 128, D], F32, kind="Internal").ap()
    dbg = {}
    for k, shp in DEBUG.items():
        dbg[k] = dout("dbg_" + k, shp)

    es = contextlib.ExitStack()
    with es:
        AR_SZ = 33856 + 10240
        A = Arena(nc, es, 212800 - AR_SZ)
        arh = es.enter_context(nc.sbuf_tensor("AR", [128, AR_SZ // 4], F32))
        ar_cur = [0]

        def ar1(shape):
            n = prod(shape)
            ap = arh[:, ar_cur[0]:ar_cur[0] + n]
            ar_cur[0] += (n + 7) // 8 * 8
            assert ar_cur[0] * 4 <= AR_SZ, ar_cur[0] * 4
            if len(shape) > 1:
                names = ["a%d" % i for i in range(len(shape))]
                kw = {nm: sz for nm, sz in zip(names[1:], shape[1:])}
                ap = ap.rearrange("p (%s) -> p %s" % (" ".join(names), " ".join(names)), **kw)
            return ap

        PS = PsumPool(nc, es)
        ps = PS.banks
        P = Prog(nc, untracked=["xs", "sgla", "sgdn", "sconv", "w_in", "w_out", "w_g", "w_u", "w_d",
                                "wgu", "cwT", "consts", "prm", "prm2", "gnc"])

        A.cur["p1"] = 74112
        A.cur["p2"] = 135168
        CT = A.alloc("p1", [NCONST], F32)
        identP = A.persist([128], F32)
        nhalfP = A.persist([8], F32)
        (c_ident, c_Uf, c_Us, c_nUf, c_nUs, c_u16f, c_u16s,
         c_mbuf, c_mbus, c_mblf, c_mbls, c_blk64, c_ones, c_nhalf) = [CT[:, i * 128:(i + 1) * 128] for i in range(14)]
        c_bind = CT[:, 1792:1808]
        CB = A.persist([6 * 128], BF16)
        b_ident, b_mbuf, b_mbus, b_mblf, b_mbls, b_ones = [CB[:, i * 128:(i + 1) * 128] for i in range(6)]
        PRM = A.persist([2568], F32)
        p_gng, p_gnd = PRM[:, 0:128], PRM[:, 128:256]
        p_lng, p_lnb = PRM[:, 256:1280], PRM[:, 1280:2304]
        p_alog, p_dtb, p_bgate = PRM[:, 2304:2308], PRM[:, 2308:2312], PRM[0:1, 2312:2568]
        WGU = A.persist([256], F32, parts=16)
        CWT = A.persist([12, 4], F32)
        NEGA = A.persist([4], F32)
        GNC = A.persist([2], F32)
        SGp = ar1([2, 128])
        SDp = ar1([4, 128])
        rU = {}
        for nm_ in ("Uf", "Us", "nUf", "nUs", "u16f", "u16s"):
            rU[nm_] = ar1([128])
        HALO = A.persist([12, 3], F32)
        stats = A.persist([16], F32)

        win = A.view(0, [8, IN_DIM], BF16)
        wout = A.view(57728, [8, D], BF16)
        wg_sb = A.view(0, [8, DFF], BF16)
        wu_sb = A.view(45056, [8, DFF], BF16)
        wd_sb = A.view(90112, [NFC, D], BF16)

        def a1(shape, dt=F32, parts=128):
            return A.alloc("p1", shape, dt, parts)

        xtb = a1([2, D])
        xT = a1([8, 128], BF16)
        gaT = a1([128], F32, parts=16)
        lz = ar1([256])
        eb = a1([256])
        enb = a1([256])
        qtT = a1([256])
        ktT = ar1([256])
        qtTm = ar1([4, 128])
        ktok = ar1([256])
        vg = ar1([512])
        gate = a1([1024])
        atm = ar1([512])
        og = a1([1024], BF16)
        oT = a1([8, 128], BF16)
        sm = a1([64])
        tokd = A.view(A.cur["p1"], [1536], F32)
        cbuf = a1([12, 176])
        SCV = A.view(A.cur["p1"], [1536], F32, parts=48)
        cv_off = A.cur["p1"]
        cv = a1([12, 128])
        sqb = a1([8, 128], BF16)
        rinv = a1([8, 128])
        junk = rinv[:, 7, :]
        ST2 = A.view(cv_off + 2048, [16, 128], F32)
        qkn = ar1([4, 2, 128])
        gB = ar1([4, 128])
        GT = a1([2, 128])
        G = a1([2, 128])
        Nn = ar1([2, 128])
        Cm = ar1([2, 128])
        Mb = ar1([2, 2, 128])
        MTb = ar1([2, 2, 128])
        Pb = ar1([2, 2, 128])
        XdT = ar1([2, 128])
        Yb = ar1([2, 128])
        XT = ar1([4, 128])
        QKm = ar1([4, 128])
        vb = ar1([512])
        kb = ar1([512])
        kd = ar1([512])
        nwT = ar1([2, 128])
        delta = ar1([2, 128])
        osb = a1([2, 128])
        ST = a1([16, 128])
        EXP = a1([2176])
        VEX = a1([16, 128])
        GT1 = a1([2, 128])
        G1 = a1([2, 128])
        bufsets = [
            (Nn, Cm, Mb, MTb, Pb, XdT, Yb, GT, G),
            (lz.rearrange("p (a b) -> p a b", a=2), ktT.rearrange("p (a b) -> p a b", a=2),
             atm.rearrange("p (a b c) -> p a b c", a=2, b=2), vg.rearrange("p (a b c) -> p a b c", a=2, b=2),
             qtTm.rearrange("p (a b) c -> p a b c", a=2), ktok.rearrange("p (a b) -> p a b", a=2),
             nwT, GT1, G1),
        ]
        nwT4 = Mb.rearrange("p a b c -> p (a b) c")
        delta4 = MTb.rearrange("p a b c -> p (a b) c")
        osb4_p = rinv[:, 0:4, :]
        osb4_s = cv[:, 0:4, :]
        p1_end = A.cur["p1"]

        ssq = sm[:, 0:8]
        rstd = sm[:, 8:16]
        dab = sm[:, 16:24]
        gg = ar1([4])
        beta = sm[:, 28:32]
        gcs = sm[:, 32:36]
        ngc = sm[:, 36:40]
        egc = sm[:, 40:44]
        bgs = sm[:, 44:48]
        egl = sm[:, 48:52]
        tmp4 = sm[:, 52:56]
        gcl = sm[:, 56:60]
        decs = a1([64])

        P.dma("sp", CT, consts_i)
        P.copy("dve", identP, c_ident)
        P.copy("dve", nhalfP, c_nhalf[:, 0:8])
        P.dma("sp", PRM, prm_i)
        P.dma("sp", WGU, wgu_i)
        P.dma("sp", GNC, gnc_i)
        P.dma("sp", CWT.rearrange("p a b -> p (a b)"), cwT_i)
        for (c0, c1) in ((1024, 1552), (3088, 3608), (1552, 2064), (2064, 2576), (2576, 3088), (0, 1024)):
            for k in range(8):
                P.dma("pool", win[:, k, c0:c1], w_in[k * 128:(k + 1) * 128, c0:c1])
        for k in range(8):
            P.dma("pool", wout[:, k, :], w_out[k * 128:(k + 1) * 128, :])
        for i, src in enumerate([c_ident, c_mbuf, c_mbus, c_mblf, c_mbls, c_ones]):
            P.copy("dve", CB[:, i * 128:(i + 1) * 128], src)
        P.act(NEGA, p_alog, AF.Exp)
        P.ts("dve", NEGA, NEGA, -1.0, None, ALU.mult)
        P.ts("pool", R(SGp.rearrange("p a b -> p (a b)")), CT[:, 0:256], 0.0, None, ALU.mult)
        P.ts("pool", R(SDp.rearrange("p a b -> p (a b)")), CT[:, 0:512], 0.0, None, ALU.mult)
        P.memset("pool", HALO, 0.0)
        P.memset("pool", EXP, 0.0)
        for nm_, src_ in (("Uf", c_Uf), ("Us", c_Us), ("nUf", c_nUf), ("nUs", c_nUs), ("u16f", c_u16f), ("u16s", c_u16s)):
            P.copy("dve", R(rU[nm_]), src_)

        conv_hoisted = {}

        def conv_gen():
            P.copy("dve", cbuf[:, :, 0:3], HALO)
            for g3 in range(3):
                bd = PS.alloc()
                for j in range(4):
                    c0 = OFF_DQKV + (g3 * 4 + j) * 128
                    for k in range(8):
                        P.mm(ps[bd][:, j * 128:(j + 1) * 128], win[:, k, c0:c0 + 128], xT[:, k, :], start=(k == 0), stop=(k == 7))
                for j in range(4):
                    P.copy("act", cbuf[:, g3 * 4 + j, 3:131], ps[bd][:, j * 128:(j + 1) * 128])
                PS.release(bd)
                yield
            P.copy("dve", HALO, cbuf[:, :, 128:131])
            for cc in range(12):
                P.act(cv[:, cc, :], cbuf[:, cc, 0:128], AF.Copy, scale=CWT[:, cc, 0:1])
                for i in range(1, 4):
                    P.stt(cv[:, cc, :], cbuf[:, cc, i:i + 128], CWT[:, cc, i:i + 1], cv[:, cc, :], ALU.mult, ALU.add)
                yield

        def front(t):
            xt = xtb[:, tile_pos[t] % 2, :]
            P.dma("sp", xt, xs[t])
            b0, b1 = PS.alloc(), PS.alloc()
            for k in range(8):
                bk = ps[b0] if k < 4 else ps[b1]
                P.tr(bk[:, (k % 4) * 128:(k % 4 + 1) * 128], xt[:, k * 128:(k + 1) * 128], c_ident)
            P.act(xT[:, 0:4, :].rearrange("p a b -> p (a b)"), ps[b0][:, :], AF.Copy)
            P.copy("dve", xT[:, 4:8, :].rearrange("p a b -> p (a b)"), ps[b1][:, :])
            PS.release(b0)
            PS.release(b1)

        def phase1_tile(t):
            sample = (t == NT - 1)
            osb4 = osb4_s if sample else osb4_p
            nseq = 16 if sample else 1
            U = rU["Us"] if sample else rU["Uf"]
            nU = rU["nUs"] if sample else rU["nUf"]
            u16 = rU["u16s"] if sample else rU["u16f"]
            m01 = c_Us if sample else c_Uf
            mbu = b_mbus if sample else b_mbuf
            mbl = b_mbls if sample else b_mblf
            Ulast = rU["Us"][:, 7:128:8] if sample else rU["Uf"][:, 126:128]
            nlast = 16 if sample else 2
            nlev = 3 if sample else 6

            xt = xtb[:, tile_pos[t] % 2, :]
            rr = xt

            def mm_tok(out_ap, c0, n):
                for k in range(8):
                    P.mm(out_ap, xT[:, k, :], win[:, k, c0:c0 + n], start=(k == 0), stop=(k == 7))

            def mm_feat(out_ap, c0, m):
                for k in range(8):
                    P.mm(out_ap, win[:, k, c0:c0 + m], xT[:, k, :], start=(k == 0), stop=(k == 7))

            bg_ = PS.alloc()
            mm_tok(ps[bg_][:, :], OFF_GG, 512)
            P.act(gate[:, 0:512], ps[bg_][:, :], AF.Silu)
            PS.release(bg_)
            bg_ = PS.alloc()
            mm_tok(ps[bg_][:, :], OFF_DG, 512)
            P.act(gate[:, 512:1024], ps[bg_][:, :], AF.Silu)
            PS.release(bg_)

            if sample:
                P.dma("sp", SCV, sconv_i)
                bh0, bh1 = PS.alloc(), PS.alloc()
                for cc in range(12):
                    bk = ps[bh0] if cc < 8 else ps[bh1]
                    P.tr(bk[:, (cc % 8) * 48:(cc % 8 + 1) * 48], SCV[:, cc * 128:(cc + 1) * 128], c_ident[0:48, 0:48])
                for cc in range(12):
                    bk = ps[bh0] if cc < 8 else ps[bh1]
                    P.copy("dve" if cc % 2 else "act",
                           cbuf[:, cc, :].rearrange("p (s w) -> p s w", w=11)[:, :, 0:3],
                           bk[:, (cc % 8) * 48:(cc % 8 + 1) * 48].rearrange("p (s r) -> p s r", r=3))
                PS.release(bh0)
                PS.release(bh1)
                for g3 in range(3):
                    bd = PS.alloc()
                    for j in range(4):
                        mm_feat(ps[bd][:, j * 128:(j + 1) * 128], OFF_DQKV + (g3 * 4 + j) * 128, 128)
                    for j in range(4):
                        cc = g3 * 4 + j
                        P.copy("act", cbuf[:, cc, :].rearrange("p (s w) -> p s w", w=11)[:, :, 3:11],
                               ps[bd][:, j * 128:(j + 1) * 128].rearrange("p (s t) -> p s t", t=8))
                    PS.release(bd)
                for cc in range(12):
                    def tap(i):
                        return cbuf[:, cc, :].rearrange("p (s w) -> p s w", w=11)[:, :, i:i + 8]
                    dst = cv[:, cc, :].rearrange("p (s t) -> p s t", t=8)
                    P.act(dst, tap(0), AF.Copy, scale=CWT[:, cc, 0:1])
                    for i in range(1, 4):
                        P.stt(dst, tap(i), CWT[:, cc, i:i + 1], dst, ALU.mult, ALU.add)
            elif not conv_hoisted.get(t):
                for _ in conv_gen():
                    pass
            P.act(cv.rearrange("p a b -> p (a b)"), cv.rearrange("p a b -> p (a b)"), AF.Silu)

            def run_il(gens):
                gens = list(gens)
                while gens:
                    for g in list(gens):
                        try:
                            next(g)
                        except StopIteration:
                            gens.remove(g)

            def gdn_gen():
                P.act(sqb.rearrange("p a b -> p (a b)"), cv[:, 0:8, :].rearrange("p a b -> p (a b)"), AF.Square)
                bn0, bn1 = PS.alloc(), PS.alloc()
                for cc in range(8):
                    bk = ps[bn0] if cc < 4 else ps[bn1]
                    P.mm(bk[:, (cc % 4) * 128:(cc % 4 + 1) * 128], b_ones, sqb[:, cc, :])
                P.act(rinv[:, 0:4, :].rearrange("p a b -> p (a b)"), ps[bn0][:, :], AF.Ln, bias=1e-6)
                P.act(rinv[:, 4:8, :].rearrange("p a b -> p (a b)"), ps[bn1][:, :], AF.Ln, bias=1e-6)
                PS.release(bn0)
                yield
                PS.release(bn1)
                yield
                P.act(rinv.rearrange("p a b -> p (a b)"), rinv.rearrange("p a b -> p (a b)"), AF.Exp, scale=-0.5)
                for h in range(4):
                    P.stt(R(qkn[:, h, 1, :]), cv[:, h, :], 128.0 ** -0.5, rinv[:, h, :], ALU.mult, ALU.mult)
                    P.tt("dve", R(qkn[:, h, 0, :]), cv[:, 4 + h, :], rinv[:, 4 + h, :], ALU.mult)
                btk, btv = PS.alloc(), PS.alloc()
                for h in range(4):
                    P.tr(ps[btk][:, h * 128:(h + 1) * 128], qkn[:, h, 0, :], c_ident)
                    P.tr(ps[btv][:, h * 128:(h + 1) * 128], cv[:, 8 + h, :], c_ident)
                yield "L2DONE"
                bgc = PS.alloc()
                P.mm(ps[bgc][:, 0:4], R(U), R(gg))
                P.copy("dve", gcs, ps[bgc][:, 0:4])
                P.ts("dve", ngc, ps[bgc][:, 0:4], -1.0, None, ALU.mult)
                P.act(egc, ps[bgc][:, 0:4], AF.Exp)
                for h in range(4):
                    P.mm(ps[bgc][:, 64 + h * 16:64 + h * 16 + nlast], R(gB[:, h, :]), R(Ulast))
                for h in range(4):
                    P.act(decs[:, h * 16:h * 16 + nseq], ps[bgc][:, 64 + h * 16 + nlast - nseq:64 + h * 16 + nlast], AF.Exp)
                if not sample:
                    P.copy("dve", gcl, ps[bgc][:, 65:129:16])
                PS.release(bgc)
                yield
                P.tt("dve", bgs, beta, egc, ALU.mult)
                if not sample:
                    for h in range(4):
                        P.act(egl[:, h:h + 1], gcs[:, h:h + 1], AF.Exp, scale=-1.0, bias=gcl[:, h:h + 1])
                    for h in range(4):
                        hs = slice(h * 128, (h + 1) * 128)
                        P.ts("dve", R(vb[:, hs]), ps[btv][:, hs], beta[:, h:h + 1], None, ALU.mult)
                        P.ts("dve", R(kb[:, hs]), ps[btk][:, hs], bgs[:, h:h + 1], None, ALU.mult)
                        P.act(R(kd[:, hs]), ps[btk][:, hs], AF.Copy, scale=egl[:, h:h + 1])
                    PS.release(btk)
                    PS.release(btv)
                    yield
                bkq0, bkq1 = PS.alloc(), PS.alloc()
                for h in range(4):
                    bk = ps[bkq0] if h < 2 else ps[bkq1]
                    P.mm(bk[:, (h % 2) * 256:(h % 2 + 1) * 256], R(qkn[:, h, 0, :]), R(qkn[:, h, :, :].rearrange("p a b -> p (a b)")))

                while not gla_done[0]:
                    yield

                def pair_gen(hp):
                    Nn, Cm, Mb, MTb, Pb, XdT, Yb, GT, G = bufsets[hp]
                    bpa, bpb = PS.alloc(), PS.alloc()
                    for i in range(2):
                        h = hp * 2 + i
                        pa = ps[bpa][:, i * 128:(i + 1) * 128]
                        pb = ps[bpb][:, i * 128:(i + 1) * 128]
                        P.mm(pa, R(gB[:, h, :]), R(U), start=True, stop=False)
                        P.mm(pa, b_ident, mbu, start=False, stop=True)
                        P.mm(pb, R(gB[:, h, :]), R(nU), start=True, stop=False)
                        P.mm(pb, b_ident, mbl, start=False, stop=True)
                    for i in range(2):
                        h = hp * 2 + i
                        P.act(GT[:, i, :], ps[bpa][:, i * 128:(i + 1) * 128], AF.Exp, bias=ngc[:, h:h + 1])
                        P.act(G[:, i, :], ps[bpb][:, i * 128:(i + 1) * 128], AF.Exp, bias=gcs[:, h:h + 1])
                    PS.release(bpa)
                    yield
                    PS.release(bpb)
                    yield
                    for i in range(2):
                        h = hp * 2 + i
                        kq = (ps[bkq0] if h < 2 else ps[bkq1])[:, (h % 2) * 256:(h % 2 + 1) * 256]
                        P.stt(R(Nn[:, i, :]), kq[:, 0:128], beta[:, h:h + 1], G[:, i, :], ALU.mult, ALU.mult)
                        P.tt("dve", R(QKm[:, h, :]), kq[:, 128:256], GT[:, i, :], ALU.mult)
                        if sample:
                            P.add("dve", lambda e, i=i, h=h: e.reduce_sum(egl[:, h:h + 1], GT[:, i, 7:128:8], mybir.AxisListType.X),
                                  [GT[:, i, 7:128:8]], [egl[:, h:h + 1]])
                        if not sample:
                            P.tt("dve", R(Cm[:, i, :]), Nn[:, i, :], c_blk64, ALU.mult)
                            P.tt("dve", R(Nn[:, i, :]), Nn[:, i, :], Cm[:, i, :], ALU.subtract)
                            nd = Cm[:, i, :]
                        else:
                            nd = Nn[:, i, :]
                        P.act(R(Mb[:, 0, i, :]), nd, AF.Copy, scale=-1.0)
                        P.stt(R(Pb[:, 0, i, :]), nd, -1.0, c_ident, ALU.mult, ALU.add)
                    bx = PS.alloc()
                    for i in range(2):
                        nd = Nn[:, i, :] if sample else Cm[:, i, :]
                        P.tr(ps[bx][:, i * 128:(i + 1) * 128], nd, c_ident)
                    P.act(R(MTb[:, 0, 0, :]), ps[bx][:, 0:128], AF.Copy, scale=-1.0)
                    P.ts("dve", R(MTb[:, 0, 1, :]), ps[bx][:, 128:256], -1.0, None, ALU.mult)
                    PS.release(bx)
                    yield
                    for k in range(1, nlev):
                        a, b = (k - 1) % 2, k % 2
                        bm = PS.alloc()
                        for i in range(2):
                            P.mm(ps[bm][:, i * 128:(i + 1) * 128], R(Mb[:, a, i, :]), R(MTb[:, a, i, :]))
                            if k < nlev - 1:
                                P.mm(ps[bm][:, 256 + i * 128:256 + (i + 1) * 128], R(MTb[:, a, i, :]), R(Mb[:, a, i, :]))
                        P.copy("act", R(MTb[:, b, :, :].rearrange("p a b -> p (a b)")), ps[bm][:, 0:256])
                        if k < nlev - 1:
                            P.copy("dve", R(Mb[:, b, :, :].rearrange("p a b -> p (a b)")), ps[bm][:, 256:512])
                        PS.release(bm)
                        yield
                        bp = PS.alloc()
                        for i in range(2):
                            P.mm(ps[bp][:, i * 128:(i + 1) * 128], R(MTb[:, b, i, :]), R(Pb[:, a, i, :]))
                        P.tt("dve", R(Pb[:, b, :, :].rearrange("p a b -> p (a b)")), ps[bp][:, 0:256],
                             Pb[:, a, :, :].rearrange("p a b -> p (a b)"), ALU.add)
                        PS.release(bp)
                        yield
                    fin = (nlev - 1) % 2
                    bx = PS.alloc()
                    for i in range(2):
                        P.tr(ps[bx][:, i * 128:(i + 1) * 128], Pb[:, fin, i, :], c_ident)
                    for i in range(2):
                        h = hp * 2 + i
                        dst = R(XT[:, h, :]) if sample else R(XdT[:, i, :])
                        P.copy("act", dst, ps[bx][:, i * 128:(i + 1) * 128])
                    PS.release(bx)
                    yield
                    if not sample:
                        by = PS.alloc()
                        for i in range(2):
                            P.mm(ps[by][:, i * 128:(i + 1) * 128], R(Nn[:, i, :]), R(XdT[:, i, :]))
                        for i in range(2):
                            P.copy("act", R(Yb[:, i, :]), ps[by][:, i * 128:(i + 1) * 128])
                        for i in range(2):
                            P.mm(ps[by][:, 256 + i * 128:256 + (i + 1) * 128], R(Pb[:, fin, i, :]), R(Yb[:, i, :]))
                        for i in range(2):
                            h = hp * 2 + i
                            P.tt("dve", R(XT[:, h, :]), XdT[:, i, :], ps[by][:, 256 + i * 128:256 + (i + 1) * 128], ALU.subtract)
                        PS.release(by)
                        yield

                gens_ = [pair_gen(0), pair_gen(1)]
                if (not sample) and tile_pos[t] + 1 < len(p1_order):
                    conv_hoisted[p1_order[tile_pos[t] + 1]] = True
                    gens_.append(conv_gen())
                run_il(gens_)
                PS.release(bkq0)
                yield
                PS.release(bkq1)
                yield
                if sample:
                    for h in range(4):
                        hs = slice(h * 128, (h + 1) * 128)
                        P.ts("dve", R(vb[:, hs]), ps[btv][:, hs], beta[:, h:h + 1], None, ALU.mult)
                        P.ts("dve", R(kb[:, hs]), ps[btk][:, hs], bgs[:, h:h + 1], None, ALU.mult)
                        P.act(R(kd[:, hs]), ps[btk][:, hs], AF.Copy, scale=egl[:, h:h + 1])
                    PS.release(btk)
                    yield
                    PS.release(btv)
                yield


            if sample or t == NT - 2:
                for g3 in range(3):
                    bd = PS.alloc()
                    mm_tok(ps[bd][:, :], OFF_DQKV + g3 * 512, 512)
                    P.copy("dve", tokd[:, g3 * 512:(g3 + 1) * 512], ps[bd][:, :])
                    PS.release(bd)
                if sample:
                    for s in range(16):
                        P.dma("sp", sconv_o[s], tokd[s * 8 + 5:s * 8 + 8, :])
                else:
                    P.dma("sp", pconv_o, tokd[125:128, :])

            bv = PS.alloc()
            mm_tok(ps[bv][:, :], OFF_GV, 512)
            P.copy("dve", R(vg), ps[bv][:, :])
            PS.release(bv)
            bqk = PS.alloc()
            for j in range(4):
                mm_feat(ps[bqk][:, j * 128:(j + 1) * 128], OFF_GQ + j * 128, 128)
            bga = PS.alloc()
            mm_feat(ps[bga][0:16, 0:128], OFF_GA, 16)
            P.copy("dve", gaT, ps[bga][0:16, 0:128])
            mm_tok(ps[bga][:, 128:136], OFF_DAB, 8)
            P.copy("dve", dab, ps[bga][:, 128:136])
            P.mm(ps[bga][:, 256:512], gaT, WGU, start=True, stop=False)
            P.mm(ps[bga][:, 256:512], c_ones[0:1, :], p_bgate, start=False, stop=True)
            if tile_pos[t] + 1 < len(p1_order):
                pass
            elif KP2 > 0:
                for k in range(8):
                    P.dma("pool", wg_sb[:, k, :], w_g[k * 128:(k + 1) * 128, :])
                for k in range(2):
                    P.dma("pool", wu_sb[:, k, :], w_u[k * 128:(k + 1) * 128, :])
                wg_loaded.append(1)

            gdn = gdn_gen()
            for _m in gdn:
                if _m == "L2DONE":
                    break

            if tile_pos[t] + 1 < len(p1_order):
                front(p1_order[tile_pos[t] + 1])

            P.act(R(lz), ps[bga][:, 256:512], AF.Exp, scale=-1.0)
            PS.release(bga)
            P.act(R(lz), lz, AF.Ln, bias=1.0)
            bb = PS.alloc()
            for c in range(2):
                P.mm(ps[bb][:, c * 128:(c + 1) * 128], R(lz[:, c * 128:(c + 1) * 128]), R(u16))
            P.act(eb, ps[bb][:, 0:256], AF.Exp)
            P.act(enb, ps[bb][:, 0:256], AF.Exp, scale=-1.0)
            PS.release(bb)
            P.stt(qtT, ps[bqk][:, 0:256], 0.125, eb, ALU.mult, ALU.mult)
            P.tt("dve", R(ktT), ps[bqk][:, 256:512], enb, ALU.mult)
            PS.release(bqk)
            for h in range(4):
                P.act(R(qtTm[:, h, :]), qtT[:, (h // 2) * 128:(h // 2 + 1) * 128], AF.Copy,
                      scale=c_blk64[:, (h % 2) * 64:(h % 2) * 64 + 1])
            P.tt("dve", tmp4, dab[:, 0:4], p_dtb, ALU.add)
            P.act(tmp4, tmp4, AF.Exp)
            P.act(tmp4, tmp4, AF.Ln, bias=1.0)
            P.tt("dve", R(gg), tmp4, NEGA, ALU.mult)
            P.act(beta, dab[:, 4:8], AF.Exp, scale=-1.0)
            P.ts("dve", beta, beta, 1.0, None, ALU.add)
            P.add("dve", lambda e: e.reciprocal(beta, beta), [beta], [beta])
            for h in range(4):
                P.act(R(gB[:, h, :]), c_ones, AF.Copy, scale=gg[:, h:h + 1])


            def gla_gen():
                bt = PS.alloc()
                for c in range(2):
                    P.tr(ps[bt][:, c * 128:(c + 1) * 128], ktT[:, c * 128:(c + 1) * 128], c_ident)
                P.copy("act", R(ktok), ps[bt][:, 0:256])
                PS.release(bt)
                yield
                bat = PS.alloc()
                for h in range(4):
                    c, r = h // 2, h % 2
                    rs = slice(r * 64, (r + 1) * 64)
                    P.mm(ps[bat][:, h * 128:(h + 1) * 128], R(ktT[:, c * 128:(c + 1) * 128]), R(qtTm[:, h, :]))
                for h in range(4):
                    P.tt("dve", R(atm[:, h * 128:(h + 1) * 128]), ps[bat][:, h * 128:(h + 1) * 128], m01, ALU.mult)
                PS.release(bat)
                yield
                bo = PS.alloc()
                for c in range(2):
                    if sample:
                        P.dma("sp", ST, sgla_i[:, 2 * c:2 * c + 2].rearrange("s r d v -> (r d) s v"))
                    for r in range(2):
                        h = 2 * c + r
                        rs = slice(r * 64, (r + 1) * 64)
                        oh = ps[bo][:, h * 128:(h + 1) * 128]
                        P.mm(oh, R(atm[:, h * 128:(h + 1) * 128]), R(vg[:, h * 128:(h + 1) * 128]), start=True, stop=False)
                        if not sample:
                            P.mm(oh, R(qtTm[:, h, :]), R(SGp[:, c, :]), start=False, stop=True)
                        else:
                            P.copy("dve", EXP.rearrange("p (s w) -> p s w", w=136)[:, :, 0:8],
                                   qtTm[:, h, :].rearrange("p (s t) -> p s t", t=8))
                            for s in range(16):
                                P.mm(oh, EXP[:, s * 128:(s + 1) * 128], ST[:, s, :], start=False, stop=(s == 15))
                    for r in range(2):
                        h = 2 * c + r
                        rs = slice(r * 64, (r + 1) * 64)
                        if not sample:
                            bs = PS.alloc()
                            P.mm(ps[bs][:, 0:128], R(ktok[:, c * 128:(c + 1) * 128]), R(vg[:, h * 128:(h + 1) * 128]))
                            ebl = eb[rs, c * 128 + 127:c * 128 + 128]
                            P.stt(R(SGp[rs, c, :]), ps[bs][rs, 0:128], 1.0, SGp[rs, c, :], ALU.mult, ALU.add)
                            P.ts("dve", R(SGp[rs, c, :]), SGp[rs, c, :], ebl, None, ALU.mult)
                            PS.release(bs)
                            yield
                        else:
                            for s in range(16):
                                if s % 2:
                                    P.ts("dve", VEX[:, s, :], vg[:, h * 128:(h + 1) * 128], c_bind[:, s:s + 1], None, ALU.mult)
                                else:
                                    P.act(VEX[:, s, :], vg[:, h * 128:(h + 1) * 128], AF.Copy, scale=c_bind[:, s:s + 1])
                            for sg in range(4):
                                bs = PS.alloc()
                                P.mm(ps[bs][:, :], ktok[:, c * 128:(c + 1) * 128],
                                     VEX[:, sg * 4:(sg + 1) * 4, :].rearrange("p a b -> p (a b)"))
                                for s4 in range(4):
                                    s = sg * 4 + s4
                                    ebl = eb[rs, c * 128 + 8 * s + 7:c * 128 + 8 * s + 8]
                                    P.stt(ST[rs, s, :], ps[bs][rs, s4 * 128:(s4 + 1) * 128], 1.0, ST[rs, s, :], ALU.mult, ALU.add)
                                    P.ts("dve", ST[rs, s, :], ST[rs, s, :], ebl, None, ALU.mult)
                                PS.release(bs)
                                yield
                    if sample:
                        P.dma("sp", sgla_o[:, 2 * c:2 * c + 2].rearrange("s r d v -> (r d) s v"), ST)
                for h in range(4):
                    P.act(junk, ps[bo][:, h * 128:(h + 1) * 128], AF.Square, accum_out=ssq[:, h:h + 1])
                P.ts("dve", rstd[:, 0:4], ssq[:, 0:4], 1.0 / 128.0, 1e-5, ALU.mult, ALU.add)
                P.tt("pool", rstd[:, 0:4], rstd[:, 0:4], c_nhalf[:, 0:4], ALU.pow)
                for h in range(4):
                    P.stt(og[:, h * 128:(h + 1) * 128], ps[bo][:, h * 128:(h + 1) * 128], rstd[:, h:h + 1],
                          gate[:, h * 128:(h + 1) * 128], ALU.mult, ALU.mult)
                PS.release(bo)
                yield


            gla_done = [False]

            def gla_wrapped():
                yield from gla_gen()
                gla_done[0] = True

            run_il([gla_wrapped(), gdn])

            def head_gen(h):
                i = h
                hs = slice(h * 128, (h + 1) * 128)
                STh = (ST if h % 2 == 0 else ST2) if sample else None
                if sample:
                    P.dma("sp", STh, sgdn_i[:, h].rearrange("s d v -> d s v"))

                def S_of(s):
                    return STh[:, s, :] if sample else SDp[:, h, :]

                def state_terms(out_ap, srcT, first):
                    if not sample:
                        P.mm(out_ap, R(srcT), R(S_of(0)), start=first, stop=True)
                    else:
                        P.copy("dve", EXP.rearrange("p (s w) -> p s w", w=136)[:, :, 0:8],
                               srcT.rearrange("p (s t) -> p s t", t=8))
                        for s in range(16):
                            P.mm(out_ap, EXP[:, s * 128:(s + 1) * 128], S_of(s), start=(first and s == 0), stop=(s == 15))

                bw = PS.alloc()
                P.mm(ps[bw][:, 0:128], R(kb[:, hs]), R(XT[:, h, :]))
                P.act(R(nwT4[:, i, :]), ps[bw][:, 0:128], AF.Copy, scale=-1.0)
                yield
                P.mm(ps[bw][:, 128:256], R(XT[:, h, :]), R(vb[:, hs]), start=True, stop=False)
                state_terms(ps[bw][:, 128:256], nwT4[:, i, :], False)
                P.copy("act", R(delta4[:, i, :]), ps[bw][:, 128:256])
                yield
                state_terms(ps[bw][:, 256:384], qkn[:, h, 1, :], True)
                P.mm(ps[bw][:, 384:512], R(QKm[:, h, :]), R(delta4[:, i, :]))
                P.act(osb4[:, i, :], ps[bw][:, 256:384], AF.Copy, scale=egc[:, h:h + 1])
                P.tt("dve", osb4[:, i, :], osb4[:, i, :], ps[bw][:, 384:512], ALU.add)
                yield
                PS.release(bw)
                yield
                P.act(junk, osb4[:, i, :], AF.Square, accum_out=ssq[:, 4 + h:5 + h])
                P.ts("dve", rstd[:, 4 + h:5 + h], ssq[:, 4 + h:5 + h], 1.0 / 128.0, 1e-5, ALU.mult, ALU.add)
                P.tt("pool", rstd[:, 4 + h:5 + h], rstd[:, 4 + h:5 + h], c_nhalf[:, 0:1], ALU.pow)
                P.stt(og[:, 512 + h * 128:512 + (h + 1) * 128], osb4[:, i, :], rstd[:, 4 + h:5 + h],
                      gate[:, 512 + h * 128:512 + (h + 1) * 128], ALU.mult, ALU.mult)
                yield
                if not sample:
                    bs = PS.alloc()
                    P.mm(ps[bs][:, 0:128], R(kd[:, hs]), R(delta4[:, i, :]))
                    P.stt(R(SDp[:, h, :]), SDp[:, h, :], decs[:, h * 16:h * 16 + 1], ps[bs][:, 0:128], ALU.mult, ALU.add)
                    PS.release(bs)
                    yield
                else:
                    for s in range(16):
                        if s % 2:
                            P.ts("dve", VEX[:, s, :], delta4[:, i, :], c_bind[:, s:s + 1], None, ALU.mult)
                        else:
                            P.act(VEX[:, s, :], delta4[:, i, :], AF.Copy, scale=c_bind[:, s:s + 1])
                    for sg in range(4):
                        bs = PS.alloc()
                        P.mm(ps[bs][:, :], kd[:, hs], VEX[:, sg * 4:(sg + 1) * 4, :].rearrange("p a b -> p (a b)"))
                        for s4 in range(4):
                            s = sg * 4 + s4
                            P.stt(STh[:, s, :], STh[:, s, :], decs[:, h * 16 + s:h * 16 + s + 1],
                                  ps[bs][:, s4 * 128:(s4 + 1) * 128], ALU.mult, ALU.add)
                        PS.release(bs)
                    P.dma("sp", sgdn_o[:, h].rearrange("s d v -> d s v"), STh)


            if sample:
                run_il([head_gen(0), head_gen(1)])
                run_il([head_gen(2), head_gen(3)])
            else:
                run_il([head_gen(0), head_gen(1), head_gen(2), head_gen(3)])

            bto = PS.alloc()
            pb16 = ps[bto].bitcast(BF16)
            for e8 in range(8):
                P.tr(pb16[:, e8 * 128:(e8 + 1) * 128], og[:, e8 * 128:(e8 + 1) * 128], b_ident)
            P.act(oT[:, 0:4, :].rearrange("p a b -> p (a b)"), pb16[:, 0:512], AF.Copy, scale=GNC[:, 0:1])
            P.ts("dve", oT[:, 4:8, :].rearrange("p a b -> p (a b)"), pb16[:, 512:1024], GNC[:, 1:2], None, ALU.mult)
            PS.release(bto)
            bm0, bm1 = PS.alloc(), PS.alloc()
            for half, bk in ((0, bm0), (1, bm1)):
                for e8 in range(8):
                    P.mm(ps[bk][:, :], oT[:, e8, :], wout[:, e8, half * 512:(half + 1) * 512], start=(e8 == 0), stop=(e8 == 7))
            for half, bk in ((0, bm0), (1, bm1)):
                P.stt(rr[:, half * 512:(half + 1) * 512], xt[:, half * 512:(half + 1) * 512], ALPHA, ps[bk][:, :], ALU.mult, ALU.add)
            PS.release(bm0)
            PS.release(bm1)
            if "rr0" in dbg and t == 0:
                P.dma("sp", dbg["rr0"], rr)
                P.dma("sp", dbg["sm"], sm)
                P.dma("sp", dbg["gate"], gate)
                P.dma("sp", dbg["xt"], xt)
            layer_norm(rr, p_lng, p_lnb)
            if "x1" in dbg and t == 0:
                P.dma("sp", dbg["x1"], rr)
            P.dma("sp", x1s[t], rr)

        def layer_norm(buf, gB_, bB_):
            for half in range(2):
                P.add("dve", lambda e, half=half: e.bn_stats(stats[:, half * 6:(half + 1) * 6], buf[:, half * 512:(half + 1) * 512]),
                      [buf[:, half * 512:(half + 1) * 512]], [stats[:, half * 6:(half + 1) * 6]])
            P.add("dve", lambda e: e.bn_aggr(stats[:, 12:14], stats[:, 0:12]), [stats[:, 0:12]], [stats[:, 12:14]])
            P.ts("dve", stats[:, 14:15], stats[:, 13:14], 1e-5, None, ALU.add)
            P.tt("pool", stats[:, 14:15], stats[:, 14:15], nhalfP[:, 0:1], ALU.pow)
            P.ts("dve", buf, buf, stats[:, 12:13], stats[:, 14:15], ALU.subtract, ALU.mult)
            P.tt("dve", buf, buf, gB_, ALU.mult)
            P.tt("dve", buf, buf, bB_, ALU.add)

        dbg_taps = {}
        import os as _os
        order = [NT - 1] + list(range(NT - 1))
        if _os.environ.get("KORDER") == "p":
            order = list(range(NT))
        import os
        KP1 = int(os.environ.get("KP1", "17"))
        KP2 = int(os.environ.get("KP2", "17"))
        wg_loaded = []
        p1_order = order[:KP1]
        tile_pos = {t: i for i, t in enumerate(p1_order)}
        if p1_order:
            front(p1_order[0])
        for t in p1_order:
            phase1_tile(t)
        P.dma("sp", pgla_o.rearrange("(c r) d v -> (r d) c v", r=2), SGp)
        P.dma("sp", pgdn_o.rearrange("h d v -> d h v"), SDp)

        def a2(shape, dt=F32, parts=128):
            return A.alloc("p2", shape, dt, parts)

        x2b = a2([2, D])
        x2T = a2([8, 128], BF16)
        hidT = a2([NFC, 128], BF16)
        hidg = a2([2, 512], BF16)
        if not wg_loaded:
            for k in range(8):
                P.dma("pool", wg_sb[:, k, :], w_g[k * 128:(k + 1) * 128, :])
        for k in range(2 if wg_loaded else 0, 8):
            P.dma("pool", wu_sb[:, k, :], w_u[k * 128:(k + 1) * 128, :])
        for f in range(NFC):
            P.dma("pool", wd_sb[:, f, :], w_d[f * 128:(f + 1) * 128, :])
        P.dma("sp", PRM[:, 256:2304], prm2_i)

        def front2(t):
            x2 = x2b[:, pos2[t] % 2, :]
            P.dma("sp", x2, x1s[t])
            b0, b1 = PS.alloc(), PS.alloc()
            for k in range(8):
                bk = ps[b0] if k < 4 else ps[b1]
                P.tr(bk[:, (k % 4) * 128:(k % 4 + 1) * 128], x2[:, k * 128:(k + 1) * 128], identP)
            P.act(x2T[:, 0:4, :].rearrange("p a b -> p (a b)"), ps[b0][:, :], AF.Copy)
            P.copy("dve", x2T[:, 4:8, :].rearrange("p a b -> p (a b)"), ps[b1][:, :])
            PS.release(b0)
            PS.release(b1)

        def phase2_tile(t):
            x2 = x2b[:, pos2[t] % 2, :]
            def hid_transposes(gi, g0, n):
                hg = hidg[:, gi % 2, :]
                bt_ = PS.alloc()
                pb16 = ps[bt_].bitcast(BF16)
                nf = n // 128
                for j in range(nf):
                    P.tr(pb16[:, j * 128:(j + 1) * 128], hg[:, j * 128:(j + 1) * 128], b_ident)
                f0 = g0 // 128
                if gi % 2 == 0:
                    P.act(hidT[:, f0:f0 + nf, :].rearrange("p a b -> p (a b)"), pb16[:, 0:nf * 128], AF.Copy)
                else:
                    P.copy("dve", hidT[:, f0:f0 + nf, :].rearrange("p a b -> p (a b)"), pb16[:, 0:nf * 128])
                PS.release(bt_)

            pending = None
            for gi, g0 in enumerate(range(0, DFF, 512)):
                n = min(512, DFF - g0)
                hg = hidg[:, gi % 2, :]
                bgt, bup = PS.alloc(), PS.alloc()
                for k in range(8):
                    P.mm(ps[bgt][:, 0:n], x2T[:, k, :], wg_sb[:, k, g0:g0 + n], start=(k == 0), stop=(k == 7))
                for k in range(8):
                    P.mm(ps[bup][:, 0:n], x2T[:, k, :], wu_sb[:, k, g0:g0 + n], start=(k == 0), stop=(k == 7))
                P.act(hg[:, 0:n], ps[bgt][:, 0:n], AF.Silu)
                P.tt("dve", hg[:, 0:n], hg[:, 0:n], ps[bup][:, 0:n], ALU.mult)
                PS.release(bgt)
                PS.release(bup)
                if pending is not None:
                    hid_transposes(*pending)
                pending = (gi, g0, n)
            if pos2[t] + 1 < len(p2_order):
                front2(p2_order[pos2[t] + 1])
            hid_transposes(*pending)
            bm0, bm1 = PS.alloc(), PS.alloc()
            for half, bk in ((0, bm0), (1, bm1)):
                for f in range(NFC):
                    P.mm(ps[bk][:, :], hidT[:, f, :], wd_sb[:, f, half * 512:(half + 1) * 512], start=(f == 0), stop=(f == NFC - 1))
            for half, bk in ((0, bm0), (1, bm1)):
                P.stt(x2[:, half * 512:(half + 1) * 512], x2[:, half * 512:(half + 1) * 512], ALPHA, ps[bk][:, :], ALU.mult, ALU.add)
            PS.release(bm0)
            PS.release(bm1)
            layer_norm(x2, p_lng, p_lnb)
            P.dma("sp", y_o[t], x2)

        p2_order = order[:KP2]
        pos2 = {t: i for i, t in enumerate(p2_order)}
        if p2_order:
            front2(p2_order[0])
        for t in p2_order:
            phase2_tile(t)
        P.barrier("sp", [y_o, pgla_o, pgdn_o, pconv_o, sgla_o, sgdn_o, sconv_o])
        st = P.emit(es)
        print("ops", len(P.ops), "per-engine (instr, waits):", st, "p1_end", p1_end, "top", A.top)
    return nc


def _consts():
    i = np.arange(128)
    same8 = (i[:, None] // 8) == (i[None, :] // 8)
    up = i[:, None] <= i[None, :]
    lowstrict = i[:, None] > i[None, :]
    Uf = up.astype(np.float32)
    Us = (up & same8).astype(np.float32)
    mats = [np.eye(128, dtype=np.float32), Uf, Us, -Uf, -Us, -Uf / 16.0, -Us / 16.0,
            np.where(up, 0.0, NEG), np.where(up & same8, 0.0, NEG),
            np.where(lowstrict, 0.0, NEG), np.where(lowstrict & same8, 0.0, NEG),
            ((i[:, None] // 64) == (i[None, :] // 64)).astype(np.float32),
            np.ones((128, 128), np.float32), np.full((128, 128), -0.5, np.float32)]
    bind = ((i[:, None] // 8) == np.arange(16)[None, :]).astype(np.float32)
    return np.ascontiguousarray(np.concatenate([m.astype(np.float32) for m in mats] + [bind], axis=1))


_NC_CACHE = {}


def kernel(x_prompt, x_sample, state_gla, state_gdn, state_gdn_conv, w_in, gla_w_gate_up,
           gla_b_gate, gla_norm_g, gdn_conv_w, gdn_a_log, gdn_dt_bias, gdn_norm_g, w_out,
           ln1_g, ln1_b, w_ffn_gate, w_ffn_up, w_ffn_down, ln2_g, ln2_b):
    f = lambda a: np.ascontiguousarray(np.asarray(a, dtype=np.float32))
    x_prompt, x_sample = f(x_prompt), f(x_sample)
    state_gla, state_gdn, state_gdn_conv = f(state_gla), f(state_gdn), f(state_gdn_conv)
    if "nc" not in _NC_CACHE:
        _NC_CACHE["nc"] = build_program()
    nc = _NC_CACHE["nc"]
    rep = lambda v: np.broadcast_to(f(v).reshape(1, -1), (128, f(v).size))
    prm = np.ascontiguousarray(np.concatenate([rep(gla_norm_g), rep(gdn_norm_g), rep(ln1_g), rep(ln1_b),
                                               rep(gdn_a_log), rep(gdn_dt_bias), rep(gla_b_gate)], axis=1))
    prm2 = np.ascontiguousarray(np.concatenate([rep(ln2_g), rep(ln2_b)], axis=1))
    cw = f(gdn_conv_w)[0]
    cwT = np.ascontiguousarray(cw.reshape(4, 12, 128).transpose(2, 1, 0).reshape(128, 48))
    gnc = np.ascontiguousarray(np.stack([f(gla_norm_g).reshape(128), f(gdn_norm_g).reshape(128)], axis=1))
    common = {"gnc": gnc, "w_in": f(w_in)[0], "w_out": f(w_out)[0], "w_g": f(w_ffn_gate)[0], "w_u": f(w_ffn_up)[0],
              "w_d": f(w_ffn_down)[0], "wgu": f(gla_w_gate_up)[0], "cwT": cwT, "consts": _consts(),
              "prm": prm, "prm2": prm2}
    in_maps = []
    for c in range(8):
        xs = np.concatenate([x_prompt[c].reshape(16, 128, D), x_sample[16 * c:16 * c + 16].reshape(1, 128, D)], axis=0)
        m = dict(common)
        m["xs"] = np.ascontiguousarray(xs)
        m["sgla"] = np.ascontiguousarray(state_gla[0, 16 * c:16 * c + 16])
        m["sgdn"] = np.ascontiguousarray(state_gdn[0, 16 * c:16 * c + 16])
        m["sconv"] = np.ascontiguousarray(state_gdn_conv[0, 16 * c:16 * c + 16].reshape(48, 1536))
        in_maps.append(m)
    res = run_bass_kernel_spmd(nc, in_maps, core_ids=list(range(8)))
    R = res.results
    _NC_CACHE["last"] = R
    yp = np.stack([R[c]["y"][0:16].reshape(2048, D) for c in range(8)])
    ys = np.concatenate([R[c]["y"][16].reshape(16, 8, D) for c in range(8)], axis=0)
    pgla = np.stack([R[c]["pgla"] for c in range(8)])[None]
    pgdn = np.stack([R[c]["pgdn"] for c in range(8)])[None]
    pconv = np.stack([R[c]["pconv"] for c in range(8)])[None]
    sgla = np.concatenate([R[c]["sgla_o"] for c in range(8)], axis=0)[None]
    sgdn = np.concatenate([R[c]["sgdn_o"] for c in range(8)], axis=0)[None]
    sconv = np.concatenate([R[c]["sconv_o"] for c in range(8)], axis=0)[None]
    return tuple(np.ascontiguousarray(a, dtype=np.float32) for a in (yp, ys, pgla, pgdn, pconv, sgla, sgdn, sconv))
```

```python
import contextlib
from math import prod
import numpy as np
import concourse.bass as bass
import concourse.mybir as mybir
from concourse.bass_utils import run_bass_kernel_spmd

F32 = mybir.dt.float32
BF16 = mybir.dt.bfloat16
F32R = mybir.dt.float32r


def R(ap):
    return ap.bitcast(F32R)
AF = mybir.ActivationFunctionType
ALU = mybir.AluOpType

N_DMA_SLOTS = 94
_ESZ = {str(F32): 4, str(BF16): 2}

D = 1024
NT = 17
IN_DIM = 3608
DFF = 2816
NFC = DFF // 128
OFF_GQ, OFF_GK, OFF_GV, OFF_GG, OFF_GA, OFF_DQKV, OFF_DG, OFF_DAB = 0, 256, 512, 1024, 1536, 1552, 3088, 3600
ALPHA = 2.0 ** 0.25
NEG = -30000.0
import os as _os0
DEBUG = {"rr0": [128, 1024], "sm": [128, 64], "gate": [128, 1024], "xt": [128, 1024], "x1": [128, 1024], "x2": [128, 1024], "r2": [128, 1024], "st2": [128, 16]} if _os0.environ.get("KDBG") else {}


_ALIAS = {}


def _box(ap):
    b = _box0(ap)
    al = _ALIAS.get(b[0])
    if al is not None:
        return (al[0], b[1], b[2], b[3] + al[1], b[4] + al[1])
    return b


def _box0(ap):
    name = ap.tensor.name
    pat = ap.ap
    off = int(ap.offset)
    es = _ESZ.get(str(ap.dtype), 4)
    sp = str(ap.space)
    if sp in ("SB", "PSUM"):
        pstep, pcnt = pat[0]
        if pstep > 0:
            p0 = off // pstep
            f0 = off - p0 * pstep
        else:
            p0, f0 = 0, off
        ext = 1
        for st, cn in pat[1:]:
            ext += abs(st) * (cn - 1)
        return (name, p0, p0 + pcnt, f0 * es, (f0 + ext) * es)
    ext = 1
    for st, cn in pat:
        ext += abs(st) * (cn - 1)
    return (name, 0, 1, off * es, (off + ext) * es)


def _overlap(a, b):
    return a[1] < b[2] and b[1] < a[2] and a[3] < b[4] and b[3] < a[4]


def _covers(a, b):
    return a[1] <= b[1] and a[2] >= b[2] and a[3] <= b[3] and a[4] >= b[4]


class Op:
    __slots__ = ("idx", "eng", "fn", "dma", "deps", "signal", "semval", "slot", "slot_prev")

    def __init__(self, idx, eng, fn, dma):
        self.idx, self.eng, self.fn, self.dma = idx, eng, fn, dma
        self.deps = set()
        self.signal = False
        self.semval = None
        self.slot = None
        self.slot_prev = 0


class Prog:
    ENGS = ("pe", "act", "dve", "pool", "sp")

    def __init__(self, nc, untracked=()):
        self.nc = nc
        self.ops = []
        self.wr = {}
        self.rd = {}
        self.untracked = set(untracked)

    def add(self, eng, fn, reads=(), writes=(), dma=False):
        op = Op(len(self.ops), eng, fn, dma)
        self.ops.append(op)
        for ap in reads:
            b = _box(ap)
            if b[0] in self.untracked:
                continue
            for (wb, wi) in self.wr.get(b[0], ()):
                if _overlap(wb, b):
                    op.deps.add(wi)
            self.rd.setdefault(b[0], []).append((b, op.idx))
        for ap in writes:
            b = _box(ap)
            if b[0] in self.untracked:
                continue
            wl = self.wr.get(b[0], [])
            for (wb, wi) in wl:
                if _overlap(wb, b):
                    w = self.ops[wi]
                    if w.dma or op.dma or w.eng != op.eng:
                        op.deps.add(wi)
            rl = self.rd.get(b[0], [])
            for (rb, ri) in rl:
                if ri != op.idx and _overlap(rb, b):
                    r = self.ops[ri]
                    if r.dma or op.dma or r.eng != op.eng:
                        op.deps.add(ri)
            self.wr[b[0]] = [(wb, wi) for (wb, wi) in wl if not _covers(b, wb)] + [(b, op.idx)]
            self.rd[b[0]] = [(rb, ri) for (rb, ri) in rl if (ri == op.idx) or not _covers(b, rb)]
        for ap in list(reads) + list(writes):
            if str(ap.space) == "PSUM":
                key = "LOCK_" + ap.tensor.name
                last = self.wr.get(key)
                if last is not None and last != op.idx and self.ops[last].eng != op.eng:
                    op.deps.add(last)
                self.wr[key] = op.idx
        return op

    def dma(self, q, out, in_, **kw):
        return self.add(q, lambda e: e.dma_start(out=out, in_=in_, **kw), [in_], [out], dma=True)

    def mm(self, out, lhsT, rhs, start=True, stop=True):
        return self.add("pe", lambda e: e.matmul(out, lhsT, rhs, start=start, stop=stop), [lhsT, rhs], [out])

    def tr(self, out, in_, ident):
        return self.add("pe", lambda e: e.transpose(out, in_, ident), [in_, ident], [out])

    def act(self, out, in_, func, bias=None, scale=None, accum_out=None):
        reads = [in_]
        kw = {}
        if bias is not None:
            kw["bias"] = bias
            if not isinstance(bias, (int, float)):
                reads.append(bias)
        if scale is not None:
            kw["scale"] = scale
            if not isinstance(scale, (int, float)):
                reads.append(scale)
        writes = [out]
        if accum_out is not None:
            kw["accum_out"] = accum_out
            writes.append(accum_out)
        return self.add("act", lambda e: e.activation(out, in_, func, **kw), reads, writes)

    def tt(self, eng, out, in0, in1, op):
        return self.add(eng, lambda e: e.tensor_tensor(out, in0, in1, op), [in0, in1], [out])

    def ts(self, eng, out, in0, s1, s2, op0, op1=None):
        reads = [in0]
        for s in (s1, s2):
            if s is not None and not isinstance(s, (int, float)):
                reads.append(s)
        if op1 is None:
            return self.add(eng, lambda e: e.tensor_scalar(out, in0, s1, None, op0), reads, [out])
        return self.add(eng, lambda e: e.tensor_scalar(out, in0, s1, s2, op0, op1), reads, [out])

    def stt(self, out, in0, scalar, in1, op0, op1):
        reads = [in0, in1]
        if not isinstance(scalar, (int, float)):
            reads.append(scalar)
        return self.add("dve", lambda e: e.scalar_tensor_tensor(out, in0, scalar, in1, op0, op1), reads, [out])

    def copy(self, eng, out, in_):
        if eng == "act":
            return self.add("act", lambda e: e.copy(out, in_), [in_], [out])
        return self.add(eng, lambda e: e.tensor_copy(out, in_), [in_], [out])

    def memset(self, eng, out, val):
        return self.add(eng, lambda e: e.memset(out, val), [], [out])

    def barrier(self, eng, reads):
        return self.add(eng, None, reads, [])

    def emit(self, es):
        nc = self.nc
        import os
        ops = self.ops[:int(os.environ.get("KSTOP", "100000000"))]
        for op in ops:
            for d in op.deps:
                ops[d].signal = True
        cnt = {e: 0 for e in self.ENGS}
        slot_uses = [0] * N_DMA_SLOTS
        half = N_DMA_SLOTS // 2
        nd = {True: 0, False: 0}
        for op in ops:
            if op.dma:
                sw = (op.eng == "pool")
                op.slot = (nd[sw] % half) + (half if sw else 0)
                nd[sw] += 1
                op.slot_prev = slot_uses[op.slot] * 16
                slot_uses[op.slot] += 1
                op.semval = slot_uses[op.slot] * 16
            elif op.signal:
                cnt[op.eng] += 1
                op.semval = cnt[op.eng]
        esem = {e: es.enter_context(nc.semaphore("s_" + e)) for e in self.ENGS}
        dsem = [es.enter_context(nc.semaphore("d%d" % i)) for i in range(N_DMA_SLOTS)]
        block = es.enter_context(nc.Block())
        stats = {}

        def gen(E):
            def body(e):
                waited = {}
                nw = ni = 0
                for op in ops:
                    if op.eng != E:
                        continue
                    need = {}
                    for d in op.deps:
                        dop = ops[d]
                        s = ("d", dop.slot) if dop.dma else ("e", dop.eng)
                        if need.get(s, 0) < dop.semval:
                            need[s] = dop.semval
                    if op.dma and op.slot_prev > 0:
                        s = ("d", op.slot)
                        if need.get(s, 0) < op.slot_prev:
                            need[s] = op.slot_prev
                    for s, v in need.items():
                        if waited.get(s, 0) >= v:
                            continue
                        waited[s] = v
                        e.wait_ge(dsem[s[1]] if s[0] == "d" else esem[s[1]], v)
                        nw += 1
                    if op.fn is None:
                        continue
                    ins = op.fn(e)
                    ni += 1
                    if op.dma:
                        ins.then_inc(dsem[op.slot], 16)
                    elif op.signal:
                        ins.then_inc(esem[E], 1)
                stats[E] = (ni, nw)
            return body

        block.tensor(gen("pe"))
        block.scalar(gen("act"))
        block.vector(gen("dve"))
        block.gpsimd(gen("pool"))
        block.sync(gen("sp"))
        return stats


class Arena:
    def __init__(self, nc, es, nbytes):
        self.nbytes = nbytes
        self.h = {BF16: es.enter_context(nc.sbuf_tensor("A", [128, nbytes // 2], BF16))}
        self.h[F32] = self.h[BF16].bitcast(F32)
        self.top = nbytes
        self.cur = {}

    def view(self, off, shape, dt, parts=128):
        es = 4 if dt == F32 else 2
        assert off % 4 == 0
        n = prod(shape)
        ap = self.h[dt][0:parts, off // es: off // es + n]
        if len(shape) > 1:
            names = ["a%d" % i for i in range(len(shape))]
            kw = {nm: s for nm, s in zip(names[1:], shape[1:])}
            ap = ap.rearrange("p (%s) -> p %s" % (" ".join(names), " ".join(names)), **kw)
        return ap

    def persist(self, shape, dt, parts=128):
        nb = (prod(shape) * (4 if dt == F32 else 2) + 31) // 32 * 32
        self.top -= nb
        return self.view(self.top, shape, dt, parts)

    def alloc(self, region, shape, dt, parts=128):
        nb = (prod(shape) * (4 if dt == F32 else 2) + 31) // 32 * 32
        off = self.cur[region]
        self.cur[region] = off + nb
        assert self.cur[region] <= self.top, (region, self.cur[region], self.top)
        return self.view(off, shape, dt, parts)


class PsumPool:
    def __init__(self, nc, es):
        self.banks = [es.enter_context(nc.psum_tensor("ps%d" % i, [128, 512], F32)) for i in range(8)]
        self.free = list(range(8))

    def alloc(self):
        assert self.free, "out of PSUM banks"
        return self.free.pop(0)

    def release(self, b):
        self.free.append(b)


def build_program():
    nc = bass.Bass("TRN2", target_bir_lowering=False)

    def din(name, shape):
        return nc.dram_tensor(name, list(shape), F32, kind="ExternalInput").ap()

    def dout(name, shape):
        return nc.dram_tensor(name, list(shape), F32, kind="ExternalOutput").ap()

    xs = din("xs", [NT, 128, D])
    sgla_i = din("sgla", [16, 4, 64, 128])
    sgdn_i = din("sgdn", [16, 4, 128, 128])
    sconv_i = din("sconv", [48, 1536])
    w_in = din("w_in", [D, IN_DIM])
    w_out = din("w_out", [D, D])
    w_g = din("w_g", [D, DFF])
    w_u = din("w_u", [D, DFF])
    w_d = din("w_d", [DFF, D])
    wgu_i = din("wgu", [16, 256])
    cwT_i = din("cwT", [128, 48])
    NCONST = 128 * 14 + 16
    consts_i = din("consts", [128, NCONST])
    prm_i = din("prm", [128, 2568])
    prm2_i = din("prm2", [128, 2048])
    gnc_i = din("gnc", [128, 2])

    y_o = dout("y", [NT, 128, D])
    pgla_o = dout("pgla", [4, 64, 128])
    pgdn_o = dout("pgdn", [4, 128, 128])
    pconv_o = dout("pconv", [3, 1536])
    sgla_o = dout("sgla_o", [16, 4, 64, 128])
    sgdn_o = dout("sgdn_o", [16, 4, 128, 128])
    sconv_o = dout("sconv_o", [16, 3, 1536])
    x1s = nc.dram_tensor("x1s", [NT, 128, D], F32, kind="Internal").ap()
    dbg = {}
    for k, shp in DEBUG.items():
        dbg[k] = dout("dbg_" + k, shp)

    es = contextlib.ExitStack()
    with es:
        AR_SZ = 33856 + 10240
        A = Arena(nc, es, 212800 - AR_SZ)
        arh = es.enter_context(nc.sbuf_tensor("AR", [128, AR_SZ // 4], F32))
        ar_cur = [0]

        def ar1(shape):
            n = prod(shape)
            ap = arh[:, ar_cur[0]:ar_cur[0] + n]
            ar_cur[0] += (n + 7) // 8 * 8
            assert ar_cur[0] * 4 <= AR_SZ, ar_cur[0] * 4
            if len(shape) > 1:
                names = ["a%d" % i for i in range(len(shape))]
                kw = {nm: sz for nm, sz in zip(names[1:], shape[1:])}
                ap = ap.rearrange("p (%s) -> p %s" % (" ".join(names), " ".join(names)), **kw)
            return ap

        PS = PsumPool(nc, es)
        ps = PS.banks
        P = Prog(nc, untracked=["xs", "sgla", "sgdn", "sconv", "w_in", "w_out", "w_g", "w_u", "w_d",
                                "wgu", "cwT", "consts", "prm", "prm2", "gnc"])

        A.cur["p1"] = 74112
        A.cur["p2"] = 135168
        CT = A.alloc("p1", [NCONST], F32)
        identP = A.persist([128], F32)
        nhalfP = A.persist([8], F32)
        (c_ident, c_Uf, c_Us, c_nUf, c_nUs, c_u16f, c_u16s,
         c_mbuf, c_mbus, c_mblf, c_mbls, c_blk64, c_ones, c_nhalf) = [CT[:, i * 128:(i + 1) * 128] for i in range(14)]
        c_bind = CT[:, 1792:1808]
        CB = A.persist([6 * 128], BF16)
        b_ident, b_mbuf, b_mbus, b_mblf, b_mbls, b_ones = [CB[:, i * 128:(i + 1) * 128] for i in range(6)]
        PRM = A.persist([2568], F32)
        p_gng, p_gnd = PRM[:, 0:128], PRM[:, 128:256]
        p_lng, p_lnb = PRM[:, 256:1280], PRM[:, 1280:2304]
        p_alog, p_dtb, p_bgate = PRM[:, 2304:2308], PRM[:, 2308:2312], PRM[0:1, 2312:2568]
        WGU = A.persist([256], F32, parts=16)
        CWT = A.persist([12, 4], F32)
        NEGA = A.persist([4], F32)
        GNC = A.persist([2], F32)
        SGp = ar1([2, 128])
        SDp = ar1([4, 128])
        rU = {}
        for nm_ in ("Uf", "Us", "nUf", "nUs", "u16f", "u16s"):
            rU[nm_] = ar1([128])
        HALO = A.persist([12, 3], F32)
        stats = A.persist([16], F32)

        win = A.view(0, [8, IN_DIM], BF16)
        wout = A.view(57728, [8, D], BF16)
        wg_sb = A.view(0, [8, DFF], BF16)
        wu_sb = A.view(45056, [8, DFF], BF16)
        wd_sb = A.view(90112, [NFC, D], BF16)

        def a1(shape, dt=F32, parts=128):
            return A.alloc("p1", shape, dt, parts)

        xtb = a1([2, D])
        xT = a1([8, 128], BF16)
        gaT = a1([128], F32, parts=16)
        lz = ar1([256])
        eb = a1([256])
        enb = a1([256])
        qtT = a1([256])
        ktT = ar1([256])
        qtTm = ar1([4, 128])
        ktok = ar1([256])
        vg = ar1([512])
        gate = a1([1024])
        atm = ar1([512])
        og = a1([1024], BF16)
        oT = a1([8, 128], BF16)
        sm = a1([64])
        tokd = A.view(A.cur["p1"], [1536], F32)
        cbuf = a1([12, 176])
        SCV = A.view(A.cur["p1"], [1536], F32, parts=48)
        cv_off = A.cur["p1"]
        cv = a1([12, 128])
        sqb = a1([8, 128], BF16)
        rinv = a1([8, 128])
        junk = rinv[:, 7, :]
        ST2 = A.view(cv_off + 2048, [16, 128], F32)
        qkn = ar1([4, 2, 128])
        gB = ar1([4, 128])
        GT = a1([2, 128])
        G = a1([2, 128])
        Nn = ar1([2, 128])
        Cm = ar1([2, 128])
        Mb = ar1([2, 2, 128])
        MTb = ar1([2, 2, 128])
        Pb = ar1([2, 2, 128])
        XdT = ar1([2, 128])
        Yb = ar1([2, 128])
        XT = ar1([4, 128])
        QKm = ar1([4, 128])
        vb = ar1([512])
        kb = ar1([512])
        kd = ar1([512])
        nwT = ar1([2, 128])
        delta = ar1([2, 128])
        osb = a1([2, 128])
        ST = a1([16, 128])
        EXP = a1([2176])
        VEX = a1([16, 128])
        GT1 = a1([2, 128])
        G1 = a1([2, 128])
        bufsets = [
            (Nn, Cm, Mb, MTb, Pb, XdT, Yb, GT, G),
            (lz.rearrange("p (a b) -> p a b", a=2), ktT.rearrange("p (a b) -> p a b", a=2),
             atm.rearrange("p (a b c) -> p a b c", a=2, b=2), vg.rearrange("p (a b c) -> p a b c", a=2, b=2),
             qtTm.rearrange("p (a b) c -> p a b c", a=2), ktok.rearrange("p (a b) -> p a b", a=2),
             nwT, GT1, G1),
        ]
        nwT4 = Mb.rearrange("p a b c -> p (a b) c")
        delta4 = MTb.rearrange("p a b c -> p (a b) c")
        osb4_p = rinv[:, 0:4, :]
        osb4_s = cv[:, 0:4, :]
        p1_end = A.cur["p1"]

        ssq = sm[:, 0:8]
        rstd = sm[:, 8:16]
        dab = sm[:, 16:24]
        gg = ar1([4])
        beta = sm[:, 28:32]
        gcs = sm[:, 32:36]
        ngc = sm[:, 36:40]
        egc = sm[:, 40:44]
        bgs = sm[:, 44:48]
        egl = sm[:, 48:52]
        tmp4 = sm[:, 52:56]
        gcl = sm[:, 56:60]
        decs = a1([64])

        P.dma("sp", CT, consts_i)
        P.copy("dve", identP, c_ident)
        P.copy("dve", nhalfP, c_nhalf[:, 0:8])
        P.dma("sp", PRM, prm_i)
        P.dma("sp", WGU, wgu_i)
        P.dma("sp", GNC, gnc_i)
        P.dma("sp", CWT.rearrange("p a b -> p (a b)"), cwT_i)
        for (c0, c1) in ((1024, 1552), (3088, 3608), (1552, 3088), (0, 1024)):
            for k in range(8):
                P.dma("pool", win[:, k, c0:c1], w_in[k * 128:(k + 1) * 128, c0:c1])
        for k in range(8):
            P.dma("pool", wout[:, k, :], w_out[k * 128:(k + 1) * 128, :])
        for i, src in enumerate([c_ident, c_mbuf, c_mbus, c_mblf, c_mbls, c_ones]):
            P.copy("dve", CB[:, i * 128:(i + 1) * 128], src)
        P.act(NEGA, p_alog, AF.Exp)
        P.ts("dve", NEGA, NEGA, -1.0, None, ALU.mult)
        P.ts("pool", R(SGp.rearrange("p a b -> p (a b)")), CT[:, 0:256], 0.0, None, ALU.mult)
        P.ts("pool", R(SDp.rearrange("p a b -> p (a b)")), CT[:, 0:512], 0.0, None, ALU.mult)
        P.memset("pool", HALO, 0.0)
        P.memset("pool", EXP, 0.0)
        for nm_, src_ in (("Uf", c_Uf), ("Us", c_Us), ("nUf", c_nUf), ("nUs", c_nUs), ("u16f", c_u16f), ("u16s", c_u16s)):
            P.copy("dve", R(rU[nm_]), src_)

        conv_hoisted = {}

        def conv_gen():
            P.copy("dve", cbuf[:, :, 0:3], HALO)
            for g3 in range(3):
                bd = PS.alloc()
                for j in range(4):
                    c0 = OFF_DQKV + (g3 * 4 + j) * 128
                    for k in range(8):
                        P.mm(ps[bd][:, j * 128:(j + 1) * 128], win[:, k, c0:c0 + 128], xT[:, k, :], start=(k == 0), stop=(k == 7))
                for j in range(4):
                    P.copy("act", cbuf[:, g3 * 4 + j, 3:131], ps[bd][:, j * 128:(j + 1) * 128])
                PS.release(bd)
                yield
            P.copy("dve", HALO, cbuf[:, :, 128:131])
            for cc in range(12):
                P.act(cv[:, cc, :], cbuf[:, cc, 0:128], AF.Copy, scale=CWT[:, cc, 0:1])
                for i in range(1, 4):
                    P.stt(cv[:, cc, :], cbuf[:, cc, i:i + 128], CWT[:, cc, i:i + 1], cv[:, cc, :], ALU.mult, ALU.add)
                yield

        def front(t):
            xt = xtb[:, tile_pos[t] % 2, :]
            P.dma("sp", xt, xs[t])
            b0, b1 = PS.alloc(), PS.alloc()
            for k in range(8):
                bk = ps[b0] if k < 4 else ps[b1]
                P.tr(bk[:, (k % 4) * 128:(k % 4 + 1) * 128], xt[:, k * 128:(k + 1) * 128], c_ident)
            P.act(xT[:, 0:4, :].rearrange("p a b -> p (a b)"), ps[b0][:, :], AF.Copy)
            P.copy("dve", xT[:, 4:8, :].rearrange("p a b -> p (a b)"), ps[b1][:, :])
            PS.release(b0)
            PS.release(b1)

        def phase1_tile(t):
            sample = (t == NT - 1)
            osb4 = osb4_s if sample else osb4_p
            nseq = 16 if sample else 1
            U = rU["Us"] if sample else rU["Uf"]
            nU = rU["nUs"] if sample else rU["nUf"]
            u16 = rU["u16s"] if sample else rU["u16f"]
            m01 = c_Us if sample else c_Uf
            mbu = b_mbus if sample else b_mbuf
            mbl = b_mbls if sample else b_mblf
            Ulast = rU["Us"][:, 7:128:8] if sample else rU["Uf"][:, 126:128]
            nlast = 16 if sample else 2
            nlev = 3 if sample else 6

            xt = xtb[:, tile_pos[t] % 2, :]
            rr = xt

            def mm_tok(out_ap, c0, n):
                for k in range(8):
                    P.mm(out_ap, xT[:, k, :], win[:, k, c0:c0 + n], start=(k == 0), stop=(k == 7))

            def mm_feat(out_ap, c0, m):
                for k in range(8):
                    P.mm(out_ap, win[:, k, c0:c0 + m], xT[:, k, :], start=(k == 0), stop=(k == 7))

            bg_ = PS.alloc()
            mm_tok(ps[bg_][:, :], OFF_GG, 512)
            P.act(gate[:, 0:512], ps[bg_][:, :], AF.Silu)
            PS.release(bg_)
            bg_ = PS.alloc()
            mm_tok(ps[bg_][:, :], OFF_DG, 512)
            P.act(gate[:, 512:1024], ps[bg_][:, :], AF.Silu)
            PS.release(bg_)

            if sample:
                P.dma("sp", SCV, sconv_i)
                bh0, bh1 = PS.alloc(), PS.alloc()
                for cc in range(12):
                    bk = ps[bh0] if cc < 8 else ps[bh1]
                    P.tr(bk[:, (cc % 8) * 48:(cc % 8 + 1) * 48], SCV[:, cc * 128:(cc + 1) * 128], c_ident[0:48, 0:48])
                for cc in range(12):
                    bk = ps[bh0] if cc < 8 else ps[bh1]
                    P.copy("dve" if cc % 2 else "act",
                           cbuf[:, cc, :].rearrange("p (s w) -> p s w", w=11)[:, :, 0:3],
                           bk[:, (cc % 8) * 48:(cc % 8 + 1) * 48].rearrange("p (s r) -> p s r", r=3))
                PS.release(bh0)
                PS.release(bh1)
                for g3 in range(3):
                    bd = PS.alloc()
                    for j in range(4):
                        mm_feat(ps[bd][:, j * 128:(j + 1) * 128], OFF_DQKV + (g3 * 4 + j) * 128, 128)
                    for j in range(4):
                        cc = g3 * 4 + j
                        P.copy("act", cbuf[:, cc, :].rearrange("p (s w) -> p s w", w=11)[:, :, 3:11],
                               ps[bd][:, j * 128:(j + 1) * 128].rearrange("p (s t) -> p s t", t=8))
                    PS.release(bd)
                for cc in range(12):
                    def tap(i):
                        return cbuf[:, cc, :].rearrange("p (s w) -> p s w", w=11)[:, :, i:i + 8]
                    dst = cv[:, cc, :].rearrange("p (s t) -> p s t", t=8)
                    P.act(dst, tap(0), AF.Copy, scale=CWT[:, cc, 0:1])
                    for i in range(1, 4):
                        P.stt(dst, tap(i), CWT[:, cc, i:i + 1], dst, ALU.mult, ALU.add)
            elif not conv_hoisted.get(t):
                for _ in conv_gen():
                    pass
            P.act(cv.rearrange("p a b -> p (a b)"), cv.rearrange("p a b -> p (a b)"), AF.Silu)

            def run_il(gens):
                gens = list(gens)
                while gens:
                    for g in list(gens):
                        try:
                            next(g)
                        except StopIteration:
                            gens.remove(g)

            def gdn_gen():
                P.act(sqb.rearrange("p a b -> p (a b)"), cv[:, 0:8, :].rearrange("p a b -> p (a b)"), AF.Square)
                bn0, bn1 = PS.alloc(), PS.alloc()
                for cc in range(8):
                    bk = ps[bn0] if cc < 4 else ps[bn1]
                    P.mm(bk[:, (cc % 4) * 128:(cc % 4 + 1) * 128], b_ones, sqb[:, cc, :])
                P.act(rinv[:, 0:4, :].rearrange("p a b -> p (a b)"), ps[bn0][:, :], AF.Ln, bias=1e-6)
                P.act(rinv[:, 4:8, :].rearrange("p a b -> p (a b)"), ps[bn1][:, :], AF.Ln, bias=1e-6)
                PS.release(bn0)
                yield
                PS.release(bn1)
                yield
                P.act(rinv.rearrange("p a b -> p (a b)"), rinv.rearrange("p a b -> p (a b)"), AF.Exp, scale=-0.5)
                for h in range(4):
                    P.stt(R(qkn[:, h, 1, :]), cv[:, h, :], 128.0 ** -0.5, rinv[:, h, :], ALU.mult, ALU.mult)
                    P.tt("dve", R(qkn[:, h, 0, :]), cv[:, 4 + h, :], rinv[:, 4 + h, :], ALU.mult)
                btk, btv = PS.alloc(), PS.alloc()
                for h in range(4):
                    P.tr(ps[btk][:, h * 128:(h + 1) * 128], qkn[:, h, 0, :], c_ident)
                    P.tr(ps[btv][:, h * 128:(h + 1) * 128], cv[:, 8 + h, :], c_ident)
                yield "L2DONE"
                bgc = PS.alloc()
                P.mm(ps[bgc][:, 0:4], R(U), R(gg))
                P.copy("dve", gcs, ps[bgc][:, 0:4])
                P.ts("dve", ngc, ps[bgc][:, 0:4], -1.0, None, ALU.mult)
                P.act(egc, ps[bgc][:, 0:4], AF.Exp)
                for h in range(4):
                    P.mm(ps[bgc][:, 64 + h * 16:64 + h * 16 + nlast], R(gB[:, h, :]), R(Ulast))
                for h in range(4):
                    P.act(decs[:, h * 16:h * 16 + nseq], ps[bgc][:, 64 + h * 16 + nlast - nseq:64 + h * 16 + nlast], AF.Exp)
                if not sample:
                    P.copy("dve", gcl, ps[bgc][:, 65:129:16])
                PS.release(bgc)
                yield
                P.tt("dve", bgs, beta, egc, ALU.mult)
                if not sample:
                    for h in range(4):
                        P.act(egl[:, h:h + 1], gcs[:, h:h + 1], AF.Exp, scale=-1.0, bias=gcl[:, h:h + 1])
                    for h in range(4):
                        hs = slice(h * 128, (h + 1) * 128)
                        P.ts("dve", R(vb[:, hs]), ps[btv][:, hs], beta[:, h:h + 1], None, ALU.mult)
                        P.ts("dve", R(kb[:, hs]), ps[btk][:, hs], bgs[:, h:h + 1], None, ALU.mult)
                        P.act(R(kd[:, hs]), ps[btk][:, hs], AF.Copy, scale=egl[:, h:h + 1])
                    PS.release(btk)
                    PS.release(btv)
                    yield
                bkq0, bkq1 = PS.alloc(), PS.alloc()
                for h in range(4):
                    bk = ps[bkq0] if h < 2 else ps[bkq1]
                    P.mm(bk[:, (h % 2) * 256:(h % 2 + 1) * 256], R(qkn[:, h, 0, :]), R(qkn[:, h, :, :].rearrange("p a b -> p (a b)")))

                while not gla_done[0]:
                    yield

                def pair_gen(hp):
                    Nn, Cm, Mb, MTb, Pb, XdT, Yb, GT, G = bufsets[hp]
                    bpa, bpb = PS.alloc(), PS.alloc()
                    for i in range(2):
                        h = hp * 2 + i
                        pa = ps[bpa][:, i * 128:(i + 1) * 128]
                        pb = ps[bpb][:, i * 128:(i + 1) * 128]
                        P.mm(pa, R(gB[:, h, :]), R(U), start=True, stop=False)
                        P.mm(pa, b_ident, mbu, start=False, stop=True)
                        P.mm(pb, R(gB[:, h, :]), R(nU), start=True, stop=False)
                        P.mm(pb, b_ident, mbl, start=False, stop=True)
                    for i in range(2):
                        h = hp * 2 + i
                        P.act(GT[:, i, :], ps[bpa][:, i * 128:(i + 1) * 128], AF.Exp, bias=ngc[:, h:h + 1])
                        P.act(G[:, i, :], ps[bpb][:, i * 128:(i + 1) * 128], AF.Exp, bias=gcs[:, h:h + 1])
                    PS.release(bpa)
                    yield
                    PS.release(bpb)
                    yield
                    for i in range(2):
                        h = hp * 2 + i
                        kq = (ps[bkq0] if h < 2 else ps[bkq1])[:, (h % 2) * 256:(h % 2 + 1) * 256]
                        P.stt(R(Nn[:, i, :]), kq[:, 0:128], beta[:, h:h + 1], G[:, i, :], ALU.mult, ALU.mult)
                        P.tt("dve", R(QKm[:, h, :]), kq[:, 128:256], GT[:, i, :], ALU.mult)
                        if sample:
                            P.add("dve", lambda e, i=i, h=h: e.reduce_sum(egl[:, h:h + 1], GT[:, i, 7:128:8], mybir.AxisListType.X),
                                  [GT[:, i, 7:128:8]], [egl[:, h:h + 1]])
                        if not sample:
                            P.tt("dve", R(Cm[:, i, :]), Nn[:, i, :], c_blk64, ALU.mult)
                            P.tt("dve", R(Nn[:, i, :]), Nn[:, i, :], Cm[:, i, :], ALU.subtract)
                            nd = Cm[:, i, :]
                        else:
                            nd = Nn[:, i, :]
                        P.act(R(Mb[:, 0, i, :]), nd, AF.Copy, scale=-1.0)
                        P.stt(R(Pb[:, 0, i, :]), nd, -1.0, c_ident, ALU.mult, ALU.add)
                    bx = PS.alloc()
                    for i in range(2):
                        nd = Nn[:, i, :] if sample else Cm[:, i, :]
                        P.tr(ps[bx][:, i * 128:(i + 1) * 128], nd, c_ident)
                    P.act(R(MTb[:, 0, 0, :]), ps[bx][:, 0:128], AF.Copy, scale=-1.0)
                    P.ts("dve", R(MTb[:, 0, 1, :]), ps[bx][:, 128:256], -1.0, None, ALU.mult)
                    PS.release(bx)
                    yield
                    for k in range(1, nlev):
                        a, b = (k - 1) % 2, k % 2
                        bm = PS.alloc()
                        for i in range(2):
                            P.mm(ps[bm][:, i * 128:(i + 1) * 128], R(Mb[:, a, i, :]), R(MTb[:, a, i, :]))
                            if k < nlev - 1:
                                P.mm(ps[bm][:, 256 + i * 128:256 + (i + 1) * 128], R(MTb[:, a, i, :]), R(Mb[:, a, i, :]))
                        P.copy("act", R(MTb[:, b, :, :].rearrange("p a b -> p (a b)")), ps[bm][:, 0:256])
                        if k < nlev - 1:
                            P.copy("dve", R(Mb[:, b, :, :].rearrange("p a b -> p (a b)")), ps[bm][:, 256:512])
                        PS.release(bm)
                        yield
                        bp = PS.alloc()
                        for i in range(2):
                            P.mm(ps[bp][:, i * 128:(i + 1) * 128], R(MTb[:, b, i, :]), R(Pb[:, a, i, :]))
                        P.tt("dve", R(Pb[:, b, :, :].rearrange("p a b -> p (a b)")), ps[bp][:, 0:256],
                             Pb[:, a, :, :].rearrange("p a b -> p (a b)"), ALU.add)
                        PS.release(bp)
                        yield
                    fin = (nlev - 1) % 2
                    bx = PS.alloc()
                    for i in range(2):
                        P.tr(ps[bx][:, i * 128:(i + 1) * 128], Pb[:, fin, i, :], c_ident)
                    for i in range(2):
                        h = hp * 2 + i
                        dst = R(XT[:, h, :]) if sample else R(XdT[:, i, :])
                        P.copy("act", dst, ps[bx][:, i * 128:(i + 1) * 128])
                    PS.release(bx)
                    yield
                    if not sample:
                        by = PS.alloc()
                        for i in range(2):
                            P.mm(ps[by][:, i * 128:(i + 1) * 128], R(Nn[:, i, :]), R(XdT[:, i, :]))
                        for i in range(2):
                            P.copy("act", R(Yb[:, i, :]), ps[by][:, i * 128:(i + 1) * 128])
                        for i in range(2):
                            P.mm(ps[by][:, 256 + i * 128:256 + (i + 1) * 128], R(Pb[:, fin, i, :]), R(Yb[:, i, :]))
                        for i in range(2):
                            h = hp * 2 + i
                            P.tt("dve", R(XT[:, h, :]), XdT[:, i, :], ps[by][:, 256 + i * 128:256 + (i + 1) * 128], ALU.subtract)
                        PS.release(by)
                        yield

                gens_ = [pair_gen(0), pair_gen(1)]
                if (not sample) and tile_pos[t] + 1 < len(p1_order):
                    conv_hoisted[p1_order[tile_pos[t] + 1]] = True
                    gens_.append(conv_gen())
                run_il(gens_)
                PS.release(bkq0)
                yield
                PS.release(bkq1)
                yield
                if sample:
                    for h in range(4):
                        hs = slice(h * 128, (h + 1) * 128)
                        P.ts("dve", R(vb[:, hs]), ps[btv][:, hs], beta[:, h:h + 1], None, ALU.mult)
                        P.ts("dve", R(kb[:, hs]), ps[btk][:, hs], bgs[:, h:h + 1], None, ALU.mult)
                        P.act(R(kd[:, hs]), ps[btk][:, hs], AF.Copy, scale=egl[:, h:h + 1])
                    PS.release(btk)
                    yield
                    PS.release(btv)
                yield


            if sample or t == NT - 2:
                for g3 in range(3):
                    bd = PS.alloc()
                    mm_tok(ps[bd][:, :], OFF_DQKV + g3 * 512, 512)
                    P.copy("dve", tokd[:, g3 * 512:(g3 + 1) * 512], ps[bd][:, :])
                    PS.release(bd)
                if sample:
                    for s in range(16):
                        P.dma("sp", sconv_o[s], tokd[s * 8 + 5:s * 8 + 8, :])
                else:
                    P.dma("sp", pconv_o, tokd[125:128, :])

            bv = PS.alloc()
            mm_tok(ps[bv][:, :], OFF_GV, 512)
            P.copy("dve", R(vg), ps[bv][:, :])
            PS.release(bv)
            bqk = PS.alloc()
            for j in range(4):
                mm_feat(ps[bqk][:, j * 128:(j + 1) * 128], OFF_GQ + j * 128, 128)
            bga = PS.alloc()
            mm_feat(ps[bga][0:16, 0:128], OFF_GA, 16)
            P.copy("dve", gaT, ps[bga][0:16, 0:128])
            mm_tok(ps[bga][:, 128:136], OFF_DAB, 8)
            P.copy("dve", dab, ps[bga][:, 128:136])
            P.mm(ps[bga][:, 256:512], gaT, WGU, start=True, stop=False)
            P.mm(ps[bga][:, 256:512], c_ones[0:1, :], p_bgate, start=False, stop=True)
            if tile_pos[t] + 1 < len(p1_order):
                pass
            elif KP2 > 0:
                for k in range(8):
                    P.dma("pool", wg_sb[:, k, :], w_g[k * 128:(k + 1) * 128, :])
                for k in range(2):
                    P.dma("pool", wu_sb[:, k, :], w_u[k * 128:(k + 1) * 128, :])
                wg_loaded.append(1)

            gdn = gdn_gen()
            for _m in gdn:
                if _m == "L2DONE":
                    break

            if tile_pos[t] + 1 < len(p1_order):
                front(p1_order[tile_pos[t] + 1])

            P.act(R(lz), ps[bga][:, 256:512], AF.Exp, scale=-1.0)
            PS.release(bga)
            P.act(R(lz), lz, AF.Ln, bias=1.0)
            bb = PS.alloc()
            for c in range(2):
                P.mm(ps[bb][:, c * 128:(c + 1) * 128], R(lz[:, c * 128:(c + 1) * 128]), R(u16))
            P.act(eb, ps[bb][:, 0:256], AF.Exp)
            P.act(enb, ps[bb][:, 0:256], AF.Exp, scale=-1.0)
            PS.release(bb)
            P.stt(qtT, ps[bqk][:, 0:256], 0.125, eb, ALU.mult, ALU.mult)
            P.tt("dve", R(ktT), ps[bqk][:, 256:512], enb, ALU.mult)
            PS.release(bqk)
            for h in range(4):
                P.act(R(qtTm[:, h, :]), qtT[:, (h // 2) * 128:(h // 2 + 1) * 128], AF.Copy,
                      scale=c_blk64[:, (h % 2) * 64:(h % 2) * 64 + 1])
            P.tt("dve", tmp4, dab[:, 0:4], p_dtb, ALU.add)
            P.act(tmp4, tmp4, AF.Exp)
            P.act(tmp4, tmp4, AF.Ln, bias=1.0)
            P.tt("dve", R(gg), tmp4, NEGA, ALU.mult)
            P.act(beta, dab[:, 4:8], AF.Exp, scale=-1.0)
            P.ts("dve", beta, beta, 1.0, None, ALU.add)
            P.add("dve", lambda e: e.reciprocal(beta, beta), [beta], [beta])
            for h in range(4):
                P.act(R(gB[:, h, :]), c_ones, AF.Copy, scale=gg[:, h:h + 1])


            def gla_gen():
                bt = PS.alloc()
                for c in range(2):
                    P.tr(ps[bt][:, c * 128:(c + 1) * 128], ktT[:, c * 128:(c + 1) * 128], c_ident)
                P.copy("act", R(ktok), ps[bt][:, 0:256])
                PS.release(bt)
                yield
                bat = PS.alloc()
                for h in range(4):
                    c, r = h // 2, h % 2
                    rs = slice(r * 64, (r + 1) * 64)
                    P.mm(ps[bat][:, h * 128:(h + 1) * 128], R(ktT[:, c * 128:(c + 1) * 128]), R(qtTm[:, h, :]))
                for h in range(4):
                    P.tt("dve", R(atm[:, h * 128:(h + 1) * 128]), ps[bat][:, h * 128:(h + 1) * 128], m01, ALU.mult)
                PS.release(bat)
                yield
                bo = PS.alloc()
                for c in range(2):
                    if sample:
                        P.dma("sp", ST, sgla_i[:, 2 * c:2 * c + 2].rearrange("s r d v -> (r d) s v"))
                    for r in range(2):
                        h = 2 * c + r
                        rs = slice(r * 64, (r + 1) * 64)
                        oh = ps[bo][:, h * 128:(h + 1) * 128]
                        P.mm(oh, R(atm[:, h * 128:(h + 1) * 128]), R(vg[:, h * 128:(h + 1) * 128]), start=True, stop=False)
                        if not sample:
                            P.mm(oh, R(qtTm[:, h, :]), R(SGp[:, c, :]), start=False, stop=True)
                        else:
                            P.copy("dve", EXP.rearrange("p (s w) -> p s w", w=136)[:, :, 0:8],
                                   qtTm[:, h, :].rearrange("p (s t) -> p s t", t=8))
                            for s in range(16):
                                P.mm(oh, EXP[:, s * 128:(s + 1) * 128], ST[:, s, :], start=False, stop=(s == 15))
                    for r in range(2):
                        h = 2 * c + r
                        rs = slice(r * 64, (r + 1) * 64)
                        if not sample:
                            bs = PS.alloc()
                            P.mm(ps[bs][:, 0:128], R(ktok[:, c * 128:(c + 1) * 128]), R(vg[:, h * 128:(h + 1) * 128]))
                            ebl = eb[rs, c * 128 + 127:c * 128 + 128]
                            P.stt(R(SGp[rs, c, :]), ps[bs][rs, 0:128], 1.0, SGp[rs, c, :], ALU.mult, ALU.add)
                            P.ts("dve", R(SGp[rs, c, :]), SGp[rs, c, :], ebl, None, ALU.mult)
                            PS.release(bs)
                            yield
                        else:
                            for s in range(16):
                                if s % 2:
                                    P.ts("dve", VEX[:, s, :], vg[:, h * 128:(h + 1) * 128], c_bind[:, s:s + 1], None, ALU.mult)
                                else:
                                    P.act(VEX[:, s, :], vg[:, h * 128:(h + 1) * 128], AF.Copy, scale=c_bind[:, s:s + 1])
                            for sg in range(4):
                                bs = PS.alloc()
                                P.mm(ps[bs][:, :], ktok[:, c * 128:(c + 1) * 128],
                                     VEX[:, sg * 4:(sg + 1) * 4, :].rearrange("p a b -> p (a b)"))
                                for s4 in range(4):
                                    s = sg * 4 + s4
                                    ebl = eb[rs, c * 128 + 8 * s + 7:c * 128 + 8 * s + 8]
                                    P.stt(ST[rs, s, :], ps[bs][rs, s4 * 128:(s4 + 1) * 128], 1.0, ST[rs, s, :], ALU.mult, ALU.add)
                                    P.ts("dve", ST[rs, s, :], ST[rs, s, :], ebl, None, ALU.mult)
                                PS.release(bs)
                                yield
                    if sample:
                        P.dma("sp", sgla_o[:, 2 * c:2 * c + 2].rearrange("s r d v -> (r d) s v"), ST)
                for h in range(4):
                    P.act(junk, ps[bo][:, h * 128:(h + 1) * 128], AF.Square, accum_out=ssq[:, h:h + 1])
                P.ts("dve", rstd[:, 0:4], ssq[:, 0:4], 1.0 / 128.0, 1e-5, ALU.mult, ALU.add)
                P.tt("pool", rstd[:, 0:4], rstd[:, 0:4], c_nhalf[:, 0:4], ALU.pow)
                for h in range(4):
                    P.stt(og[:, h * 128:(h + 1) * 128], ps[bo][:, h * 128:(h + 1) * 128], rstd[:, h:h + 1],
                          gate[:, h * 128:(h + 1) * 128], ALU.mult, ALU.mult)
                PS.release(bo)
                yield


            gla_done = [False]

            def gla_wrapped():
                yield from gla_gen()
                gla_done[0] = True

            run_il([gla_wrapped(), gdn])

            def head_gen(h):
                i = h
                hs = slice(h * 128, (h + 1) * 128)
                STh = (ST if h % 2 == 0 else ST2) if sample else None
                if sample:
                    P.dma("sp", STh, sgdn_i[:, h].rearrange("s d v -> d s v"))

                def S_of(s):
                    return STh[:, s, :] if sample else SDp[:, h, :]

                def state_terms(out_ap, srcT, first):
                    if not sample:
                        P.mm(out_ap, R(srcT), R(S_of(0)), start=first, stop=True)
                    else:
                        P.copy("dve", EXP.rearrange("p (s w) -> p s w", w=136)[:, :, 0:8],
                               srcT.rearrange("p (s t) -> p s t", t=8))
                        for s in range(16):
                            P.mm(out_ap, EXP[:, s * 128:(s + 1) * 128], S_of(s), start=(first and s == 0), stop=(s == 15))

                bw = PS.alloc()
                P.mm(ps[bw][:, 0:128], R(kb[:, hs]), R(XT[:, h, :]))
                P.act(R(nwT4[:, i, :]), ps[bw][:, 0:128], AF.Copy, scale=-1.0)
                yield
                P.mm(ps[bw][:, 128:256], R(XT[:, h, :]), R(vb[:, hs]), start=True, stop=False)
                state_terms(ps[bw][:, 128:256], nwT4[:, i, :], False)
                P.copy("act", R(delta4[:, i, :]), ps[bw][:, 128:256])
                yield
                state_terms(ps[bw][:, 256:384], qkn[:, h, 1, :], True)
                P.mm(ps[bw][:, 384:512], R(QKm[:, h, :]), R(delta4[:, i, :]))
                P.act(osb4[:, i, :], ps[bw][:, 256:384], AF.Copy, scale=egc[:, h:h + 1])
                P.tt("dve", osb4[:, i, :], osb4[:, i, :], ps[bw][:, 384:512], ALU.add)
                yield
                PS.release(bw)
                yield
                P.act(junk, osb4[:, i, :], AF.Square, accum_out=ssq[:, 4 + h:5 + h])
                P.ts("dve", rstd[:, 4 + h:5 + h], ssq[:, 4 + h:5 + h], 1.0 / 128.0, 1e-5, ALU.mult, ALU.add)
                P.tt("pool", rstd[:, 4 + h:5 + h], rstd[:, 4 + h:5 + h], c_nhalf[:, 0:1], ALU.pow)
                P.stt(og[:, 512 + h * 128:512 + (h + 1) * 128], osb4[:, i, :], rstd[:, 4 + h:5 + h],
                      gate[:, 512 + h * 128:512 + (h + 1) * 128], ALU.mult, ALU.mult)
                yield
                if not sample:
                    bs = PS.alloc()
                    P.mm(ps[bs][:, 0:128], R(kd[:, hs]), R(delta4[:, i, :]))
                    P.stt(R(SDp[:, h, :]), SDp[:, h, :], decs[:, h * 16:h * 16 + 1], ps[bs][:, 0:128], ALU.mult, ALU.add)
                    PS.release(bs)
                    yield
                else:
                    for s in range(16):
                        if s % 2:
                            P.ts("dve", VEX[:, s, :], delta4[:, i, :], c_bind[:, s:s + 1], None, ALU.mult)
                        else:
                            P.act(VEX[:, s, :], delta4[:, i, :], AF.Copy, scale=c_bind[:, s:s + 1])
                    for sg in range(4):
                        bs = PS.alloc()
                        P.mm(ps[bs][:, :], kd[:, hs], VEX[:, sg * 4:(sg + 1) * 4, :].rearrange("p a b -> p (a b)"))
                        for s4 in range(4):
                            s = sg * 4 + s4
                            P.stt(STh[:, s, :], STh[:, s, :], decs[:, h * 16 + s:h * 16 + s + 1],
                                  ps[bs][:, s4 * 128:(s4 + 1) * 128], ALU.mult, ALU.add)
                        PS.release(bs)
                    P.dma("sp", sgdn_o[:, h].rearrange("s d v -> d s v"), STh)


            if sample:
                run_il([head_gen(0), head_gen(1)])
                run_il([head_gen(2), head_gen(3)])
            else:
                run_il([head_gen(0), head_gen(1), head_gen(2), head_gen(3)])

            bto = PS.alloc()
            pb16 = ps[bto].bitcast(BF16)
            for e8 in range(8):
                P.tr(pb16[:, e8 * 128:(e8 + 1) * 128], og[:, e8 * 128:(e8 + 1) * 128], b_ident)
            P.act(oT[:, 0:4, :].rearrange("p a b -> p (a b)"), pb16[:, 0:512], AF.Copy, scale=GNC[:, 0:1])
            P.ts("dve", oT[:, 4:8, :].rearrange("p a b -> p (a b)"), pb16[:, 512:1024], GNC[:, 1:2], None, ALU.mult)
            PS.release(bto)
            bm0, bm1 = PS.alloc(), PS.alloc()
            for half, bk in ((0, bm0), (1, bm1)):
                for e8 in range(8):
                    P.mm(ps[bk][:, :], oT[:, e8, :], wout[:, e8, half * 512:(half + 1) * 512], start=(e8 == 0), stop=(e8 == 7))
            for half, bk in ((0, bm0), (1, bm1)):
                P.stt(rr[:, half * 512:(half + 1) * 512], xt[:, half * 512:(half + 1) * 512], ALPHA, ps[bk][:, :], ALU.mult, ALU.add)
            PS.release(bm0)
            PS.release(bm1)
            if "rr0" in dbg and t == 0:
                P.dma("sp", dbg["rr0"], rr)
                P.dma("sp", dbg["sm"], sm)
                P.dma("sp", dbg["gate"], gate)
                P.dma("sp", dbg["xt"], xt)
            layer_norm(rr, p_lng, p_lnb)
            if "x1" in dbg and t == 0:
                P.dma("sp", dbg["x1"], rr)
            P.dma("sp", x1s[t], rr)

        def layer_norm(buf, gB_, bB_):
            for half in range(2):
                P.add("dve", lambda e, half=half: e.bn_stats(stats[:, half * 6:(half + 1) * 6], buf[:, half * 512:(half + 1) * 512]),
                      [buf[:, half * 512:(half + 1) * 512]], [stats[:, half * 6:(half + 1) * 6]])
            P.add("dve", lambda e: e.bn_aggr(stats[:, 12:14], stats[:, 0:12]), [stats[:, 0:12]], [stats[:, 12:14]])
            P.ts("dve", stats[:, 14:15], stats[:, 13:14], 1e-5, None, ALU.add)
            P.tt("pool", stats[:, 14:15], stats[:, 14:15], nhalfP[:, 0:1], ALU.pow)
            P.ts("dve", buf, buf, stats[:, 12:13], stats[:, 14:15], ALU.subtract, ALU.mult)
            P.tt("dve", buf, buf, gB_, ALU.mult)
            P.tt("dve", buf, buf, bB_, ALU.add)

        dbg_taps = {}
        import os as _os
        order = [NT - 1] + list(range(NT - 1))
        if _os.environ.get("KORDER") == "p":
            order = list(range(NT))
        import os
        KP1 = int(os.environ.get("KP1", "17"))
        KP2 = int(os.environ.get("KP2", "17"))
        wg_loaded = []
        p1_order = order[:KP1]
        tile_pos = {t: i for i, t in enumerate(p1_order)}
        if p1_order:
            front(p1_order[0])
        for t in p1_order:
            phase1_tile(t)
        P.dma("sp", pgla_o.rearrange("(c r) d v -> (r d) c v", r=2), SGp)
        P.dma("sp", pgdn_o.rearrange("h d v -> d h v"), SDp)

        def a2(shape, dt=F32, parts=128):
            return A.alloc("p2", shape, dt, parts)

        x2b = a2([2, D])
        x2T = a2([8, 128], BF16)
        hidT = a2([NFC, 128], BF16)
        hidg = a2([2, 512], BF16)
        if not wg_loaded:
            for k in range(8):
                P.dma("pool", wg_sb[:, k, :], w_g[k * 128:(k + 1) * 128, :])
        for k in range(2 if wg_loaded else 0, 8):
            P.dma("pool", wu_sb[:, k, :], w_u[k * 128:(k + 1) * 128, :])
        for f in range(NFC):
            P.dma("pool", wd_sb[:, f, :], w_d[f * 128:(f + 1) * 128, :])
        P.dma("sp", PRM[:, 256:2304], prm2_i)

        def front2(t):
            x2 = x2b[:, pos2[t] % 2, :]
            P.dma("sp", x2, x1s[t])
            b0, b1 = PS.alloc(), PS.alloc()
            for k in range(8):
                bk = ps[b0] if k < 4 else ps[b1]
                P.tr(bk[:, (k % 4) * 128:(k % 4 + 1) * 128], x2[:, k * 128:(k + 1) * 128], identP)
            P.act(x2T[:, 0:4, :].rearrange("p a b -> p (a b)"), ps[b0][:, :], AF.Copy)
            P.copy("dve", x2T[:, 4:8, :].rearrange("p a b -> p (a b)"), ps[b1][:, :])
            PS.release(b0)
            PS.release(b1)

        def phase2_tile(t):
            x2 = x2b[:, pos2[t] % 2, :]
            def hid_transposes(gi, g0, n):
                hg = hidg[:, gi % 2, :]
                bt_ = PS.alloc()
                pb16 = ps[bt_].bitcast(BF16)
                nf = n // 128
                for j in range(nf):
                    P.tr(pb16[:, j * 128:(j + 1) * 128], hg[:, j * 128:(j + 1) * 128], b_ident)
                f0 = g0 // 128
                if gi % 2 == 0:
                    P.act(hidT[:, f0:f0 + nf, :].rearrange("p a b -> p (a b)"), pb16[:, 0:nf * 128], AF.Copy)
                else:
                    P.copy("dve", hidT[:, f0:f0 + nf, :].rearrange("p a b -> p (a b)"), pb16[:, 0:nf * 128])
                PS.release(bt_)

            pending = None
            for gi, g0 in enumerate(range(0, DFF, 512)):
                n = min(512, DFF - g0)
                hg = hidg[:, gi % 2, :]
                bgt, bup = PS.alloc(), PS.alloc()
                for k in range(8):
                    P.mm(ps[bgt][:, 0:n], x2T[:, k, :], wg_sb[:, k, g0:g0 + n], start=(k == 0), stop=(k == 7))
                for k in range(8):
                    P.mm(ps[bup][:, 0:n], x2T[:, k, :], wu_sb[:, k, g0:g0 + n], start=(k == 0), stop=(k == 7))
                P.act(hg[:, 0:n], ps[bgt][:, 0:n], AF.Silu)
                P.tt("dve", hg[:, 0:n], hg[:, 0:n], ps[bup][:, 0:n], ALU.mult)
                PS.release(bgt)
                PS.release(bup)
                if pending is not None:
                    hid_transposes(*pending)
                pending = (gi, g0, n)
            if pos2[t] + 1 < len(p2_order):
                front2(p2_order[pos2[t] + 1])
            hid_transposes(*pending)
            bm0, bm1 = PS.alloc(), PS.alloc()
            for half, bk in ((0, bm0), (1, bm1)):
                for f in range(NFC):
                    P.mm(ps[bk][:, :], hidT[:, f, :], wd_sb[:, f, half * 512:(half + 1) * 512], start=(f == 0), stop=(f == NFC - 1))
            for half, bk in ((0, bm0), (1, bm1)):
                P.stt(x2[:, half * 512:(half + 1) * 512], x2[:, half * 512:(half + 1) * 512], ALPHA, ps[bk][:, :], ALU.mult, ALU.add)
            PS.release(bm0)
            PS.release(bm1)
            layer_norm(x2, p_lng, p_lnb)
            P.dma("sp", y_o[t], x2)

        p2_order = order[:KP2]
        pos2 = {t: i for i, t in enumerate(p2_order)}
        if p2_order:
            front2(p2_order[0])
        for t in p2_order:
            phase2_tile(t)
        P.barrier("sp", [y_o, pgla_o, pgdn_o, pconv_o, sgla_o, sgdn_o, sconv_o])
        st = P.emit(es)
        print("ops", len(P.ops), "per-engine (instr, waits):", st, "p1_end", p1_end, "top", A.top)
    return nc


def _consts():
    i = np.arange(128)
    same8 = (i[:, None] // 8) == (i[None, :] // 8)
    up = i[:, None] <= i[None, :]
    lowstrict = i[:, None] > i[None, :]
    Uf = up.astype(np.float32)
    Us = (up & same8).astype(np.float32)
    mats = [np.eye(128, dtype=np.float32), Uf, Us, -Uf, -Us, -Uf / 16.0, -Us / 16.0,
            np.where(up, 0.0, NEG), np.where(up & same8, 0.0, NEG),
            np.where(lowstrict, 0.0, NEG), np.where(lowstrict & same8, 0.0, NEG),
            ((i[:, None] // 64) == (i[None, :] // 64)).astype(np.float32),
            np.ones((128, 128), np.float32), np.full((128, 128), -0.5, np.float32)]
    bind = ((i[:, None] // 8) == np.arange(16)[None, :]).astype(np.float32)
    return np.ascontiguousarray(np.concatenate([m.astype(np.float32) for m in mats] + [bind], axis=1))


_NC_CACHE = {}


def kernel(x_prompt, x_sample, state_gla, state_gdn, state_gdn_conv, w_in, gla_w_gate_up,
           gla_b_gate, gla_norm_g, gdn_conv_w, gdn_a_log, gdn_dt_bias, gdn_norm_g, w_out,
           ln1_g, ln1_b, w_ffn_gate, w_ffn_up, w_ffn_down, ln2_g, ln2_b):
    f = lambda a: np.ascontiguousarray(np.asarray(a, dtype=np.float32))
    x_prompt, x_sample = f(x_prompt), f(x_sample)
    state_gla, state_gdn, state_gdn_conv = f(state_gla), f(state_gdn), f(state_gdn_conv)
    if "nc" not in _NC_CACHE:
        _NC_CACHE["nc"] = build_program()
    nc = _NC_CACHE["nc"]
    rep = lambda v: np.broadcast_to(f(v).reshape(1, -1), (128, f(v).size))
    prm = np.ascontiguousarray(np.concatenate([rep(gla_norm_g), rep(gdn_norm_g), rep(ln1_g), rep(ln1_b),
                                               rep(gdn_a_log), rep(gdn_dt_bias), rep(gla_b_gate)], axis=1))
    prm2 = np.ascontiguousarray(np.concatenate([rep(ln2_g), rep(ln2_b)], axis=1))
    cw = f(gdn_conv_w)[0]
    cwT = np.ascontiguousarray(cw.reshape(4, 12, 128).transpose(2, 1, 0).reshape(128, 48))
    gnc = np.ascontiguousarray(np.stack([f(gla_norm_g).reshape(128), f(gdn_norm_g).reshape(128)], axis=1))
    common = {"gnc": gnc, "w_in": f(w_in)[0], "w_out": f(w_out)[0], "w_g": f(w_ffn_gate)[0], "w_u": f(w_ffn_up)[0],
              "w_d": f(w_ffn_down)[0], "wgu": f(gla_w_gate_up)[0], "cwT": cwT, "consts": _consts(),
              "prm": prm, "prm2": prm2}
    in_maps = []
    for c in range(8):
        xs = np.concatenate([x_prompt[c].reshape(16, 128, D), x_sample[16 * c:16 * c + 16].reshape(1, 128, D)], axis=0)
        m = dict(common)
        m["xs"] = np.ascontiguousarray(xs)
        m["sgla"] = np.ascontiguousarray(state_gla[0, 16 * c:16 * c + 16])
        m["sgdn"] = np.ascontiguousarray(state_gdn[0, 16 * c:16 * c + 16])
        m["sconv"] = np.ascontiguousarray(state_gdn_conv[0, 16 * c:16 * c + 16].reshape(48, 1536))
        in_maps.append(m)
    res = run_bass_kernel_spmd(nc, in_maps, core_ids=list(range(8)))
    R = res.results
    _NC_CACHE["last"] = R
    yp = np.stack([R[c]["y"][0:16].reshape(2048, D) for c in range(8)])
    ys = np.concatenate([R[c]["y"][16].reshape(16, 8, D) for c in range(8)], axis=0)
    pgla = np.stack([R[c]["pgla"] for c in range(8)])[None]
    pgdn = np.stack([R[c]["pgdn"] for c in range(8)])[None]
    pconv = np.stack([R[c]["pconv"] for c in range(8)])[None]
    sgla = np.concatenate([R[c]["sgla_o"] for c in range(8)], axis=0)[None]
    sgdn = np.concatenate([R[c]["sgdn_o"] for c in range(8)], axis=0)[None]
    sconv = np.concatenate([R[c]["sconv_o"] for c in range(8)], axis=0)[None]
    return tuple(np.ascontiguousarray(a, dtype=np.float32) for a in (yp, ys, pgla, pgdn, pconv, sgla, sgdn, sconv))
```
